# Optimizing a Trainium2 kernel written in Bass

```python
import math
import jax, jax.numpy as jnp
from jax import lax
import numpy as np

D_MODEL = 1024
BATCH = 8
SEQ = 2048
DEPTH = 2
DEC_BATCH = 128
DEC_SEQ = 8
PAST_LEN = 16384
PAGE_SIZE = 128

A_HEAD_DIM = 64
A_WIDTH = D_MODEL // 2
A_HEADS = A_WIDTH // A_HEAD_DIM
R_DECAY = 64
R_AAA = 64
R_GATE = 128
A_COLS = 3 * A_WIDTH + R_DECAY + R_AAA + R_GATE
B_WIDTH = D_MODEL // 4
B_BLOCKS = 4
B_BLOCK_DIM = B_WIDTH // B_BLOCKS
CONV_WIDTH = 4
LRU_C = 8.0
C_WIDTH = D_MODEL - A_WIDTH - B_WIDTH
S5_GROUP_CH = 16
S5_GROUPS = C_WIDTH // S5_GROUP_CH
S5_STATE = 64
MIX_WIDTH = A_WIDTH + B_WIDTH + C_WIDTH
IN_COLS = A_COLS + 2 * B_WIDTH + C_WIDTH
D_FF = ((8 * D_MODEL + 3 * 256 - 1) // (3 * 256)) * 256
PLE_DIM = 256
RMS_EPS = 1e-6
GN_EPS = 64e-5

kernel_name = 'hybrid_rwkv7_rglru_s5_decode_step'


def rmsnorm(x, g):
    x32 = x.astype(jnp.float32)
    inv = lax.rsqrt(jnp.mean(x32 * x32, axis=-1, keepdims=True) + RMS_EPS)
    return (x32 * inv).astype(x.dtype) * g


def _linear_op(e1, e2):
    a1, b1 = e1
    a2, b2 = e2
    return a1 * a2, a2 * b1 + b2


def _complex_linear_op(e1, e2):
    ar1, ai1, br1, bi1 = e1
    ar2, ai2, br2, bi2 = e2
    return (ar2 * ar1 - ai2 * ai1, ar2 * ai1 + ai2 * ar1,
            ar2 * br1 - ai2 * bi1 + br2, ar2 * bi1 + ai2 * br1 + bi2)


def rwkv7_mix(cols, shift0, wkv0, lp):
    bsz, t = cols.shape[0], cols.shape[1]
    prev = jnp.concatenate([shift0[:, None, :], cols[:, :-1]], axis=1)
    xs = cols + (prev - cols) * lp['mu_a']
    o = 3 * A_WIDTH
    r = xs[..., :A_WIDTH]
    k = xs[..., A_WIDTH:2 * A_WIDTH]
    v = xs[..., 2 * A_WIDTH:o]
    xw = xs[..., o:o + R_DECAY]
    xa = xs[..., o + R_DECAY:o + R_DECAY + R_AAA]
    xg = xs[..., o + R_DECAY + R_AAA:]
    w_log = -jax.nn.softplus(-(lp['w0'] + jnp.tanh(xw) @ lp['w_dec2'])) - 0.5
    decay = jnp.exp(-jnp.exp(w_log))
    a = jax.nn.sigmoid(lp['a0'] + xa @ lp['w_a2'])
    g = jax.nn.sigmoid(xg) @ lp['w_g2']

    def heads(z):
        return z.reshape(bsz, t, A_HEADS, A_HEAD_DIM)

    kk32 = heads(k * lp['k_k']).astype(jnp.float32)
    kk = (kk32 * lax.rsqrt(jnp.maximum(jnp.sum(kk32 * kk32, axis=-1, keepdims=True), 1e-24))).astype(k.dtype)
    k_mod = k * (1 + (a - 1) * lp['k_a'])
    r_h, k_h, v_h = heads(r), heads(k_mod), heads(v)
    seq = tuple(jnp.moveaxis(z, 1, 0) for z in (r_h, heads(decay), k_h, v_h, kk, heads(a)))

    def step(s, inp):
        r_t, w_t, k_t, v_t, kk_t, a_t = inp
        s_kk = jnp.einsum('bhij,bhj->bhi', s, kk_t)
        s = (s * w_t[:, :, None, :]
             - s_kk[..., None] * (kk_t * a_t)[:, :, None, :]
             + v_t[..., None] * k_t[:, :, None, :])
        return s, jnp.einsum('bhij,bhj->bhi', s, r_t)

    wkv1, y = lax.scan(step, wkv0, seq)
    y = jnp.moveaxis(y, 0, 1)
    y32 = y.astype(jnp.float32)
    mu = jnp.mean(y32, axis=-1, keepdims=True)
    var = jnp.mean((y32 - mu) ** 2, axis=-1, keepdims=True)
    yn = ((y32 - mu) * lax.rsqrt(var + GN_EPS)).astype(y.dtype)
    yn = yn * lp['lnx_w'].reshape(A_HEADS, A_HEAD_DIM) + lp['lnx_b'].reshape(A_HEADS, A_HEAD_DIM)
    bonus = jnp.sum(r_h * k_h * lp['r_k'], axis=-1, keepdims=True) * v_h
    out = (yn + bonus).reshape(bsz, t, A_WIDTH) * g
    return out, cols[:, -1], wkv1


def rglru_mix(cols, conv0, lru0, lp):
    bsz, t = cols.shape[0], cols.shape[1]
    gate_br = cols[..., :B_WIDTH]
    xb = cols[..., B_WIDTH:]
    xpad = jnp.concatenate([conv0, xb], axis=1)
    xc = lp['conv_b'] + sum(xpad[:, j:j + t] * lp['conv_w'][j] for j in range(CONV_WIDTH))
    conv1 = xpad[:, -(CONV_WIDTH - 1):]
    xblk = xc.reshape(bsz, t, B_BLOCKS, B_BLOCK_DIM)
    gate_r = jax.nn.sigmoid(jnp.einsum('btnd,nde->btne', xblk, lp['w_rg']).reshape(bsz, t, B_WIDTH) + lp['b_rg'])
    gate_i = jax.nn.sigmoid(jnp.einsum('btnd,nde->btne', xblk, lp['w_ig']).reshape(bsz, t, B_WIDTH) + lp['b_ig'])
    log_a = -LRU_C * gate_r * jax.nn.softplus(-lp['lru_lambda'])
    a = jnp.exp(log_a)
    mult = jnp.sqrt(-jnp.expm1(2 * log_a))
    b = mult * gate_i * xc
    b = b.at[:, 0].add(a[:, 0] * lru0)
    _, hs = lax.associative_scan(_linear_op, (a, b), axis=1)
    y = hs * jax.nn.gelu(gate_br)
    return y, conv1, hs[:, -1]


def s5_mix(u, h0r, h0i, lp):
    bsz, t = u.shape[0], u.shape[1]
    dt = jnp.exp(lp['s5_log_dt'])[:, None]
    lr, li = lp['s5_lam_re'], lp['s5_lam_im']
    mag = jnp.exp(lr * dt)
    ar = mag * jnp.cos(li * dt)
    ai = mag * jnp.sin(li * dt)
    den = lr * lr + li * li
    cr = ((ar - 1) * lr + ai * li) / den
    ci = (ai * lr - (ar - 1) * li) / den
    bb_re = cr[..., None] * lp['s5_b_re'] - ci[..., None] * lp['s5_b_im']
    bb_im = cr[..., None] * lp['s5_b_im'] + ci[..., None] * lp['s5_b_re']
    ug = u.reshape(bsz, t, S5_GROUPS, S5_GROUP_CH)
    bu_re = jnp.einsum('btgc,gpc->btgp', ug, bb_re)
    bu_im = jnp.einsum('btgc,gpc->btgp', ug, bb_im)
    bu_re = bu_re.at[:, 0].add(ar * h0r - ai * h0i)
    bu_im = bu_im.at[:, 0].add(ar * h0i + ai * h0r)
    a_re = jnp.broadcast_to(ar, bu_re.shape)
    a_im = jnp.broadcast_to(ai, bu_im.shape)
    _, _, h_re, h_im = lax.associative_scan(_complex_linear_op, (a_re, a_im, bu_re, bu_im), axis=1)
    y = (jnp.einsum('btgp,gcp->btgc', h_re, lp['s5_c_re'])
         - jnp.einsum('btgp,gcp->btgc', h_im, lp['s5_c_im']))
    y = y.reshape(bsz, t, C_WIDTH) + lp['s5_d'] * u
    z = jax.nn.gelu(y)
    out = z * jax.nn.sigmoid(z @ lp['w_glu'] + lp['b_glu'])
    return out, h_re[:, -1], h_im[:, -1]


def hybrid_layer(h, p_l, shift0, wkv0, conv0, lru0, s5r0, s5i0, lp):
    xn = rmsnorm(h, lp['g_mix'])
    cols = jnp.einsum('btd,dc->btc', xn, lp['w_in'])
    cols_a = cols[..., :A_COLS]
    cols_b = cols[..., A_COLS:A_COLS + 2 * B_WIDTH]
    cols_c = cols[..., A_COLS + 2 * B_WIDTH:]
    y_a, shift1, wkv1 = rwkv7_mix(cols_a, shift0, wkv0, lp)
    y_b, conv1, lru1 = rglru_mix(cols_b, conv0, lru0, lp)
    y_c, s5r1, s5i1 = s5_mix(cols_c, s5r0, s5i0, lp)
    mixed = jnp.concatenate([y_a, rmsnorm(y_b, lp['g_out_b']), rmsnorm(y_c, lp['g_out_c'])], axis=-1)
    h = h + jnp.einsum('btm,md->btd', mixed, lp['w_out'])
    xf = rmsnorm(h, lp['g_ffn'])
    gu = jnp.einsum('btd,df->btf', xf, lp['w_ffn_up'])
    h = h + jnp.einsum('btf,fd->btd', jax.nn.silu(gu[..., :D_FF]) * gu[..., D_FF:], lp['w_ffn_down'])
    ple = jnp.einsum('btq,qd->btd', p_l, lp['w_ple'])
    h = h + ple * jax.nn.sigmoid(jnp.einsum('btd,de->bte', rmsnorm(h, lp['g_ple']), lp['w_ple_gate']))
    return h, (shift1, wkv1, conv1, lru1, s5r1, s5i1)


def run_trunk(x, p, states, layer_params, g_final):
    h = x
    news = []
    for l in range(DEPTH):
        h, st = hybrid_layer(h, p[l], states[0][l], states[1][l], states[2][l], states[3][l],
                             states[4][l], states[5][l], layer_params[l])
        news.append(st)
    stacked = tuple(jnp.stack([n[j] for n in news], axis=0) for j in range(6))
    return rmsnorm(h, g_final), stacked


def setup_inputs(seed: int = 0) -> dict:
    key = jax.random.key(seed)
    ks = iter(jax.random.split(key, 64))
    f32 = jnp.float32
    L = DEPTH

    def nrm(shape, scale=1.0):
        return scale * jax.random.normal(next(ks), shape, f32)

    def unif(shape, lo, hi):
        return jax.random.uniform(next(ks), shape, f32, lo, hi)

    a_init = unif((L, B_WIDTH), 0.9, 0.999)
    s_init = a_init ** (1.0 / LRU_C)
    inp = {}
    inp['x_prompt'] = nrm((BATCH, SEQ, D_MODEL))
    inp['x_sample'] = nrm((DEC_BATCH, DEC_SEQ, D_MODEL))
    inp['p_prompt'] = nrm((DEPTH, BATCH, SEQ, PLE_DIM))
    inp['p_sample'] = nrm((DEPTH, DEC_BATCH, DEC_SEQ, PLE_DIM))
    inp['state_shift'] = nrm((L, DEC_BATCH, A_COLS))
    inp['state_wkv'] = nrm((L, DEC_BATCH, A_HEADS, A_HEAD_DIM, A_HEAD_DIM), 0.5)
    inp['state_conv'] = nrm((L, DEC_BATCH, CONV_WIDTH - 1, B_WIDTH))
    inp['state_lru'] = nrm((L, DEC_BATCH, B_WIDTH), 0.5)
    inp['state_s5_re'] = nrm((L, DEC_BATCH, S5_GROUPS, S5_STATE), 0.5)
    inp['state_s5_im'] = nrm((L, DEC_BATCH, S5_GROUPS, S5_STATE), 0.5)
    inp['g_mix'] = 1.0 + nrm((L, D_MODEL), 0.05)
    inp['w_in'] = nrm((L, D_MODEL, IN_COLS), D_MODEL ** -0.5)
    inp['mu_a'] = unif((L, A_COLS), 0.0, 1.0)
    inp['w0'] = jnp.linspace(-6.0, -1.0, A_WIDTH, dtype=f32)[None, :] + nrm((L, A_WIDTH), 0.1)
    inp['w_dec2'] = nrm((L, R_DECAY, A_WIDTH), 0.1)
    inp['a0'] = nrm((L, A_WIDTH), 0.1)
    inp['w_a2'] = nrm((L, R_AAA, A_WIDTH), R_AAA ** -0.5)
    inp['w_g2'] = nrm((L, R_GATE, A_WIDTH), R_GATE ** -0.5)
    inp['k_k'] = 0.85 + nrm((L, A_WIDTH), 0.05)
    inp['k_a'] = 1.0 + nrm((L, A_WIDTH), 0.05)
    inp['r_k'] = nrm((L, A_HEADS, A_HEAD_DIM), 0.1)
    inp['lnx_w'] = 1.0 + nrm((L, A_WIDTH), 0.05)
    inp['lnx_b'] = nrm((L, A_WIDTH), 0.01)
    inp['conv_w'] = nrm((L, CONV_WIDTH, B_WIDTH), CONV_WIDTH ** -0.5)
    inp['conv_b'] = nrm((L, B_WIDTH), 0.01)
    inp['w_rg'] = nrm((L, B_BLOCKS, B_BLOCK_DIM, B_BLOCK_DIM), B_BLOCK_DIM ** -0.5)
    inp['b_rg'] = nrm((L, B_WIDTH), 0.01)
    inp['w_ig'] = nrm((L, B_BLOCKS, B_BLOCK_DIM, B_BLOCK_DIM), B_BLOCK_DIM ** -0.5)
    inp['b_ig'] = nrm((L, B_WIDTH), 0.01)
    inp['lru_lambda'] = jnp.log(s_init) - jnp.log1p(-s_init)
    inp['g_out_b'] = 1.0 + nrm((L, B_WIDTH), 0.05)
    inp['s5_lam_re'] = -0.5 + nrm((L, S5_GROUPS, S5_STATE), 0.01)
    inp['s5_lam_im'] = math.pi * jnp.arange(S5_STATE, dtype=f32)[None, None, :] + nrm((L, S5_GROUPS, S5_STATE), 0.01)
    inp['s5_log_dt'] = unif((L, S5_GROUPS), math.log(1e-3), math.log(1e-1))
    inp['s5_b_re'] = nrm((L, S5_GROUPS, S5_STATE, S5_GROUP_CH), (2 * S5_GROUP_CH) ** -0.5)
    inp['s5_b_im'] = nrm((L, S5_GROUPS, S5_STATE, S5_GROUP_CH), (2 * S5_GROUP_CH) ** -0.5)
    inp['s5_c_re'] = nrm((L, S5_GROUPS, S5_GROUP_CH, S5_STATE), (2 * S5_STATE) ** -0.5)
    inp['s5_c_im'] = nrm((L, S5_GROUPS, S5_GROUP_CH, S5_STATE), (2 * S5_STATE) ** -0.5)
    inp['s5_d'] = nrm((L, C_WIDTH), 0.5)
    inp['w_glu'] = nrm((L, C_WIDTH, C_WIDTH), C_WIDTH ** -0.5)
    inp['b_glu'] = nrm((L, C_WIDTH), 0.01)
    inp['g_out_c'] = 1.0 + nrm((L, C_WIDTH), 0.05)
    inp['w_out'] = nrm((L, MIX_WIDTH, D_MODEL), MIX_WIDTH ** -0.5)
    inp['g_ffn'] = 1.0 + nrm((L, D_MODEL), 0.05)
    inp['w_ffn_up'] = nrm((L, D_MODEL, 2 * D_FF), D_MODEL ** -0.5)
    inp['w_ffn_down'] = nrm((L, D_FF, D_MODEL), D_FF ** -0.5)
    inp['g_ple'] = 1.0 + nrm((L, D_MODEL), 0.05)
    inp['w_ple'] = nrm((L, PLE_DIM, D_MODEL), PLE_DIM ** -0.5)
    inp['w_ple_gate'] = nrm((L, D_MODEL, D_MODEL), D_MODEL ** -0.5)
    inp['g_final'] = 1.0 + nrm((D_MODEL,), 0.05)
    return inp


def reference(x_prompt, x_sample, p_prompt, p_sample, state_shift, state_wkv, state_conv, state_lru,
              state_s5_re, state_s5_im, g_mix, w_in, mu_a, w0, w_dec2, a0, w_a2, w_g2, k_k, k_a, r_k,
              lnx_w, lnx_b, conv_w, conv_b, w_rg, b_rg, w_ig, b_ig, lru_lambda, g_out_b, s5_lam_re,
              s5_lam_im, s5_log_dt, s5_b_re, s5_b_im, s5_c_re, s5_c_im, s5_d, w_glu, b_glu, g_out_c,
              w_out, g_ffn, w_ffn_up, w_ffn_down, g_ple, w_ple, w_ple_gate, g_final):
    layer_params = [dict(g_mix=g_mix[l], w_in=w_in[l], mu_a=mu_a[l], w0=w0[l], w_dec2=w_dec2[l],
                         a0=a0[l], w_a2=w_a2[l], w_g2=w_g2[l], k_k=k_k[l], k_a=k_a[l], r_k=r_k[l],
                         lnx_w=lnx_w[l], lnx_b=lnx_b[l], conv_w=conv_w[l], conv_b=conv_b[l],
                         w_rg=w_rg[l], b_rg=b_rg[l], w_ig=w_ig[l], b_ig=b_ig[l],
                         lru_lambda=lru_lambda[l], g_out_b=g_out_b[l], s5_lam_re=s5_lam_re[l],
                         s5_lam_im=s5_lam_im[l], s5_log_dt=s5_log_dt[l], s5_b_re=s5_b_re[l],
                         s5_b_im=s5_b_im[l], s5_c_re=s5_c_re[l], s5_c_im=s5_c_im[l], s5_d=s5_d[l],
                         w_glu=w_glu[l], b_glu=b_glu[l], g_out_c=g_out_c[l], w_out=w_out[l],
                         g_ffn=g_ffn[l], w_ffn_up=w_ffn_up[l], w_ffn_down=w_ffn_down[l],
                         g_ple=g_ple[l], w_ple=w_ple[l], w_ple_gate=w_ple_gate[l])
                    for l in range(DEPTH)]
    bp = x_prompt.shape[0]
    dt_ = x_prompt.dtype
    zero_states = (jnp.zeros((DEPTH, bp, A_COLS), dt_),
                   jnp.zeros((DEPTH, bp, A_HEADS, A_HEAD_DIM, A_HEAD_DIM), dt_),
                   jnp.zeros((DEPTH, bp, CONV_WIDTH - 1, B_WIDTH), dt_),
                   jnp.zeros((DEPTH, bp, B_WIDTH), dt_),
                   jnp.zeros((DEPTH, bp, S5_GROUPS, S5_STATE), dt_),
                   jnp.zeros((DEPTH, bp, S5_GROUPS, S5_STATE), dt_))
    y_prompt, new_p = run_trunk(x_prompt, p_prompt, zero_states, layer_params, g_final)
    sample_states = (state_shift, state_wkv, state_conv, state_lru, state_s5_re, state_s5_im)
    y_sample, new_s = run_trunk(x_sample, p_sample, sample_states, layer_params, g_final)
    shift_p, wkv_p, conv_p, lru_p, s5re_p, s5im_p = new_p
    shift_s, wkv_s, conv_s, lru_s, s5re_s, s5im_s = new_s
    return (y_prompt, y_sample, shift_p, wkv_p, conv_p, lru_p, s5re_p, s5im_p,
            shift_s, wkv_s, conv_s, lru_s, s5re_s, s5im_s)
```

```python
import math
import numpy as np
from collections import defaultdict
import concourse.bass as bass
import concourse.mybir as mybir
from concourse.bass_utils import run_bass_kernel_spmd

F32 = mybir.dt.float32
BF16 = mybir.dt.bfloat16
ALU = mybir.AluOpType
AF = mybir.ActivationFunctionType
AX = mybir.AxisListType

NCORES = 8
D = 1024
NL = 2
SEQ = 2048
TBP = 512
NPB = SEQ // TBP
DEC_B = 128
DEC_T = 8
SB_PER = DEC_B // NCORES
A_COLS = 1792
IN_COLS = 2560
DFF = 2816
NF = DFF // 128
COL_ORDER = [18, 19, 14, 15, 16, 17, 12, 13, 0, 4, 8, 1, 5, 9, 2, 6, 10, 3, 7, 11]
RMS_EPS = 1e-6
GN_EPS = 64e-5
DEBUG = {}

SAME_ENGINE_SYNC = True
NDSEM = 8
NW = 4

PC = {}
_off = 0
for _n, _c in [("mu", 14), ("w0", 4), ("a0", 4), ("k_k", 4), ("k_a", 4), ("r_k", 4), ("lnx_w", 4),
               ("lnx_b", 4), ("conv_w", 8), ("conv_b", 2), ("b_rg", 2), ("b_ig", 2), ("lam", 2),
               ("g_out_b", 2), ("s5_d", 2), ("b_glu", 2), ("g_out_c", 2), ("s5_lr", 8), ("s5_li", 8),
               ("s5_ldt", 8), ("g_mix", 8), ("g_ffn", 8), ("g_ple", 8)]:
    PC[_n] = _off
    _off += _c
NPAR = _off

CC = {}
_off = 0
for _n, _c in [("ident", 128), ("onesblk", 128), ("avgblk", 128), ("ones", 128), ("maskA", 192),
               ("maskSL", 128), ("cumcoef", 256), ("startmask", 128)]:
    CC[_n] = _off
    _off += _c
NCONST = _off


def _cols(v, n):
    return np.ascontiguousarray(np.asarray(v, np.float32).reshape(n, 128).T)


def make_consts():
    c = np.zeros((128, NCONST), np.float32)
    p = np.arange(128)[:, None]
    q = np.arange(128)[None, :]
    same = (p // 64) == (q // 64)
    c[:, CC["ident"]:CC["ident"] + 128] = np.eye(128)
    c[:, CC["onesblk"]:CC["onesblk"] + 128] = same
    c[:, CC["avgblk"]:CC["avgblk"] + 128] = same / 64.0
    c[:, CC["ones"]:CC["ones"] + 128] = 1.0
    c[:, CC["maskA"]:CC["maskA"] + 128] = same & ((p % 64) < (q % 64))
    q64 = np.arange(64)[None, :]
    c[:, CC["maskA"] + 128:CC["maskA"] + 192] = (p % 64) <= q64
    c[:, CC["maskSL"]:CC["maskSL"] + 128] = same & ((p % 64) > (q % 64))
    cc = np.ones((128, 256), np.float32)
    cc[:, 0::64] = 0.0
    c[:, CC["cumcoef"]:CC["cumcoef"] + 256] = cc
    sm = np.ones((128, 128), np.float32)
    sm[:, 0::8] = 0.0
    c[:, CC["startmask"]:CC["startmask"] + 128] = sm
    return c


def pack_params(inp, l):
    P = np.zeros((128, NPAR), np.float32)

    def put(name, arr, n):
        P[:, PC[name]:PC[name] + n] = _cols(arr, n)
    put("mu", inp["mu_a"][l], 14)
    for nm in ("w0", "a0", "k_k", "k_a", "lnx_w", "lnx_b"):
        put(nm, inp[nm][l], 4)
    put("r_k", inp["r_k"][l].reshape(512), 4)
    cw = np.asarray(inp["conv_w"][l], np.float32)
    for ct in range(2):
        for j in range(4):
            P[:, PC["conv_w"] + ct * 4 + j] = cw[j, ct * 128:(ct + 1) * 128]
    for nm, src in (("conv_b", "conv_b"), ("b_rg", "b_rg"), ("b_ig", "b_ig"), ("lam", "lru_lambda"),
                    ("g_out_b", "g_out_b"), ("s5_d", "s5_d"), ("b_glu", "b_glu"), ("g_out_c", "g_out_c")):
        put(nm, inp[src][l], 2)
    put("s5_lr", inp["s5_lam_re"][l].reshape(1024), 8)
    put("s5_li", inp["s5_lam_im"][l].reshape(1024), 8)
    put("s5_ldt", np.repeat(np.asarray(inp["s5_log_dt"][l], np.float32), 64), 8)
    put("g_mix", inp["g_mix"][l], 8)
    put("g_ffn", inp["g_ffn"][l], 8)
    put("g_ple", inp["g_ple"][l], 8)
    return P


def prep_shared(inp):
    f = lambda a: np.ascontiguousarray(np.asarray(a, np.float32))
    sh = {}
    sh["consts"] = make_consts()
    sh["params"] = np.stack([pack_params(inp, l) for l in range(NL)], 0)
    order = np.concatenate([np.arange(t * 128, (t + 1) * 128) for t in COL_ORDER])
    sh["w_in"] = f(np.asarray(inp["w_in"])[:, :, order])
    sh["w_out"] = f(inp["w_out"])
    wu = np.asarray(inp["w_ffn_up"], np.float32)
    uo = []
    for p in range(NF // 2):
        uo.append(np.arange(256 * p, 256 * p + 256))
        uo.append(np.arange(DFF + 256 * p, DFF + 256 * p + 256))
    sh["w_up"] = f(wu[:, :, np.concatenate(uo)])
    sh["w_down"] = f(inp["w_ffn_down"])
    sh["w_ple"] = f(inp["w_ple"])
    sh["w_gate"] = f(inp["w_ple_gate"])
    sh["g_final"] = f(inp["g_final"]).reshape(1, D)
    sh["lora1"] = f(np.concatenate([np.asarray(inp["w_dec2"]), np.asarray(inp["w_a2"])], axis=1))
    sh["w_g2"] = f(inp["w_g2"])
    def bd(w):
        w = np.asarray(w, np.float32)
        o = np.zeros((NL, 2, 128, 128), np.float32)
        for ct in range(2):
            for h in range(2):
                o[:, ct, h * 64:(h + 1) * 64, h * 64:(h + 1) * 64] = w[:, ct * 2 + h]
        return o
    sh["w_rg"] = bd(inp["w_rg"])
    sh["w_ig"] = bd(inp["w_ig"])
    sh["w_glu"] = f(inp["w_glu"])
    def bpad(b):
        b = np.asarray(b, np.float32)
        o = np.zeros((NL, 8, 128, 128), np.float32)
        for s in range(8):
            for g2 in range(2):
                g = 2 * s + g2
                k0 = (g % 8) * 16
                o[:, s, k0:k0 + 16, g2 * 64:(g2 + 1) * 64] = np.transpose(b[:, g], (0, 2, 1))
        return o
    def cpad(c):
        c = np.asarray(c, np.float32)
        o = np.zeros((NL, 8, 128, 128), np.float32)
        for s in range(8):
            for g2 in range(2):
                g = 2 * s + g2
                m0 = (g % 8) * 16
                o[:, s, g2 * 64:(g2 + 1) * 64, m0:m0 + 16] = np.transpose(c[:, g], (0, 2, 1))
        return o
    s5w = np.stack([bpad(inp["s5_b_re"]), bpad(inp["s5_b_im"]), cpad(inp["s5_c_re"]), cpad(inp["s5_c_im"])], 1)
    sh["s5w"] = f(np.transpose(s5w, (0, 1, 3, 2, 4)))
    return sh


def prep_core(inp, c):
    f = lambda a: np.ascontiguousarray(np.asarray(a, np.float32))
    b0, b1 = c * SB_PER, (c + 1) * SB_PER
    d = {}
    d["xp"] = f(inp["x_prompt"][c])
    d["xs"] = f(np.asarray(inp["x_sample"])[b0:b1].reshape(SB_PER * DEC_T, D))
    d["pp"] = f(np.asarray(inp["p_prompt"])[:, c])
    d["psm"] = f(np.asarray(inp["p_sample"])[:, b0:b1].reshape(NL, SB_PER * DEC_T, 256))
    ss = np.asarray(inp["state_shift"], np.float32)[:, b0:b1]
    d["st_shift"] = f(np.transpose(ss.reshape(NL, SB_PER, 14, 128), (0, 3, 2, 1)))
    d["st_wkv"] = f(np.asarray(inp["state_wkv"])[:, b0:b1].reshape(NL, SB_PER * 8, 4096))
    sc = np.asarray(inp["state_conv"], np.float32)[:, b0:b1]
    d["st_conv"] = f(np.transpose(sc.reshape(NL, SB_PER, 3, 2, 128), (0, 4, 3, 1, 2)))
    sl = np.asarray(inp["state_lru"], np.float32)[:, b0:b1]
    d["st_lru"] = f(np.transpose(sl.reshape(NL, SB_PER, 2, 128), (0, 3, 2, 1)))
    for nm, src in (("st_s5re", "state_s5_re"), ("st_s5im", "state_s5_im")):
        s5 = np.asarray(inp[src], np.float32)[:, b0:b1].reshape(NL, SB_PER, 8, 128)
        d[nm] = f(np.transpose(s5, (0, 3, 2, 1)))
    return d


class V:
    __slots__ = ("t", "ap")

    def __init__(self, t, ap):
        self.t = t
        self.ap = ap

    def __getitem__(self, k):
        return V(self.t, self.ap[k])

    def re(self, s, **kw):
        return V(self.t, self.ap.rearrange(s, **kw))

    def bc(self, shape):
        return V(self.t, self.ap.to_broadcast(shape))

    def un(self, axis):
        return V(self.t, self.ap.unsqueeze(axis))


class T:
    def __init__(self, ap, name="", scr=False):
        self.ap = ap
        self.name = name
        self.w = None
        self.r = []
        self.scr = scr
        self.psum = False

    def __getitem__(self, k):
        return V(self, self.ap[k])

    def v(self):
        return V(self, self.ap)

    def re(self, s, **kw):
        return V(self, self.ap.rearrange(s, **kw))

    def un(self, axis):
        return V(self, self.ap.unsqueeze(axis))

    def bc(self, shape):
        return V(self, self.ap.to_broadcast(shape))


class KB:
    def __init__(self, nc):
        self.nc = nc
        self.eng = {"pe": nc.tensor, "act": nc.scalar, "dve": nc.vector, "pool": nc.gpsimd, "sp": nc.sync}
        self.sems = {}
        self.semval = defaultdict(int)
        for e in self.eng:
            self.sems[e] = nc.alloc_semaphore(name=e + "_c")
        for q in ("sp", "act", "pool"):
            for i in range(NDSEM):
                self.sems[(q, i)] = nc.alloc_semaphore(name=f"{q}_d{i}")
        self.dma_i = defaultdict(int)
        self.waited = {e: defaultdict(int) for e in self.eng}
        self.ninst = defaultdict(int)
        self.out_dma = []
        self.scr_dma = []
        self._n = 0
        self.rr_i = 0
        self.rec = None

    def sb(self, shape, dt=F32, name=None):
        self._n += 1
        name = "s_" + (name or f"sb{self._n}")
        h = self.nc.alloc_sbuf_tensor(name, list(shape), dt)
        return T(h.ap(), name)

    def ps(self, shape, dt=F32, name=None):
        self._n += 1
        name = "p_" + (name or f"ps{self._n}")
        h = self.nc.alloc_psum_tensor(name, list(shape), dt)
        t = T(h.ap(), name)
        t.psum = True
        return t

    def _wait(self, e, deps):
        eng = self.eng[e]
        best = {}
        for (sk, val, de) in deps:
            if de == e and (e == "pe" or not SAME_ENGINE_SYNC):
                continue
            if val > best.get(sk, 0):
                best[sk] = val
        for sk, val in best.items():
            if self.waited[e][sk] < val:
                eng.wait_ge(self.sems[sk], val)
                self.waited[e][sk] = val
                self.ninst[e] += 1

    @staticmethod
    def _deps(reads, writes, e=None):
        deps = []
        for t in reads:
            if t.w is not None:
                deps.append(t.w)
            if t.psum:
                deps.extend(x for x in t.r if x[2] != e)
        for t in writes:
            if t.w is not None:
                deps.append(t.w)
            deps.extend(t.r)
        return deps

    @staticmethod
    def _commit(reads, writes, tok):
        for t in reads:
            if len(t.r) > 6:
                best = {}
                for x in t.r:
                    if x[1] > best.get(x[0], (None, 0, None))[1]:
                        best[x[0]] = x
                t.r = list(best.values())
            t.r.append(tok)
        for t in writes:
            t.w = tok
            t.r = []

    def op(self, e, meth, *args, reads=(), writes=(), **kw):
        if self.rec is not None:
            self.rec.append(lambda: self._op(e, meth, *args, reads=reads, writes=writes, **kw))
            return None
        return self._op(e, meth, *args, reads=reads, writes=writes, **kw)

    def _op(self, e, meth, *args, reads=(), writes=(), **kw):
        rd = list(reads)
        wr = list(writes)
        kw2 = {}
        for kk_, v in kw.items():
            if isinstance(v, V):
                (wr if kk_ in ("out", "accum_out") else rd).append(v.t)
                kw2[kk_] = v.ap
            elif isinstance(v, T):
                (wr if kk_ in ("out", "accum_out") else rd).append(v)
                kw2[kk_] = v.ap
            else:
                kw2[kk_] = v
        self._wait(e, self._deps(rd, wr, e))
        inst = getattr(self.eng[e], meth)(*args, **kw2)
        self.semval[e] += 1
        inst.then_inc(self.sems[e], 1)
        self.ninst[e] += 1
        self._commit(rd, wr, (e, self.semval[e], e))
        return inst

    def mm(self, out, lhsT, rhs, start=True, stop=True):
        return self.op("pe", "matmul", out=out, lhsT=lhsT, rhs=rhs, start=start, stop=stop)

    def tr(self, out, in_, ident):
        return self.op("pe", "transpose", out=out, in_=in_, identity=ident)

    def dma(self, q, out, in_, is_output=False, **kw):
        if self.rec is not None:
            self.rec.append(lambda: self._dma(q, out, in_, is_output=is_output, **kw))
            return None
        return self._dma(q, out, in_, is_output=is_output, **kw)

    def start_rec(self):
        assert self.rec is None
        self.rec = []

    def stop_rec(self):
        r = self.rec
        self.rec = None
        return r

    @staticmethod
    def interleave(*lists):
        lists = [x for x in lists if x]
        pos = [0] * len(lists)
        while True:
            best, bf = -1, 2.0
            for i, x in enumerate(lists):
                if pos[i] < len(x):
                    f = pos[i] / len(x)
                    if f < bf:
                        best, bf = i, f
            if best < 0:
                break
            lists[best][pos[best]]()
            pos[best] += 1

    def _dma(self, q, out, in_, is_output=False, **kw):
        rd, wr = [], []
        if isinstance(in_, (V, T)):
            rd.append(in_.t if isinstance(in_, V) else in_)
            in_ap = in_.ap
        else:
            in_ap = in_
        if isinstance(out, (V, T)):
            wr.append(out.t if isinstance(out, V) else out)
            out_ap = out.ap
        else:
            out_ap = out
        i = self.dma_i[q] % NDSEM
        self.dma_i[q] += 1
        sk = (q, i)
        deps = self._deps(rd, wr)
        if self.semval[sk] > 0:
            deps.append((sk, self.semval[sk], "dma"))
        self._wait(q, deps)
        inst = self.eng[q].dma_start(out=out_ap, in_=in_ap, allow_slow_non_contiguous=True, **kw)
        self.semval[sk] += 16
        inst.then_inc(self.sems[sk], 16)
        self.ninst[q] += 1
        tok = (sk, self.semval[sk], "dma")
        self._commit(rd, wr, tok)
        if is_output:
            self.out_dma.append(tok)
        if any(t.scr for t in rd + wr):
            self.scr_dma.append(tok)
        return inst

    def copy(self, out, in_, eng=None):
        if eng is None:
            self.rr_i += 1
            eng = "dve" if self.rr_i % 2 else "act"
        if eng == "act":
            return self.op("act", "copy", out=out, in_=in_)
        return self.op(eng, "tensor_copy", out=out, in_=in_)

    def tt(self, out, in0, in1, op, eng="dve"):
        return self.op(eng, "tensor_tensor", out=out, in0=in0, in1=in1, op=op)

    def ts(self, out, in0, s1, op0, s2=None, op1=None, eng="dve"):
        if op1 is None:
            return self.op(eng, "tensor_scalar", out=out, in0=in0, scalar1=s1, scalar2=None, op0=op0)
        return self.op(eng, "tensor_scalar", out=out, in0=in0, scalar1=s1, scalar2=s2, op0=op0, op1=op1)

    def stt(self, out, in0, scalar, in1, op0, op1, eng="dve"):
        return self.op(eng, "scalar_tensor_tensor", out=out, in0=in0, scalar=scalar, in1=in1, op0=op0, op1=op1)

    def act(self, out, in_, func, bias=None, scale=None, accum_out=None):
        kw = {}
        if bias is not None:
            kw["bias"] = bias
        if scale is not None:
            kw["scale"] = scale
        if accum_out is not None:
            kw["accum_out"] = accum_out
        return self.op("act", "activation", out=out, in_=in_, func=func, **kw)

    def memset(self, view, val, eng="dve"):
        if isinstance(view, T):
            view = view.v()
        return self.op(eng, "memset", view.ap, val, writes=[view.t])

    def barrier(self):
        assert self.rec is None
        toks = []
        for e in ("pe", "act", "dve", "pool"):
            if self.semval[e] > 0:
                toks.append((e, self.semval[e], "x"))
        toks.extend((sk, v, "dma") for (sk, v, _) in self.scr_dma)
        self.scr_dma = []
        for e in ("pe", "act", "dve", "pool", "sp"):
            self._wait(e, [(sk, v, "x") for (sk, v, _) in toks])

    def finish(self):
        deps = list(self.out_dma)
        for q in ("sp", "act", "pool"):
            for i in range(NDSEM):
                sk = (q, i)
                if self.semval[sk] > 0:
                    deps.append((sk, self.semval[sk], "dma"))
        for e in ("pe", "act", "dve", "pool"):
            if self.semval[e] > 0:
                deps.append((e, self.semval[e], "x"))
        self._wait("sp", deps)


class Scratch:
    def __init__(self, k, nbytes):
        self.k = k
        self.words = nbytes // 4
        self.base = k.nc.alloc_sbuf_tensor("scratch", [128, self.words], F32).ap()
        self.off = 0

    def reset(self):
        self.off = 0

    def alloc(self, shape, dt=F32, name=""):
        n = 1
        for s in shape[1:]:
            n *= s
        if dt == F32:
            w = n
        else:
            w = (n + 1) // 2
        w = (w + 7) // 8 * 8
        assert self.off + w <= self.words, f"scratch overflow {name} {self.off + w} > {self.words}"
        ap = self.base[:, self.off:self.off + w]
        self.off += w
        if dt != F32:
            ap = ap.bitcast(dt)
        ap = ap[:, 0:n]
        if len(shape) == 3:
            ap = ap.rearrange("p (a b) -> p a b", a=shape[1])
        elif len(shape) == 4:
            ap = ap.rearrange("p (a b c) -> p a b c", a=shape[1], b=shape[2])
        return T(ap, name, scr=True)


class Blk:
    def __init__(self, kind, idx):
        self.kind = kind
        self.idx = idx
        if kind == "p":
            self.nseq, self.T, self.ntok, self.nt = 1, TBP, TBP, TBP // 128
        else:
            self.nseq, self.T, self.ntok, self.nt = SB_PER, DEC_T, SB_PER * DEC_T, 1
        self.first = (kind == "s") or idx == 0
        self.last = (kind == "s") or idx == NPB - 1


def build_program(dbg=None):
    dbg = dbg or {}
    nc = bass.Bass("TRN2", target_bir_lowering=False)

    def din(name, shape):
        return nc.dram_tensor(name, list(shape), F32, kind="ExternalInput").ap()

    def dout(name, shape):
        return nc.dram_tensor(name, list(shape), F32, kind="ExternalOutput").ap()

    I = {}
    for nm, shp in [("xp", [SEQ, D]), ("xs", [128, D]), ("pp", [NL, SEQ, 256]), ("psm", [NL, 128, 256]),
                    ("st_shift", [NL, 128, 14, 16]), ("st_wkv", [NL, 128, 4096]), ("st_conv", [NL, 128, 2, 16, 3]),
                    ("st_lru", [NL, 128, 2, 16]), ("st_s5re", [NL, 128, 8, 16]), ("st_s5im", [NL, 128, 8, 16]),
                    ("consts", [128, NCONST]), ("params", [NL, 128, NPAR]), ("w_in", [NL, D, IN_COLS]),
                    ("w_out", [NL, D, D]), ("w_up", [NL, D, 2 * DFF]), ("w_down", [NL, DFF, D]),
                    ("w_ple", [NL, 256, D]), ("w_gate", [NL, D, D]), ("g_final", [1, D]),
                    ("lora1", [NL, 128, 512]), ("w_g2", [NL, 128, 512]), ("w_rg", [NL, 2, 128, 128]),
                    ("w_ig", [NL, 2, 128, 128]), ("w_glu", [NL, 256, 256]), ("s5w", [NL, 4, 128, 8, 128])]:
        I[nm] = din(nm, shp)
    O = {}
    for nm, shp in [("y_p", [SEQ, D]), ("y_s", [128, D]),
                    ("o_shift_p", [NL, 128, 14]), ("o_wkv_p", [NL, 8, 64, 64]), ("o_conv_p", [NL, 128, 2, 3]),
                    ("o_lru_p", [NL, 128, 2]), ("o_s5re_p", [NL, 128, 8]), ("o_s5im_p", [NL, 128, 8]),
                    ("o_shift_s", [NL, 128, 14, 16]), ("o_wkv_s", [NL, 128, 4096]), ("o_conv_s", [NL, 128, 2, 16, 3]),
                    ("o_lru_s", [NL, 128, 2, 16]), ("o_s5re_s", [NL, 128, 8, 16]), ("o_s5im_s", [NL, 128, 8, 16])]:
        O[nm] = dout(nm, shp)
    DBG = {}

    k = KB(nc)
    MUL, ADD, SUB = ALU.mult, ALU.add, ALU.subtract

    def dump(name, view, shape):
        if name in dbg:
            DBG[name] = dout("dbg_" + name, shape)
            isbf = (view.ap if isinstance(view, (V, T)) else view).dtype == BF16
            k.dma("pool" if isbf else "sp", DBG[name], view, is_output=True)

    consts = k.sb([128, NCONST], F32, "consts")
    C = lambda nm, n: consts[:, CC[nm]:CC[nm] + n]
    ident = C("ident", 128)
    idb = k.sb([128, 128], BF16, "idb")
    par = [k.sb([128, NPAR], F32, f"par{l}") for l in range(NL)]
    P = lambda l, nm, i=0, n=1: par[l][:, PC[nm] + i:PC[nm] + i + n]
    lora1 = [k.sb([128, 512], F32, f"lora1_{l}") for l in range(NL)]
    wg2 = [k.sb([128, 512], F32, f"wg2_{l}") for l in range(NL)]
    wrg = [k.sb([128, 2, 128], F32, f"wrg{l}") for l in range(NL)]
    wig = [k.sb([128, 2, 128], F32, f"wig{l}") for l in range(NL)]
    wglu = [k.sb([128, 2, 256], F32, f"wglu{l}") for l in range(NL)]
    gfin = k.sb([128, D], F32, "gfin")
    s5d = [k.sb([128, 16, 8], F32, f"s5d{l}") for l in range(NL)]
    S5N = {n: i for i, n in enumerate(["dt", "mag", "ang", "ar", "ai", "nai", "cr", "ci", "c1", "s1", "t0", "t1", "Qr", "Qi", "nQi"])}
    S5 = lambda l, nm: s5d[l][:, S5N[nm], :]
    Tc = [k.sb([128, 8, 128], F32, f"Tc{l}") for l in range(NL)]
    Ts = [k.sb([128, 8, 128], F32, f"Ts{l}") for l in range(NL)]
    c8 = [k.sb([128, 2], F32, f"c8_{l}") for l in range(NL)]
    h = k.sb([128, 4, D], F32, "h")
    xn = [k.sb([128, D], BF16, f"xn{i}") for i in range(2)]
    junk = k.sb([128, D], BF16, "junk")
    stat = [k.sb([128, 4], F32, f"stat{i}") for i in range(2)]
    xnT = k.sb([128, 8, TBP], BF16, "xnT")
    wring = [k.sb([128, 4096], BF16, f"wring{i}") for i in range(NW)]
    mixed = k.sb([128, 8, TBP], BF16, "mixed")
    ptile = k.sb([128, 4, 256], F32, "ptile")
    pbf = k.sb([128, 4, 256], BF16, "pbf")
    pT = k.sb([128, 2, TBP], BF16, "pT")
    tmpt = [k.sb([128, 512], F32, f"tmpt{i}") for i in range(2)]
    wple_t = k.sb([128, 2, D], BF16, "wple_t")
    histA = [k.sb([128, 14], F32, f"histA{l}") for l in range(NL)]
    histB = [k.sb([128, 2, 3], F32, f"histB{l}") for l in range(NL)]
    lruc = [k.sb([128, 2], F32, f"lruc{l}") for l in range(NL)]
    s5cr = [k.sb([128, 8], F32, f"s5cr{l}") for l in range(NL)]
    s5ci = [k.sb([128, 8], F32, f"s5ci{l}") for l in range(NL)]
    Pbd = [[k.sb([128, 128], F32, f"Pbd{l}_{hp}") for hp in range(4)] for l in range(NL)]
    small = k.sb([128, 64], F32, "small")
    PF = [k.ps([128, 512], F32, f"PF{i}") for i in range(4)]
    PHb = [k.ps([128, 512], F32, f"PHb{i}") for i in range(2)]
    PT = [k.ps([128, 1024], BF16, f"PTb{i}") for i in range(2)]
    scr = Scratch(k, min(nc.sbuf_bytes_remaining - 1024, 77 * 1024))
    ps_i = [0]

    pf_pool = [[PF[0], PF[1], PF[2]]]

    def pf():
        ps_i[0] += 1
        return pf_pool[0][ps_i[0] % len(pf_pool[0])]
    ph_i = [0]
    PHpool = [PHb[0], PHb[1], PF[0], PF[1], PF[2]]

    ph_pool = [PHpool]

    def ph():
        ph_i[0] += 1
        return ph_pool[0][ph_i[0] % len(ph_pool[0])]

    blocks = [Blk("p", i) for i in range(NPB)] + [Blk("s", 0)]
    sched = []
    NCH = 32
    for bi, B in enumerate(blocks):
        for l in range(NL):
            for c in range(5):
                sched.append((I["w_in"][l][:, c * 512:(c + 1) * 512].rearrange("(c p) n -> p c n", p=128), (8, 512)))
            for j in range(2):
                sched.append((I["w_out"][l][:, j * 512:(j + 1) * 512].rearrange("(c p) n -> p c n", p=128), (8, 512)))
            for p_ in range(11):
                sched.append((I["w_up"][l][:, p_ * 512:(p_ + 1) * 512].rearrange("(c p) n -> p c n", p=128), (8, 512)))
            for j in range(2):
                for g in range(6):
                    nfc = 4 if g < 5 else 2
                    sched.append((I["w_down"][l][g * 512:g * 512 + nfc * 128, j * 512:(j + 1) * 512]
                                  .rearrange("(c p) n -> p c n", p=128), (nfc, 512)))
            for j in range(2):
                sched.append((I["w_gate"][l][:, j * 512:(j + 1) * 512].rearrange("(c p) n -> p c n", p=128), (8, 512)))
    issued = [0]

    def wget(idx):
        while issued[0] < min(idx + NW, len(sched)):
            i = issued[0]
            src, (a, b) = sched[i]
            dst = wring[i % NW][:, 0:a * b].re("p (a b) -> p a b", a=a)
            k.dma("pool", dst, src)
            issued[0] += 1
        a, b = sched[idx][1]
        return wring[idx % NW][:, 0:a * b].re("p (a b) -> p a b", a=a)

    k.dma("sp", consts, I["consts"])
    for l in range(NL):
        k.dma("sp", par[l], I["params"][l])
    k.dma("sp", gfin, I["g_final"].to_broadcast([128, D]))
    for l in range(NL):
        k.dma("sp", lora1[l], I["lora1"][l])
        k.dma("sp", wg2[l], I["w_g2"][l])
        k.dma("sp", wrg[l], I["w_rg"][l].rearrange("c p n -> p c n"))
        k.dma("sp", wig[l], I["w_ig"][l].rearrange("c p n -> p c n"))
        k.dma("sp", wglu[l], I["w_glu"][l].rearrange("(c p) n -> p c n", p=128))
    k.copy(idb, ident, "dve")
    wget(0)
    for l in range(NL):
        k.memset(histA[l], 0.0)
        k.memset(histB[l], 0.0)
        k.memset(lruc[l], 0.0)
        k.memset(s5cr[l], 0.0)
        k.memset(s5ci[l], 0.0)
        for hp in range(4):
            k.memset(Pbd[l][hp], 0.0)

    def range_reduce(v):
        tq = S5(0, "t1") if False else small[:, 0:8]
        for m in range(6):
            k.ts(tq, v, math.pi, ALU.is_ge, -2 * math.pi, MUL)
            k.tt(v, v, tq, ADD)

    for l in range(NL):
        lr = P(l, "s5_lr", 0, 8)
        li = P(l, "s5_li", 0, 8)
        k.act(S5(l, "dt"), P(l, "s5_ldt", 0, 8), AF.Exp)
        k.tt(S5(l, "t0"), lr, S5(l, "dt"), MUL)
        k.act(S5(l, "mag"), S5(l, "t0"), AF.Exp)
        k.tt(S5(l, "ang"), li, S5(l, "dt"), MUL)
        k.copy(S5(l, "t0"), S5(l, "ang"), "dve")
        range_reduce(S5(l, "t0"))
        k.act(S5(l, "s1"), S5(l, "t0"), AF.Sin)
        k.ts(S5(l, "t0"), S5(l, "ang"), math.pi / 2, ADD)
        range_reduce(S5(l, "t0"))
        k.act(S5(l, "c1"), S5(l, "t0"), AF.Sin)
        k.tt(S5(l, "ar"), S5(l, "mag"), S5(l, "c1"), MUL)
        k.tt(S5(l, "ai"), S5(l, "mag"), S5(l, "s1"), MUL)
        k.ts(S5(l, "nai"), S5(l, "ai"), -1.0, MUL)
        k.tt(S5(l, "t0"), lr, lr, MUL)
        k.tt(S5(l, "t1"), li, li, MUL)
        k.tt(S5(l, "t0"), S5(l, "t0"), S5(l, "t1"), ADD)
        k.op("dve", "reciprocal", out=S5(l, "t0"), in_=S5(l, "t0"))
        k.ts(S5(l, "t1"), S5(l, "ar"), -1.0, ADD)
        k.tt(S5(l, "cr"), S5(l, "t1"), lr, MUL)
        k.tt(S5(l, "ci"), S5(l, "ai"), li, MUL)
        k.tt(S5(l, "cr"), S5(l, "cr"), S5(l, "ci"), ADD)
        k.tt(S5(l, "cr"), S5(l, "cr"), S5(l, "t0"), MUL)
        k.tt(S5(l, "ci"), S5(l, "ai"), lr, MUL)
        k.tt(S5(l, "t1"), S5(l, "t1"), li, MUL)
        k.tt(S5(l, "ci"), S5(l, "ci"), S5(l, "t1"), SUB)
        k.tt(S5(l, "ci"), S5(l, "ci"), S5(l, "t0"), MUL)
        k.memset(Tc[l][:, :, 0:1], 1.0)
        k.memset(Ts[l][:, :, 0:1], 0.0)
        cn, sn = S5(l, "c1"), S5(l, "s1")
        ta = tmpt[0][:, 0:8 * 64].re("p (a b) -> p a b", a=8)
        tb = tmpt[1][:, 0:8 * 64].re("p (a b) -> p a b", a=8)
        n = 1
        while n < 128:
            cb = cn.un(2).bc([128, 8, n])
            sb_ = sn.un(2).bc([128, 8, n])
            k.tt(ta[:, :, 0:n], Ts[l][:, :, 0:n], sb_, MUL)
            k.tt(tb[:, :, 0:n], Tc[l][:, :, 0:n], sb_, MUL)
            k.tt(Tc[l][:, :, n:2 * n], Tc[l][:, :, 0:n], cb, MUL)
            k.tt(Tc[l][:, :, n:2 * n], Tc[l][:, :, n:2 * n], ta[:, :, 0:n], SUB)
            k.tt(Ts[l][:, :, n:2 * n], Ts[l][:, :, 0:n], cb, MUL)
            k.tt(Ts[l][:, :, n:2 * n], Ts[l][:, :, n:2 * n], tb[:, :, 0:n], ADD)
            k.tt(S5(l, "t0"), cn, cn, MUL)
            k.tt(S5(l, "t1"), sn, sn, MUL)
            k.tt(sn, cn, sn, MUL)
            k.ts(sn, sn, 2.0, MUL)
            k.tt(cn, S5(l, "t0"), S5(l, "t1"), SUB)
            n *= 2
        c127, s127 = Tc[l][:, :, 127], Ts[l][:, :, 127]
        k.tt(S5(l, "Qr"), S5(l, "ar"), c127, MUL)
        k.tt(S5(l, "t0"), S5(l, "ai"), s127, MUL)
        k.tt(S5(l, "Qr"), S5(l, "Qr"), S5(l, "t0"), SUB)
        k.tt(S5(l, "Qi"), S5(l, "ar"), s127, MUL)
        k.tt(S5(l, "t0"), S5(l, "ai"), c127, MUL)
        k.tt(S5(l, "Qi"), S5(l, "Qi"), S5(l, "t0"), ADD)
        k.ts(S5(l, "nQi"), S5(l, "Qi"), -1.0, MUL)
        k.act(c8[l], P(l, "lam", 0, 2), AF.Exp, scale=-1.0)
        k.act(c8[l], c8[l], AF.Ln, bias=1.0)
        k.ts(c8[l], c8[l], -8.0, MUL)

    def rstd_of(ss_view, out_view, scale, eps):
        k.ts(out_view, ss_view, scale, MUL, eps, ADD)
        k.act(out_view, out_view, AF.Ln)
        k.act(out_view, out_view, AF.Exp, scale=-0.5)

    def norm_T(B, l, gname):
        for i in range(B.nt):
            st = stat[i % 2]
            xb = xn[i % 2]
            k.act(junk, h[:, i, :], AF.Square, accum_out=st[:, 0:1])
            rstd_of(st[:, 0:1], st[:, 1:2], 1.0 / D, RMS_EPS)
            k.act(xb, h[:, i, :], AF.Copy, scale=st[:, 1:2])
            for half in range(2):
                pt = PT[half]
                for c4 in range(4):
                    c = half * 4 + c4
                    k.tr(pt[:, c4 * 128:(c4 + 1) * 128], xb[:, c * 128:(c + 1) * 128], idb)
                for c4 in range(4):
                    c = half * 4 + c4
                    dst = xnT[:, c, i * 128:(i + 1) * 128]
                    if half == 0:
                        k.act(dst, pt[:, c4 * 128:(c4 + 1) * 128], AF.Copy, scale=P(l, gname, c))
                    else:
                        k.ts(dst, pt[:, c4 * 128:(c4 + 1) * 128], P(l, gname, c), MUL)

    def inproj(B, l, wbase, tile, evac):
        pos = COL_ORDER.index(tile)
        wc = wget(wbase + pos // 4)
        w0_ = (pos % 4) * 128
        ps = pf()
        for c in range(8):
            k.mm(ps[:, 0:B.ntok], wc[:, c, w0_:w0_ + 128], xnT[:, c, 0:B.ntok], start=(c == 0), stop=(c == 7))
        evac(ps[:, 0:B.ntok])

    def seq3(v, B):
        return v.re("p (s t) -> p s t", t=B.T)

    def rms_pair(B, l, ytiles, gname, mix0):
        nt_ = B.ntok
        ps = pf()
        for ct in range(2):
            sq = tmpt[ct]
            k.act(sq[:, 0:nt_], ytiles[ct], AF.Square)
            k.mm(ps[:, 0:nt_], C("ones", 128), sq[:, 0:nt_], start=(ct == 0), stop=(ct == 1))
        rs = tmpt[0]
        rstd_of(ps[:, 0:nt_], rs[:, 0:nt_], 1.0 / 256, RMS_EPS)
        for ct in range(2):
            k.stt(mixed[:, mix0 + ct, 0:nt_], ytiles[ct], P(l, gname, ct), rs[:, 0:nt_], MUL, MUL)

    small2 = k.sb([128, 64], F32, "small2")

    def mixer_c(B, l, wbase):
        nt_ = B.ntok
        scr.reset()
        s5w = [scr.alloc([128, 8, 128], F32, f"s5w{i}") for i in range(4)]
        u = scr.alloc([128, 2, TBP], F32, "u")
        bufs = []
        for si in range(2):
            bufs.append([scr.alloc([128, TBP], F32, f"{nm}{si}") for nm in ("t1", "t2", "bur", "bui", "gr", "gi")])
        yc = scr.alloc([128, 2, TBP], F32, "yc")
        z = scr.alloc([128, 2, TBP], F32, "z")
        coefs = [scr.alloc([128, 128], F32, f"coef{si}") for si in range(2)]
        st_r = scr.alloc([128, 8, 16], F32, "st_r")
        st_i = scr.alloc([128, 8, 16], F32, "st_i")
        so_r = scr.alloc([128, 8, 16], F32, "so_r")
        so_i = scr.alloc([128, 8, 16], F32, "so_i")
        for kind in range(4):
            k.dma("sp", s5w[kind], I["s5w"][l, kind])
        k.ts(s5w[3], s5w[3], -1.0, MUL)
        if B.kind == "s":
            k.dma("sp", st_r, I["st_s5re"][l])
            k.dma("sp", st_i, I["st_s5im"][l])
        for ct in range(2):
            inproj(B, l, wbase, 18 + ct, lambda ps, ct=ct: k.copy(u[:, ct, 0:nt_], ps))
        dump(f"u{l}", u[:, :, 0:nt_], [128, 2, nt_])
        if B.kind == "p":
            nseg, sl = 4, 128
        else:
            nseg, sl = 16, 8
        v3 = lambda T_: T_[:, 0:nt_].re("p (s t) -> p s t", t=sl)
        Yps = [PF[0], PF[1]]
        banks = [(PF[2], PF[3]), (PHb[0], PHb[1])]
        smalls = [small, small2]

        def tile_body(s, si):
            t1, t2, bur, bui, gr, gi = bufs[si]
            bre, bim = banks[si]
            sm = smalls[si]
            coef = coefs[si]
            ct = s // 4
            cr_, ci_ = S5(l, "cr")[:, s:s + 1], S5(l, "ci")[:, s:s + 1]
            ar_, ai_, nai_ = S5(l, "ar")[:, s:s + 1], S5(l, "ai")[:, s:s + 1], S5(l, "nai")[:, s:s + 1]
            Qr_, Qi_, nQi_ = S5(l, "Qr")[:, s:s + 1], S5(l, "Qi")[:, s:s + 1], S5(l, "nQi")[:, s:s + 1]
            k.mm(bre[:, 0:nt_], s5w[0][:, s, :], u[:, ct, 0:nt_])
            k.mm(bim[:, 0:nt_], s5w[1][:, s, :], u[:, ct, 0:nt_])
            k.act(t1[:, 0:nt_], bim[:, 0:nt_], AF.Copy, scale=ci_)
            k.stt(bur[:, 0:nt_], bre[:, 0:nt_], cr_, t1[:, 0:nt_], MUL, SUB)
            k.act(t2[:, 0:nt_], bre[:, 0:nt_], AF.Copy, scale=ci_)
            k.stt(bui[:, 0:nt_], bim[:, 0:nt_], cr_, t2[:, 0:nt_], MUL, ADD)
            tcb = Tc[l][:, s, 0:sl].un(1).bc([128, nseg, sl])
            tsb = Ts[l][:, s, 0:sl].un(1).bc([128, nseg, sl])
            k.tt(v3(t1), v3(bur), tcb, MUL)
            k.tt(v3(t2), v3(bui), tsb, MUL)
            k.tt(gr[:, 0:nt_], t1[:, 0:nt_], t2[:, 0:nt_], ADD)
            k.tt(v3(t1), v3(bui), tcb, MUL)
            k.tt(v3(t2), v3(bur), tsb, MUL)
            k.tt(gi[:, 0:nt_], t1[:, 0:nt_], t2[:, 0:nt_], SUB)
            magb = S5(l, "mag")[:, s:s + 1]
            if B.kind == "p":
                cre, cim = s5cr[l][:, s:s + 1], s5ci[l][:, s:s + 1]
                for sg in range(nseg):
                    c0 = sg * sl
                    if sg == 0:
                        k.stt(sm[:, 0:1], cre, ar_, gr[:, c0:c0 + 1], MUL, ADD)
                        k.stt(gr[:, c0:c0 + 1], cim, nai_, sm[:, 0:1], MUL, ADD)
                        k.stt(sm[:, 1:2], cim, ar_, gi[:, c0:c0 + 1], MUL, ADD)
                        k.stt(gi[:, c0:c0 + 1], cre, ai_, sm[:, 1:2], MUL, ADD)
                    else:
                        glr, gli = t1[:, c0 - 1:c0], t2[:, c0 - 1:c0]
                        k.stt(sm[:, 0:1], glr, Qr_, gr[:, c0:c0 + 1], MUL, ADD)
                        k.stt(gr[:, c0:c0 + 1], gli, nQi_, sm[:, 0:1], MUL, ADD)
                        k.stt(sm[:, 1:2], gli, Qr_, gi[:, c0:c0 + 1], MUL, ADD)
                        k.stt(gi[:, c0:c0 + 1], glr, Qi_, sm[:, 1:2], MUL, ADD)
                    k.op("dve", "tensor_tensor_scan", out=t1[:, c0:c0 + sl], data0=magb.bc([128, sl]),
                         data1=gr[:, c0:c0 + sl], initial=0.0, op0=MUL, op1=ADD)
                    k.op("dve", "tensor_tensor_scan", out=t2[:, c0:c0 + sl], data0=magb.bc([128, sl]),
                         data1=gi[:, c0:c0 + sl], initial=0.0, op0=MUL, op1=ADD)
                cl, sl_ = Tc[l][:, s, sl - 1:sl], Ts[l][:, s, sl - 1:sl]
                hrl, hil = t1[:, nt_ - 1:nt_], t2[:, nt_ - 1:nt_]
                k.ts(sm[:, 2:3], hil, sl_, MUL)
                k.stt(cre, hrl, cl, sm[:, 2:3], MUL, SUB)
                k.ts(sm[:, 3:4], hrl, sl_, MUL)
                k.stt(cim, hil, cl, sm[:, 3:4], MUL, ADD)
            else:
                g3r, g3i = v3(gr), v3(gi)
                hr0, hi0 = st_r[:, s, :], st_i[:, s, :]
                k.stt(sm[:, 0:16], hr0, ar_, g3r[:, :, 0], MUL, ADD)
                k.stt(g3r[:, :, 0], hi0, nai_, sm[:, 0:16], MUL, ADD)
                k.stt(sm[:, 16:32], hi0, ar_, g3i[:, :, 0], MUL, ADD)
                k.stt(g3i[:, :, 0], hr0, ai_, sm[:, 16:32], MUL, ADD)
                k.ts(coef, C("startmask", 128), magb, MUL)
                k.op("dve", "tensor_tensor_scan", out=t1[:, 0:nt_], data0=coef[:, 0:nt_],
                     data1=gr[:, 0:nt_], initial=0.0, op0=MUL, op1=ADD)
                k.op("dve", "tensor_tensor_scan", out=t2[:, 0:nt_], data0=coef[:, 0:nt_],
                     data1=gi[:, 0:nt_], initial=0.0, op0=MUL, op1=ADD)
            RB = dbg.get("rb_eng", "dve")
            k.tt(v3(gr), v3(t1), tcb, MUL, eng=RB)
            k.tt(v3(gi), v3(t2), tsb, MUL, eng=RB)
            k.tt(bur[:, 0:nt_], gr[:, 0:nt_], gi[:, 0:nt_], SUB, eng=RB)
            k.tt(v3(gr), v3(t2), tcb, MUL, eng=RB)
            k.tt(v3(gi), v3(t1), tsb, MUL, eng=RB)
            k.tt(bui[:, 0:nt_], gr[:, 0:nt_], gi[:, 0:nt_], ADD, eng=RB)
            if B.kind == "s":
                k.copy(so_r[:, s, :], v3(bur)[:, :, sl - 1], "dve")
                k.copy(so_i[:, s, :], v3(bui)[:, :, sl - 1], "dve")
            k.mm(Yps[ct][:, 0:nt_], s5w[2][:, s, :], bur[:, 0:nt_], start=(s % 4 == 0), stop=False)
            k.mm(Yps[ct][:, 0:nt_], s5w[3][:, s, :], bui[:, 0:nt_], start=False, stop=(s % 4 == 3))

        for s0 in range(0, 8, 2):
            k.start_rec()
            tile_body(s0, 0)
            sa = k.stop_rec()
            k.start_rec()
            tile_body(s0 + 1, 1)
            sb_ = k.stop_rec()
            k.interleave(sa, sb_)
        if B.kind == "s":
            k.dma("sp", O["o_s5re_s"][l], so_r, is_output=True)
            k.dma("sp", O["o_s5im_s"][l], so_i, is_output=True)
        elif B.last:
            k.dma("sp", O["o_s5re_p"][l], s5cr[l], is_output=True)
            k.dma("sp", O["o_s5im_p"][l], s5ci[l], is_output=True)
        for ct in range(2):
            k.stt(yc[:, ct, 0:nt_], u[:, ct, 0:nt_], P(l, "s5_d", ct), Yps[ct][:, 0:nt_], MUL, ADD)
            k.act(z[:, ct, 0:nt_], yc[:, ct, 0:nt_], AF.Gelu)
        for co in range(2):
            ps = PF[2 + co]
            for ci2 in range(2):
                k.mm(ps[:, 0:nt_], wglu[l][:, ci2, co * 128:(co + 1) * 128], z[:, ci2, 0:nt_],
                     start=(ci2 == 0), stop=(ci2 == 1))
            k.act(yc[:, co, 0:nt_], ps[:, 0:nt_], AF.Sigmoid, bias=P(l, "b_glu", co))
            k.tt(yc[:, co, 0:nt_], yc[:, co, 0:nt_], z[:, co, 0:nt_], MUL)
        dump(f"yc{l}", yc[:, :, 0:nt_], [128, 2, nt_])
        rms_pair(B, l, [yc[:, 0, 0:nt_], yc[:, 1, 0:nt_]], "g_out_c", 6)

    def mixer_b(B, l, wbase):
        nt_ = B.ntok
        k.barrier()
        scr.reset()
        Tn = B.T
        gate = scr.alloc([128, 2, TBP], F32, "gate")
        xbuf = [scr.alloc([128, B.nseq, 3 + Tn], F32, f"xbuf{ct}") for ct in range(2)]
        xc = scr.alloc([128, TBP], F32, "xc")
        gr_ = scr.alloc([128, TBP], F32, "gr_")
        gi_ = scr.alloc([128, TBP], F32, "gi_")
        a_ = scr.alloc([128, TBP], F32, "a_")
        b_ = scr.alloc([128, TBP], F32, "b_")
        yb = scr.alloc([128, 2, TBP], F32, "yb")
        so = scr.alloc([128, 2, 16], F32, "so")
        st = scr.alloc([128, 2, 16], F32, "st")
        sconv = scr.alloc([128, 2, 16, 3], F32, "sconv")
        if B.kind == "s":
            k.dma("sp", st, I["st_lru"][l])
            k.dma("sp", sconv, I["st_conv"][l])
        for ct in range(2):
            inproj(B, l, wbase, 14 + ct, lambda ps, ct=ct: k.copy(gate[:, ct, 0:nt_], ps))
        for ct in range(2):
            if B.kind == "p":
                k.copy(xbuf[ct][:, 0, 0:3], histB[l][:, ct, :], "dve")
            else:
                k.copy(xbuf[ct][:, :, 0:3], sconv[:, ct, :, :], "dve")
            inproj(B, l, wbase, 16 + ct, lambda ps, ct=ct: k.copy(xbuf[ct][:, :, 3:3 + Tn], seq3(ps, B)))
        for ct in range(2):
            xb = xbuf[ct]
            if B.kind == "p":
                k.copy(histB[l][:, ct, :], xb[:, 0, Tn:Tn + 3], "dve")
                if B.last:
                    k.dma("sp", O["o_conv_p"][l][:, ct, :], histB[l][:, ct, :], is_output=True)
            else:
                k.dma("sp", O["o_conv_s"][l][:, ct], xb[:, :, Tn:Tn + 3], is_output=True)
            xc3 = seq3(xc[:, 0:nt_], B)
            k.ts(xc3, xb[:, :, 0:Tn], P(l, "conv_w", ct * 4 + 0), MUL, P(l, "conv_b", ct), ADD)
            for j in range(1, 4):
                k.stt(xc3, xb[:, :, j:j + Tn], P(l, "conv_w", ct * 4 + j), xc3, MUL, ADD)
            ps1, ps2 = pf(), pf()
            k.mm(ps1[:, 0:nt_], wrg[l][:, ct, :], xc[:, 0:nt_])
            k.mm(ps2[:, 0:nt_], wig[l][:, ct, :], xc[:, 0:nt_])
            k.act(gr_[:, 0:nt_], ps1[:, 0:nt_], AF.Sigmoid, bias=P(l, "b_rg", ct))
            k.act(gi_[:, 0:nt_], ps2[:, 0:nt_], AF.Sigmoid, bias=P(l, "b_ig", ct))
            k.act(a_[:, 0:nt_], gr_[:, 0:nt_], AF.Exp, scale=c8[l][:, ct:ct + 1])
            k.tt(b_[:, 0:nt_], a_[:, 0:nt_], a_[:, 0:nt_], MUL)
            k.ts(b_[:, 0:nt_], b_[:, 0:nt_], -1.0, MUL, 1.0, ADD)
            k.act(b_[:, 0:nt_], b_[:, 0:nt_], AF.Sqrt)
            k.tt(b_[:, 0:nt_], b_[:, 0:nt_], gi_[:, 0:nt_], MUL)
            k.tt(b_[:, 0:nt_], b_[:, 0:nt_], xc[:, 0:nt_], MUL)
            hs = gr_
            if B.kind == "p":
                k.op("dve", "tensor_tensor_scan", out=hs[:, 0:nt_], data0=a_[:, 0:nt_], data1=b_[:, 0:nt_],
                     initial=lruc[l][:, ct:ct + 1], op0=MUL, op1=ADD)
                k.copy(lruc[l][:, ct:ct + 1], hs[:, nt_ - 1:nt_], "dve")
                if B.last and ct == 1:
                    k.dma("sp", O["o_lru_p"][l], lruc[l], is_output=True)
            else:
                a3, b3 = seq3(a_[:, 0:nt_], B), seq3(b_[:, 0:nt_], B)
                k.tt(small[:, 0:16], a3[:, :, 0], st[:, ct, :], MUL)
                k.tt(b3[:, :, 0], b3[:, :, 0], small[:, 0:16], ADD)
                k.tt(a_[:, 0:nt_], a_[:, 0:nt_], C("startmask", 128), MUL)
                k.op("dve", "tensor_tensor_scan", out=hs[:, 0:nt_], data0=a_[:, 0:nt_], data1=b_[:, 0:nt_],
                     initial=0.0, op0=MUL, op1=ADD)
                k.copy(so[:, ct, :], seq3(hs[:, 0:nt_], B)[:, :, Tn - 1], "dve")
            k.act(gi_[:, 0:nt_], gate[:, ct, 0:nt_], AF.Gelu)
            k.tt(yb[:, ct, 0:nt_], hs[:, 0:nt_], gi_[:, 0:nt_], MUL)
        if B.kind == "s":
            k.dma("sp", O["o_lru_s"][l], so, is_output=True)
        dump(f"yb{l}", yb[:, :, 0:nt_], [128, 2, nt_])
        rms_pair(B, l, [yb[:, 0, 0:nt_], yb[:, 1, 0:nt_]], "g_out_b", 4)

    D1 = [T(nc.dram_tensor(f"d1_{l}", [128, 6 * 512], F32, kind="Internal").ap(), f"d1_{l}") for l in range(NL)]
    D2 = [T(nc.dram_tensor(f"d2_{l}", [128, 512], F32, kind="Internal").ap(), f"d2_{l}") for l in range(NL)]
    EXPM05 = math.exp(-0.5)

    def mixer_a(B, l, wbase):
        nt_ = B.ntok
        Tn, ns = B.T, B.nseq
        prompt = B.kind == "p"
        k.barrier()
        scr.reset()
        lo = [scr.alloc([128, ns, 1 + Tn], F32, f"lo{j}") for j in range(2)]
        rkv = [scr.alloc([128, ns, 1 + Tn], F32, f"rkv{j}") for j in range(3)]
        dtmp = scr.alloc([128, nt_], F32, "dtmp")
        l12 = scr.alloc([128, nt_], F32, "l12")
        sgc = scr.alloc([128, nt_], F32, "sgc")
        xs = [scr.alloc([128, nt_], F32, f"xs{j}") for j in range(3)]
        UN = 256 if prompt else 128
        lw, cs, a_, g_, kk, kmod, beta, tA, tB, bonus, ysb, yc = [
            scr.alloc([128, UN], F32, nm) for nm in
            ("lw", "cs", "a_", "g_", "kk", "kmod", "beta", "tA", "tB", "bonus", "ysb", "yc")]
        gam = scr.alloc([128, 8], F32, "gam")
        if prompt:
            ARt = scr.alloc([128, 4, 192], BF16, "ARt")
            Bt_bd = scr.alloc([128, 4, 128], BF16, "Bt_bd")
            Kt_bd = scr.alloc([128, 4, 128], BF16, "Kt_bd")
            Vt_bd = scr.alloc([128, 4, 128], BF16, "Vt_bd")
            XM = scr.alloc([128, 4, 192], BF16, "XM")
            KM = scr.alloc([128, 4, 192], BF16, "KM")
            Btok = scr.alloc([128, 4, 128], BF16, "Btok")
            Ktok = scr.alloc([128, 4, 128], BF16, "Ktok")
            Vbd = scr.alloc([128, 4, 128], BF16, "Vbd")
            TT5 = scr.alloc([128, 4, 128], BF16, "TT5")
            Xr = [scr.alloc([128, 4, 128], BF16, f"Xr{i}") for i in range(2)]
            Zr = [scr.alloc([128, 4, 128], BF16, f"Zr{i}") for i in range(2)]
            TTt = [scr.alloc([128, 4, 128], BF16, f"TTt{i}") for i in range(2)]
            Wsb = scr.alloc([128, 128], BF16, "Wsb")
            Ubd = scr.alloc([128, 128], BF16, "Ubd")
            Pb16 = scr.alloc([128, 128], BF16, "Pb16")
            for t_ in (ARt, Bt_bd, Kt_bd, Vt_bd):
                k.memset(t_, 0.0)
        else:
            S = scr.alloc([128, 4096], F32, "S")
            tmp3 = scr.alloc([128, 4096], F32, "tmp3")
            RW = scr.alloc([128, 8, 6, 64], F32, "RW")
            KKA = scr.alloc([128, 8, 64], F32, "KKA")
            Ys = scr.alloc([128, 8, 64], F32, "Ys")
            TM = T(tmp3.ap[:, 0:3072].rearrange("p (q h j) -> p q h j", q=6, h=4), "TM", scr=True)
            YT = scr.alloc([128, 8, 64], F32, "YT")
            skk = scr.alloc([128, 64], F32, "skk")
            bonus_s = scr.alloc([128, 4, 128], F32, "bonus_s")
            g_s = scr.alloc([128, 4, 128], F32, "g_s")

        def load_shift(raw, tile):
            if prompt:
                k.copy(raw[:, 0, 0:1], histA[l][:, tile:tile + 1], "dve")
            else:
                k.dma("sp", raw[:, :, 0], I["st_shift"][l][:, tile, :])
            inproj(B, l, wbase, tile, lambda ps: k.copy(raw[:, :, 1:1 + Tn], seq3(ps, B)))
            if prompt:
                k.copy(histA[l][:, tile:tile + 1], raw[:, 0, Tn:Tn + 1], "dve")
            else:
                k.dma("sp", O["o_shift_s"][l][:, tile, :], raw[:, :, Tn], is_output=True)

        def shift(raw, tile, dst):
            k.tt(seq3(dtmp[:, 0:nt_], B), raw[:, :, 0:Tn], raw[:, :, 1:1 + Tn], SUB)
            k.stt(seq3(dst[:, 0:nt_], B), seq3(dtmp[:, 0:nt_], B), P(l, "mu", tile), raw[:, :, 1:1 + Tn], MUL, ADD)

        load_shift(lo[0], 12)
        shift(lo[0], 12, l12)
        load_shift(lo[1], 13)
        shift(lo[1], 13, sgc)
        k.act(l12[0:64, 0:nt_], l12[0:64, 0:nt_], AF.Tanh)
        k.act(sgc[:, 0:nt_], sgc[:, 0:nt_], AF.Sigmoid)

        units = [(0, 256), (256, 256)] if prompt else [(0, 128)]
        for hp in range(4):
            Pb = Pbd[l][hp]
            if prompt:
                k.copy(Pb16, Pb, "dve")
            for j, tile in enumerate((hp, 4 + hp, 8 + hp)):
                load_shift(rkv[j], tile)
                shift(rkv[j], tile, xs[j])
            if l == 0 and hp == 0:
                dump("xs0", xs[0][:, 0:nt_], [128, nt_])
            hsl = slice(hp * 128, (hp + 1) * 128)
            for (c0, n) in units:
                sl = slice(c0, c0 + n)
                xr_, xk_, xv_ = xs[0][:, sl], xs[1][:, sl], xs[2][:, sl]
                N_ = slice(0, n)
                ps = ph()
                k.mm(ps[:, N_], lora1[l][0:64, hsl], l12[0:64, sl])
                k.act(lw[:, N_], ps[:, N_], AF.Sigmoid, bias=P(l, "w0", hp))
                k.ts(lw[:, N_], lw[:, N_], -EXPM05, MUL)
                ps = ph()
                k.mm(ps[:, N_], lora1[l][64:128, hsl], l12[64:128, sl])
                k.act(a_[:, N_], ps[:, N_], AF.Sigmoid, bias=P(l, "a0", hp))
                ps = ph()
                k.mm(ps[:, N_], wg2[l][:, hsl], sgc[:, sl])
                gdst = g_[:, N_] if prompt else g_s[:, hp, :]
                k.copy(gdst, ps[:, N_])
                k.ts(kk[:, N_], xk_, P(l, "k_k", hp), MUL)
                k.tt(tA[:, N_], kk[:, N_], kk[:, N_], MUL)
                ps = ph()
                k.mm(ps[:, N_], C("onesblk", 128), tA[:, N_])
                k.ts(tB[:, N_], ps[:, N_], 1e-24, ALU.max)
                k.act(tB[:, N_], tB[:, N_], AF.Ln)
                k.act(tB[:, N_], tB[:, N_], AF.Exp, scale=-0.5)
                k.tt(kk[:, N_], kk[:, N_], tB[:, N_], MUL)
                k.ts(tA[:, N_], a_[:, N_], -1.0, ADD, P(l, "k_a", hp), MUL)
                k.stt(kmod[:, N_], tA[:, N_], 1.0, xk_, ADD, MUL)
                k.tt(beta[:, N_], kk[:, N_], a_[:, N_], MUL)
                k.stt(tA[:, N_], xr_, P(l, "r_k", hp), kmod[:, N_], MUL, MUL)
                ps = ph()
                k.mm(ps[:, N_], C("onesblk", 128), tA[:, N_])
                bdst = bonus[:, N_] if prompt else bonus_s[:, hp, :]
                k.tt(bdst, ps[:, N_], xv_, MUL)
                if prompt:
                    ch = lambda v: v.re("p (c t) -> p c t", t=64)
                    k.op("dve", "tensor_tensor_scan", out=cs[:, N_], data0=C("cumcoef", 256), data1=lw[:, N_],
                         initial=0.0, op0=MUL, op1=ADD)
                    k.tt(lw[:, N_], cs[:, N_], lw[:, N_], SUB)
                    k.act(tA[:, N_], cs[:, N_], AF.Exp)
                    k.copy(gam[:, 0:4], tA[:, 63:256:64], "dve")
                    k.tt(ARt[:, :, 128:192], ch(xr_), ch(tA[:, N_]), MUL)
                    k.act(tB[:, N_], cs[:, N_], AF.Exp, scale=-1.0)
                    for hh in range(2):
                        rows = slice(hh * 64, hh * 64 + 64)
                        cols = slice(hh * 64, hh * 64 + 64)
                        k.tt(Bt_bd[rows, :, cols], ch(beta[rows, N_]), ch(tB[rows, N_]), MUL)
                        k.tt(Kt_bd[rows, :, cols], ch(kmod[rows, N_]), ch(tB[rows, N_]), MUL)
                        k.copy(Vt_bd[rows, :, cols], ch(xs[2][rows, sl]), "act")
                    k.act(tA[:, N_], lw[:, N_], AF.Exp)
                    for hh in range(2):
                        rows = slice(hh * 64, hh * 64 + 64)
                        cols = slice(hh * 64, hh * 64 + 64)
                        k.stt(ARt[rows, :, cols], ch(kk[rows, N_]), -1.0, ch(tA[rows, N_]), MUL, MUL)
                    for c in range(4):
                        psa = ph()
                        k.mm(psa[:, 0:192], Bt_bd[:, c, :], ARt[:, c, :])
                        k.tt(XM[:, c, :], psa[:, 0:192], C("maskA", 192), MUL)
                        psb = ph()
                        k.mm(psb[:, 0:192], Kt_bd[:, c, :], ARt[:, c, :])
                        k.tt(KM[:, c, :], psb[:, 0:192], C("maskA", 192), MUL)
                        psc = ph()
                        k.mm(psc[:, 0:128], ARt[:, c, 0:128], Bt_bd[:, c, :])
                        k.tt(Zr[0][:, c, :], psc[:, 0:128], C("maskSL", 128), MUL)
                        for (src, dst) in ((Bt_bd, Btok), (Kt_bd, Ktok), (Vt_bd, Vbd)):
                            pst = ph()
                            pst16 = V(pst, pst.ap.bitcast(BF16))
                            k.tr(pst16[:, 0:128], src[:, c, :], idb)
                            k.copy(dst[:, c, :], pst16[:, 0:128])
                        k.tt(TTt[0][:, c, :], XM[:, c, 0:128], ident, ADD)
                    for r in range(1, 7):
                        for c in range(4):
                            Zp = Zr[(r - 1) % 2][:, c, :]
                            Xp = XM[:, c, 0:128] if r == 1 else Xr[(r - 1) % 2][:, c, :]
                            if r <= 4:
                                ps1 = ph()
                                k.mm(ps1[:, 0:128], Zp, Xp)
                                k.copy(Xr[r % 2][:, c, :], ps1[:, 0:128])
                            if 2 <= r <= 5:
                                psT = ph()
                                k.mm(psT[:, 0:128], Zp, TTt[(r - 2) % 2][:, c, :])
                                k.tt(TTt[(r - 1) % 2][:, c, :], psT[:, 0:128], TTt[(r - 2) % 2][:, c, :], ADD)
                            if r <= 5:
                                ps2 = ph()
                                k.mm(ps2[:, 0:128], Xp, Zp)
                                k.copy(Zr[r % 2][:, c, :], ps2[:, 0:128])
                            if r == 6:
                                psT = ph()
                                k.mm(psT[:, 0:128], Zp, TTt[0][:, c, :])
                                k.tt(TT5[:, c, :], psT[:, 0:128], TTt[0][:, c, :], ADD)
                    Yps = PF[3]
                    for c in range(4):
                        Wps = ph()
                        k.mm(Wps[:, 0:128], ARt[:, c, 0:128], Pb16, True, False)
                        k.mm(Wps[:, 0:128], KM[:, c, 0:128], Vbd[:, c, :], False, True)
                        k.copy(Wsb, Wps[:, 0:128])
                        k.ts(Pb, Pb, gam[:, c:c + 1], MUL)
                        Ups = ph()
                        k.mm(Ups[:, 0:128], TT5[:, c, :], Wsb)
                        k.copy(Ubd, Ups[:, 0:128])
                        ycol = Yps[:, c * 64:(c + 1) * 64]
                        k.mm(ycol, Pb16, ARt[:, c, 128:192], True, False)
                        k.mm(ycol, Ubd, XM[:, c, 128:192], False, False)
                        k.mm(ycol, Vbd[:, c, :], KM[:, c, 128:192], False, True)
                        Pps = ph()
                        k.mm(Pps[:, 0:128], Btok[:, c, :], Ubd, True, False)
                        k.mm(Pps[:, 0:128], Ktok[:, c, :], Vbd[:, c, :], False, True)
                        k.stt(Pb, Pps[:, 0:128], gam[:, c:c + 1], Pb, MUL, ADD)
                        k.copy(Pb16, Pb, "dve")
                    k.copy(ysb[:, N_], Yps[:, N_])
                    finish_unit(l, hp, sl, N_, ysb, yc, tA, tB, bonus[:, N_], g_[:, N_])
                else:
                    k.act(tA[:, N_], lw[:, N_], AF.Exp)
                    for qi, src in enumerate((xr_, tA[:, N_], kmod[:, N_], xv_, kk[:, N_], a_[:, N_])):
                        pst = ph()
                        k.tr(pst[:, 0:128], src, ident)
                        k.copy(TM[:, qi, hp, :], pst[:, 0:128])
            if prompt and B.last:
                for hh in range(2):
                    rows = slice(hh * 64, hh * 64 + 64)
                    k.dma("sp", O["o_wkv_p"][l][2 * hp + hh], Pb[rows, rows], is_output=True)
        if prompt:
            if B.last:
                k.dma("sp", O["o_shift_p"][l], histA[l], is_output=True)
            return
        k.dma("sp", D1[l], TM.re("p q h j -> p (q h j)"))
        for b in range(SB_PER):
            src = D1[l][b * 8:(b + 1) * 8, :].re("t (q h j) -> h (t q) j", q=6, h=8)
            k.dma("sp", RW[b * 8:(b + 1) * 8].re("p t q j -> p (t q) j"), src)
        k.tt(KKA, RW[:, :, 4, :], RW[:, :, 5, :], MUL)
        ISPL = dbg.get("ispl", 64)
        parts = []
        for (i0, i1, eng) in ((0, ISPL, "dve"), (ISPL, 64, "pool")):
            if i1 <= i0:
                continue
            ni = i1 - i0
            S_ = T(S.ap[:, i0 * 64:i1 * 64].rearrange("p (i j) -> p i j", j=64), f"S_{eng}", scr=True)
            T_ = T(tmp3.ap[:, i0 * 64:i1 * 64].rearrange("p (i j) -> p i j", j=64), f"T3_{eng}", scr=True)
            k_ = T(skk.ap[:, i0:i1], f"skk_{eng}", scr=True)
            Y_ = T(Ys.ap[:, :, i0:i1], f"Ys_{eng}", scr=True)
            k.dma("sp", S_.re("p i j -> p (i j)"), I["st_wkv"][l][:, i0 * 64:i1 * 64])
            parts.append((i0, i1, ni, eng, S_, T_, k_, Y_))
        for t in range(DEC_T):
            for (i0, i1, ni, eng, S_, T_, k_, Y_) in parts:
                bi = lambda v: v.un(1).bc([128, ni, 64])
                bj = lambda v: v.un(2).bc([128, ni, 64])
                k.tt(T_, S_, bi(RW[:, t, 4, :]), MUL, eng=eng)
                k.op("dve", "tensor_reduce", out=k_, in_=T_, axis=AX.X, op=ADD)
                k.tt(S_, S_, bi(RW[:, t, 1, :]), MUL, eng=eng)
                k.tt(T_, bj(k_), bi(KKA[:, t, :]), MUL, eng=eng)
                k.tt(S_, S_, T_, SUB, eng=eng)
                k.tt(T_, bj(RW[:, t, 3, i0:i1]), bi(RW[:, t, 2, :]), MUL, eng=eng)
                k.tt(S_, S_, T_, ADD, eng=eng)
                k.tt(T_, S_, bi(RW[:, t, 0, :]), MUL, eng=eng)
                k.op("dve", "tensor_reduce", out=Y_[:, t, :], in_=T_, axis=AX.X, op=ADD)
        for (i0, i1, ni, eng, S_, T_, k_, Y_) in parts:
            k.dma("sp", O["o_wkv_s"][l][:, i0 * 64:i1 * 64], S_.re("p i j -> p (i j)"), is_output=True)
        Ysd = [p_[7] for p_ in parts]
        for (i0, i1, ni, eng, S_, T_, k_, Y_) in parts:
            k.dma("sp", D2[l].re("p (t i) -> p t i", t=8)[:, :, i0:i1], Y_)
        for b in range(SB_PER):
            src = D2[l][b * 8:(b + 1) * 8, :].re("h (t i) -> t h i", t=8)
            k.dma("sp", YT[b * 8:(b + 1) * 8], src)
        N_ = slice(0, 128)
        for hp in range(4):
            pst = ph()
            k.tr(pst[:, 0:128], YT[:, 2 * hp:2 * hp + 2, :].re("p h i -> p (h i)"), ident)
            k.copy(ysb[:, N_], pst[:, 0:128])
            finish_unit(l, hp, slice(0, 128), N_, ysb, yc, tA, tB, bonus_s[:, hp, :], g_s[:, hp, :])

    def mixer_a_prompt(B, l, wbase):
        nt_, Tn = B.ntok, B.T
        k.barrier()
        scr.reset()
        lo = [scr.alloc([128, 1, 1 + Tn], F32, f"lo{j}") for j in range(2)]
        rkv = [scr.alloc([128, 1, 1 + Tn], F32, f"rkv{j}") for j in range(3)]
        dtmp = scr.alloc([128, nt_], F32, "dtmp")
        l12 = scr.alloc([128, nt_], F32, "l12")
        sgc = scr.alloc([128, nt_], F32, "sgc")
        xs = [scr.alloc([128, nt_], F32, f"xs{j}") for j in range(3)]
        UN = 256
        lw, cs, a_, kk, kmod, beta, tA, tB, ysb, yc, tAf, tBf = [
            scr.alloc([128, UN], F32, nm) for nm in
            ("lw", "cs", "a_", "kk", "kmod", "beta", "tA", "tB", "ysb", "yc", "tAf", "tBf")]
        sets3, sets2e, sets2g = [], [], []
        for si in range(3):
            d = {}
            d["ARt"] = scr.alloc([128, 4, 192], BF16, f"ARt{si}")
            d["gam"] = scr.alloc([128, 8], F32, f"gam{si}")
            d["bonus"] = scr.alloc([128, UN], F32, f"bonus{si}")
            d["g_"] = scr.alloc([128, UN], F32, f"g_{si}")
            sets3.append(d)
        for si in range(2):
            d = {}
            for nm in ("Bt_bd", "Kt_bd", "Vt_bd"):
                d[nm] = scr.alloc([128, 4, 128], BF16, f"{nm}{si}")
            sets2e.append(d)
            d = {}
            d["MrT"] = scr.alloc([128, 8, 64], BF16, f"MrT{si}")
            for nm in ("Xs", "MakT", "Btok", "Ktok", "Vbd", "TT5"):
                d[nm] = scr.alloc([128, 4, 128], BF16, f"{nm}{si}")
            sets2g.append(d)
        Xr = [scr.alloc([128, 4, 128], BF16, f"Xr{i}") for i in range(2)]
        Zr = [scr.alloc([128, 4, 128], BF16, f"Zr{i}") for i in range(2)]
        TTt = [scr.alloc([128, 4, 128], BF16, f"TTt{i}") for i in range(2)]
        Wsb = scr.alloc([128, 128], BF16, "Wsb")
        Ubd = scr.alloc([128, 128], BF16, "Ubd")
        Pb16 = scr.alloc([128, 128], BF16, "Pb16")
        for d in sets3:
            k.memset(d["ARt"], 0.0)
        for d in sets2e:
            for nm in ("Bt_bd", "Kt_bd", "Vt_bd"):
                k.memset(d[nm], 0.0)

        def load_shift(raw, tile):
            k.copy(raw[:, 0, 0:1], histA[l][:, tile:tile + 1], "dve")
            inproj(B, l, wbase, tile, lambda ps: k.copy(raw[:, :, 1:1 + Tn], seq3(ps, B)))
            k.copy(histA[l][:, tile:tile + 1], raw[:, 0, Tn:Tn + 1], "dve")

        def shift(raw, tile, dst):
            k.tt(seq3(dtmp[:, 0:nt_], B), raw[:, :, 0:Tn], raw[:, :, 1:1 + Tn], SUB)
            k.stt(seq3(dst[:, 0:nt_], B), seq3(dtmp[:, 0:nt_], B), P(l, "mu", tile), raw[:, :, 1:1 + Tn], MUL, ADD)

        load_shift(lo[0], 12)
        shift(lo[0], 12, l12)
        load_shift(lo[1], 13)
        shift(lo[1], 13, sgc)
        k.act(l12[0:64, 0:nt_], l12[0:64, 0:nt_], AF.Tanh)
        k.act(sgc[:, 0:nt_], sgc[:, 0:nt_], AF.Sigmoid)
        ch = lambda v: v.re("p (c t) -> p c t", t=64)

        def ew(ui):
            hp, sub = ui // 2, ui % 2
            D3, D2e = sets3[ui % 3], sets2e[ui % 2]
            ARt, gam, bonus, g_ = D3["ARt"], D3["gam"], D3["bonus"], D3["g_"]
            Bt_bd, Kt_bd, Vt_bd = D2e["Bt_bd"], D2e["Kt_bd"], D2e["Vt_bd"]
            if sub == 0:
                for j, tile in enumerate((hp, 4 + hp, 8 + hp)):
                    load_shift(rkv[j], tile)
                    shift(rkv[j], tile, xs[j])
            hsl = slice(hp * 128, (hp + 1) * 128)
            c0, n = sub * 256, 256
            sl = slice(c0, c0 + n)
            xr_, xk_, xv_ = xs[0][:, sl], xs[1][:, sl], xs[2][:, sl]
            N_ = slice(0, n)
            ps = ph()
            k.mm(ps[:, N_], lora1[l][0:64, hsl], l12[0:64, sl])
            k.act(lw[:, N_], ps[:, N_], AF.Sigmoid, bias=P(l, "w0", hp))
            k.act(lw[:, N_], lw[:, N_], AF.Copy, scale=-EXPM05)
            ps = ph()
            k.mm(ps[:, N_], lora1[l][64:128, hsl], l12[64:128, sl])
            k.act(a_[:, N_], ps[:, N_], AF.Sigmoid, bias=P(l, "a0", hp))
            ps = ph()
            k.mm(ps[:, N_], wg2[l][:, hsl], sgc[:, sl])
            k.copy(g_[:, N_], ps[:, N_])
            k.act(kk[:, N_], xk_, AF.Copy, scale=P(l, "k_k", hp))
            k.tt(tA[:, N_], kk[:, N_], kk[:, N_], MUL)
            ps = ph()
            k.mm(ps[:, N_], C("onesblk", 128), tA[:, N_])
            k.ts(tB[:, N_], ps[:, N_], 1e-24, ALU.max)
            k.act(tB[:, N_], tB[:, N_], AF.Ln)
            k.act(tB[:, N_], tB[:, N_], AF.Exp, scale=-0.5)
            k.tt(kk[:, N_], kk[:, N_], tB[:, N_], MUL)
            k.ts(tA[:, N_], a_[:, N_], -1.0, ADD, P(l, "k_a", hp), MUL)
            k.stt(kmod[:, N_], tA[:, N_], 1.0, xk_, ADD, MUL)
            k.tt(beta[:, N_], kk[:, N_], a_[:, N_], MUL)
            k.stt(tA[:, N_], xr_, P(l, "r_k", hp), kmod[:, N_], MUL, MUL)
            ps = ph()
            k.mm(ps[:, N_], C("onesblk", 128), tA[:, N_])
            k.tt(bonus[:, N_], ps[:, N_], xv_, MUL)
            k.op("dve", "tensor_tensor_scan", out=cs[:, N_], data0=C("cumcoef", 256), data1=lw[:, N_],
                 initial=0.0, op0=MUL, op1=ADD)
            k.tt(lw[:, N_], cs[:, N_], lw[:, N_], SUB)
            k.act(tA[:, N_], cs[:, N_], AF.Exp)
            k.copy(gam[:, 0:4], tA[:, 63:256:64], "dve")
            k.tt(ARt[:, :, 128:192], ch(xr_), ch(tA[:, N_]), MUL)
            k.act(tB[:, N_], cs[:, N_], AF.Exp, scale=-1.0)
            for hh in range(2):
                rows = slice(hh * 64, hh * 64 + 64)
                cols = slice(hh * 64, hh * 64 + 64)
                k.tt(Bt_bd[rows, :, cols], ch(beta[rows, N_]), ch(tB[rows, N_]), MUL)
                k.tt(Kt_bd[rows, :, cols], ch(kmod[rows, N_]), ch(tB[rows, N_]), MUL)
                k.copy(Vt_bd[rows, :, cols], ch(xs[2][rows, sl]), "act")
            k.act(tA[:, N_], lw[:, N_], AF.Exp)
            for hh in range(2):
                rows = slice(hh * 64, hh * 64 + 64)
                cols = slice(hh * 64, hh * 64 + 64)
                k.stt(ARt[rows, :, cols], ch(kk[rows, N_]), -1.0, ch(tA[rows, N_]), MUL, MUL)

        def gn(ui):
            D3, D2e, D2g = sets3[ui % 3], sets2e[ui % 2], sets2g[ui % 2]
            ARt = D3["ARt"]
            Bt_bd, Kt_bd, Vt_bd = D2e["Bt_bd"], D2e["Kt_bd"], D2e["Vt_bd"]
            Xs, MakT, MrT, Btok, Ktok, Vbd, TT5 = (D2g[n] for n in ("Xs", "MakT", "MrT", "Btok", "Ktok", "Vbd", "TT5"))
            c4 = lambda v: v[:, 0:512].re("p (c t) -> p c t", c=4)
            mSU = C("maskA", 128).un(1).bc([128, 4, 128])
            mSL = C("maskSL", 128).un(1).bc([128, 4, 128])
            mIU = consts[:, CC["maskA"] + 128:CC["maskA"] + 192].un(1).bc([128, 8, 64])
            idb4 = ident.un(1).bc([128, 4, 128])
            bX = ph()
            for c in range(4):
                k.mm(bX[:, c * 128:(c + 1) * 128], Bt_bd[:, c, :], ARt[:, c, 0:128])
            k.tt(Xs, c4(bX), mSU, MUL)
            bZ = ph()
            for c in range(4):
                k.mm(bZ[:, c * 128:(c + 1) * 128], ARt[:, c, 0:128], Bt_bd[:, c, :])
            k.tt(Zr[0], c4(bZ), mSL, MUL)
            bK = ph()
            for c in range(4):
                k.mm(bK[:, c * 128:(c + 1) * 128], Kt_bd[:, c, :], ARt[:, c, 0:128])
            k.tt(MakT, c4(bK), mSU, MUL)
            bR = ph()
            for c in range(4):
                k.mm(bR[:, c * 64:(c + 1) * 64], Bt_bd[:, c, :], ARt[:, c, 128:192])
            for c in range(4):
                k.mm(bR[:, 256 + c * 64:256 + (c + 1) * 64], Kt_bd[:, c, :], ARt[:, c, 128:192])
            k.tt(MrT, bR[:, 0:512].re("p (c t) -> p c t", c=8), mIU, MUL)
            k.tt(TTt[0], Xs, idb4, ADD)
            for (src, dst) in ((Bt_bd, Btok), (Kt_bd, Ktok), (Vt_bd, Vbd)):
                pst = ph()
                pt_ = pst.t if isinstance(pst, V) else pst
                pst16 = V(pt_, pst.ap.bitcast(BF16))
                for c in range(4):
                    k.tr(pst16[:, c * 128:(c + 1) * 128], src[:, c, :], idb)
                k.copy(dst, pst16[:, 0:512].re("p (c t) -> p c t", c=4))
            for r in range(1, 7):
                Zp = Zr[(r - 1) % 2]
                Xp = Xs if r == 1 else Xr[(r - 1) % 2]
                if r <= 4:
                    bA = ph()
                    for c in range(4):
                        k.mm(bA[:, c * 128:(c + 1) * 128], Zp[:, c, :], Xp[:, c, :])
                    k.copy(Xr[r % 2], c4(bA), "act")
                if 2 <= r <= 5:
                    bB = ph()
                    for c in range(4):
                        k.mm(bB[:, c * 128:(c + 1) * 128], idb, TTt[(r - 2) % 2][:, c, :], True, False)
                        k.mm(bB[:, c * 128:(c + 1) * 128], Zp[:, c, :], TTt[(r - 2) % 2][:, c, :], False, True)
                    k.copy(TTt[(r - 1) % 2], c4(bB), "act")
                if r <= 5:
                    bC = ph()
                    for c in range(4):
                        k.mm(bC[:, c * 128:(c + 1) * 128], Xp[:, c, :], Zp[:, c, :])
                    k.copy(Zr[r % 2], c4(bC), "act")
                if r == 6:
                    bB = ph()
                    for c in range(4):
                        k.mm(bB[:, c * 128:(c + 1) * 128], idb, TTt[0][:, c, :], True, False)
                        k.mm(bB[:, c * 128:(c + 1) * 128], Zp[:, c, :], TTt[0][:, c, :], False, True)
                    k.copy(TT5, c4(bB), "act")

        def tail(ui):
            hp, sub = ui // 2, ui % 2
            D3, D2g = sets3[ui % 3], sets2g[ui % 2]
            ARt, gam, bonus, g_ = D3["ARt"], D3["gam"], D3["bonus"], D3["g_"]
            MakT, MrT, Btok, Ktok, Vbd, TT5 = (D2g[n] for n in ("MakT", "MrT", "Btok", "Ktok", "Vbd", "TT5"))
            Pb = Pbd[l][hp]
            c0, n = sub * 256, 256
            sl = slice(c0, c0 + n)
            N_ = slice(0, n)
            if sub == 0:
                k.copy(Pb16, Pb, "dve")
            Yps = PF[3]
            for c in range(4):
                Wps = ph()
                k.mm(Wps[:, 0:128], ARt[:, c, 0:128], Pb16, True, False)
                k.mm(Wps[:, 0:128], MakT[:, c, :], Vbd[:, c, :], False, True)
                k.copy(Wsb, Wps[:, 0:128], "act")
                k.ts(Pb, Pb, gam[:, c:c + 1], MUL)
                Ups = ph()
                k.mm(Ups[:, 0:128], TT5[:, c, :], Wsb)
                k.copy(Ubd, Ups[:, 0:128], "act")
                ycol = Yps[:, c * 64:(c + 1) * 64]
                k.mm(ycol, Pb16, ARt[:, c, 128:192], True, False)
                k.mm(ycol, Ubd, MrT[:, c, :], False, False)
                k.mm(ycol, Vbd[:, c, :], MrT[:, 4 + c, :], False, True)
                Pps = ph()
                k.mm(Pps[:, 0:128], Btok[:, c, :], Ubd, True, False)
                k.mm(Pps[:, 0:128], Ktok[:, c, :], Vbd[:, c, :], False, True)
                k.stt(Pb, Pps[:, 0:128], gam[:, c:c + 1], Pb, MUL, ADD)
                k.copy(Pb16, Pb, "dve")
            k.copy(ysb[:, N_], Yps[:, N_])
            finish_unit(l, hp, sl, N_, ysb, yc, tAf, tBf, bonus[:, N_], g_[:, N_])
            if sub == 1 and B.last:
                for hh in range(2):
                    rows = slice(hh * 64, hh * 64 + 64)
                    k.dma("sp", O["o_wkv_p"][l][2 * hp + hh], Pb[rows, rows], is_output=True)

        PTf = [V(PT[i], PT[i].ap.bitcast(F32)) for i in range(2)]
        pools = {"tail": ([PHb[0]], None), "gn": ([PHb[1], PF[0], PTf[0], PTf[1]], None),
                 "ew": ([PF[1], PF[2]], [PF[1], PF[2]])}
        pf_save = pf_pool[0]

        def rec(kind, fnc, ui):
            if ui is None or ui > 7:
                return []
            ph_pool[0] = pools[kind][0]
            if pools[kind][1] is not None:
                pf_pool[0] = pools[kind][1]
            k.start_rec()
            fnc(ui)
            return k.stop_rec()

        k.interleave(rec("ew", ew, 0))
        k.interleave(rec("gn", gn, 0), rec("ew", ew, 1))
        for ui in range(8):
            k.interleave(rec("tail", tail, ui), rec("gn", gn, ui + 1), rec("ew", ew, ui + 2))
        pf_pool[0] = pf_save
        ph_pool[0] = PHpool
        if B.last:
            k.dma("sp", O["o_shift_p"][l], histA[l], is_output=True)

    def finish_unit(l, hp, sl, N_, ysb, yc, tA, tB, bonus_v, g_v):
        ps = ph()
        k.mm(ps[:, N_], C("avgblk", 128), ysb[:, N_])
        k.tt(yc[:, N_], ysb[:, N_], ps[:, N_], SUB)
        k.act(tA[:, N_], yc[:, N_], AF.Square)
        ps = ph()
        k.mm(ps[:, N_], C("avgblk", 128), tA[:, N_])
        rstd_of(ps[:, N_], tB[:, N_], 1.0, GN_EPS)
        k.tt(yc[:, N_], yc[:, N_], tB[:, N_], MUL)
        k.act(yc[:, N_], yc[:, N_], AF.Identity, scale=P(l, "lnx_w", hp), bias=P(l, "lnx_b", hp))
        k.tt(yc[:, N_], yc[:, N_], bonus_v, ADD)
        k.tt(mixed[:, hp, sl], yc[:, N_], g_v, MUL)

    def outproj(B, l, wbase):
        for j in range(2):
            wc = wget(wbase + 5 + j)
            for i in range(B.nt):
                ps = pf()
                for m in range(8):
                    k.mm(ps, mixed[:, m, i * 128:(i + 1) * 128], wc[:, m, :], start=(m == 0), stop=(m == 7))
                hv = h[:, i, j * 512:(j + 1) * 512]
                k.tt(hv, hv, ps, ADD)

    ffn_bufs = {}

    def ffn_prepare():
        k.barrier()
        scr.reset()
        ffn_bufs["hT"] = scr.alloc([128, NF, TBP], BF16, "hT")
        ffn_bufs["sgt"] = [scr.alloc([128, TBP], F32, f"sgt{i}") for i in range(2)]

    def ffn(B, l, wbase):
        nt_ = B.ntok
        if "hT" not in ffn_bufs:
            ffn_prepare()
        hT, sgt = ffn_bufs.pop("hT"), ffn_bufs.pop("sgt")
        norm_T(B, l, "g_ffn")
        for p_ in range(11):
            wc = wget(wbase + 7 + p_)
            for q in range(2):
                f_ = 2 * p_ + q
                gps, ups = PF[2 * (f_ % 2)], PF[2 * (f_ % 2) + 1]
                for c in range(8):
                    k.mm(gps[:, 0:nt_], wc[:, c, q * 128:(q + 1) * 128], xnT[:, c, 0:nt_], start=(c == 0), stop=(c == 7))
                for c in range(8):
                    k.mm(ups[:, 0:nt_], wc[:, c, 256 + q * 128:256 + (q + 1) * 128], xnT[:, c, 0:nt_],
                         start=(c == 0), stop=(c == 7))
                sg_ = sgt[f_ % 2]
                k.act(sg_[:, 0:nt_], gps[:, 0:nt_], AF.Silu)
                k.tt(hT[:, f_, 0:nt_], sg_[:, 0:nt_], ups[:, 0:nt_], MUL)
        for j in range(2):
            for g in range(6):
                nfc = 4 if g < 5 else 2
                wc = wget(wbase + 18 + j * 6 + g)
                for fc in range(nfc):
                    f_ = g * 4 + fc
                    for i in range(B.nt):
                        k.mm(PF[i], hT[:, f_, i * 128:(i + 1) * 128], wc[:, fc, :], start=(f_ == 0), stop=(f_ == NF - 1))
            for i in range(B.nt):
                hv = h[:, i, j * 512:(j + 1) * 512]
                k.tt(hv, hv, PF[i], ADD)

    def ple(B, l, wbase, tok0):
        src = (I["pp"][l][tok0:tok0 + B.ntok, :] if B.kind == "p" else I["psm"][l]).rearrange("(i p) q -> p i q", p=128)
        k.dma("sp", ptile[:, 0:B.nt, :], src)
        k.copy(pbf[:, 0:B.nt, :], ptile[:, 0:B.nt, :], "dve")
        for i in range(B.nt):
            pt = PT[i % 2]
            for q in range(2):
                k.tr(pt[:, q * 128:(q + 1) * 128], pbf[:, i, q * 128:(q + 1) * 128], idb)
            for q in range(2):
                k.copy(pT[:, q, i * 128:(i + 1) * 128], pt[:, q * 128:(q + 1) * 128], "act" if i % 2 == 0 else "dve")
        norm_T(B, l, "g_ple")
        wp = wple_t
        k.dma("pool", wple_t, I["w_ple"][l].rearrange("(c p) n -> p c n", p=128))
        for j in range(2):
            wc = wget(wbase + 30 + j)
            for i in range(B.nt):
                gps, pps = PF[2 * (i % 2)], PF[2 * (i % 2) + 1]
                for c in range(8):
                    k.mm(gps, xnT[:, c, i * 128:(i + 1) * 128], wc[:, c, :], start=(c == 0), stop=(c == 7))
                for q in range(2):
                    k.mm(pps, pT[:, q, i * 128:(i + 1) * 128], wp[:, q, j * 512:(j + 1) * 512], start=(q == 0), stop=(q == 1))
                tg = tmpt[i % 2]
                k.act(tg, gps, AF.Sigmoid)
                k.tt(tg, tg, pps, MUL)
                hv = h[:, i, j * 512:(j + 1) * 512]
                k.tt(hv, hv, tg, ADD)

    def final_norm(B, tok0):
        ydst = O["y_p"][tok0:tok0 + B.ntok, :] if B.kind == "p" else O["y_s"]
        ydst = ydst.rearrange("(i p) d -> p i d", p=128)
        for i in range(B.nt):
            st = stat[i % 2]
            k.act(junk, h[:, i, :], AF.Square, accum_out=st[:, 0:1])
            rstd_of(st[:, 0:1], st[:, 1:2], 1.0 / D, RMS_EPS)
            k.stt(h[:, i, :], h[:, i, :], st[:, 1:2], gfin, MUL, MUL)
        k.dma("sp", ydst, h[:, 0:B.nt, :], is_output=True)

    stop_after = dbg.get("stop")
    for bi, B in enumerate(blocks):
        if dbg.get("blocks") is not None and bi not in dbg["blocks"]:
            continue
        tok0 = B.idx * TBP
        src = (I["xp"][tok0:tok0 + B.ntok, :] if B.kind == "p" else I["xs"]).rearrange("(i p) d -> p i d", p=128)
        k.dma("sp", h[:, 0:B.nt, :], src)
        stages = dbg.get("stages", "nCBAoFP")
        for l in range(NL):
            if dbg.get("layers") is not None and l not in dbg["layers"]:
                continue
            wbase = (bi * NL + l) * NCH
            k.barrier()
            if "n" in stages:
                norm_T(B, l, "g_mix")
            if "C" in stages:
                mixer_c(B, l, wbase)
            if "B" in stages:
                mixer_b(B, l, wbase)
            if "A" in stages:
                if B.kind == "p" and not dbg.get("nopipe"):
                    mixer_a_prompt(B, l, wbase)
                else:
                    mixer_a(B, l, wbase)
            dump(f"mixed{l}", mixed[:, :, 0:B.ntok], [128, 8, B.ntok])
            if "o" in stages:
                if "F" in stages:
                    ffn_prepare()
                outproj(B, l, wbase)
            dump(f"h_mix{l}", h[:, 0:B.nt, :], [128, B.nt, D])
            if "F" in stages:
                ffn(B, l, wbase)
            dump(f"h_ffn{l}", h[:, 0:B.nt, :], [128, B.nt, D])
            if "P" in stages:
                ple(B, l, wbase, tok0)
            dump(f"h_ple{l}", h[:, 0:B.nt, :], [128, B.nt, D])
        final_norm(B, tok0)
    k.finish()
    nc._kb = (k, I, O, DBG)
    return nc


_CACHE = {}


def kernel(**inputs):
    dbg = dict(DEBUG)
    sh = prep_shared(inputs)
    in_maps = []
    for c in range(NCORES):
        d = dict(sh)
        d.update(prep_core(inputs, c))
        in_maps.append(d)
    nc = build_program(dbg)
    res = run_bass_kernel_spmd(nc, in_maps, core_ids=list(range(NCORES)))
    R = res.results
    if dbg:
        _CACHE["res"] = R
    g = lambda nm: np.stack([np.asarray(R[c][nm], np.float32) for c in range(NCORES)], 0)
    y_prompt = g("y_p")
    y_sample = g("y_s").reshape(DEC_B, DEC_T, D)
    shift_p = np.transpose(g("o_shift_p"), (1, 0, 3, 2)).reshape(NL, NCORES, A_COLS)
    wkv_p = np.transpose(g("o_wkv_p"), (1, 0, 2, 4, 3))
    conv_p = np.transpose(g("o_conv_p"), (1, 0, 4, 3, 2)).reshape(NL, NCORES, 3, 256)
    lru_p = np.transpose(g("o_lru_p"), (1, 0, 3, 2)).reshape(NL, NCORES, 256)
    s5re_p = np.transpose(g("o_s5re_p"), (1, 0, 3, 2)).reshape(NL, NCORES, 16, 64)
    s5im_p = np.transpose(g("o_s5im_p"), (1, 0, 3, 2)).reshape(NL, NCORES, 16, 64)
    t_ = g("o_shift_s")
    shift_s = np.transpose(t_, (1, 0, 4, 3, 2)).reshape(NL, DEC_B, A_COLS)
    wkv_s = np.transpose(g("o_wkv_s"), (1, 0, 2, 3)).reshape(NL, DEC_B, 8, 64, 64)
    t_ = g("o_conv_s")
    conv_s = np.transpose(t_, (1, 0, 4, 5, 3, 2)).reshape(NL, DEC_B, 3, 256)
    t_ = g("o_lru_s")
    lru_s = np.transpose(t_, (1, 0, 4, 3, 2)).reshape(NL, DEC_B, 256)
    t_ = g("o_s5re_s")
    s5re_s = np.transpose(t_, (1, 0, 4, 3, 2)).reshape(NL, DEC_B, 16, 64)
    t_ = g("o_s5im_s")
    s5im_s = np.transpose(t_, (1, 0, 4, 3, 2)).reshape(NL, DEC_B, 16, 64)
    outs = (y_prompt, y_sample, shift_p, wkv_p, conv_p, lru_p, s5re_p, s5im_p,
            shift_s, wkv_s, conv_s, lru_s, s5re_s, s5im_s)
    return tuple(np.ascontiguousarray(o, dtype=np.float32) for o in outs)
```

```python
import math
import numpy as np
from collections import defaultdict
import concourse.bass as bass
import concourse.mybir as mybir
from concourse.bass_utils import run_bass_kernel_spmd

F32 = mybir.dt.float32
BF16 = mybir.dt.bfloat16
ALU = mybir.AluOpType
AF = mybir.ActivationFunctionType
AX = mybir.AxisListType

NCORES = 8
D = 1024
NL = 2
SEQ = 2048
TBP = 512
NPB = SEQ // TBP
DEC_B = 128
DEC_T = 8
SB_PER = DEC_B // NCORES
A_COLS = 1792
IN_COLS = 2560
DFF = 2816
NF = DFF // 128
COL_ORDER = [18, 19, 14, 15, 16, 17, 12, 13, 0, 4, 8, 1, 5, 9, 2, 6, 10, 3, 7, 11]
RMS_EPS = 1e-6
GN_EPS = 64e-5
DEBUG = {}

SAME_ENGINE_SYNC = True
NDSEM = 16
NW = 4

PC = {}
_off = 0
for _n, _c in [("mu", 14), ("w0", 4), ("a0", 4), ("k_k", 4), ("k_a", 4), ("r_k", 4), ("lnx_w", 4),
               ("lnx_b", 4), ("conv_w", 8), ("conv_b", 2), ("b_rg", 2), ("b_ig", 2), ("lam", 2),
               ("g_out_b", 2), ("s5_d", 2), ("b_glu", 2), ("g_out_c", 2), ("s5_lr", 8), ("s5_li", 8),
               ("s5_ldt", 8), ("g_mix", 8), ("g_ffn", 8), ("g_ple", 8)]:
    PC[_n] = _off
    _off += _c
NPAR = _off

CC = {}
_off = 0
for _n, _c in [("ident", 128), ("onesblk", 128), ("avgblk", 128), ("ones", 128), ("maskA", 192),
               ("maskSL", 128), ("cumcoef", 256), ("startmask", 128)]:
    CC[_n] = _off
    _off += _c
NCONST = _off


def _cols(v, n):
    return np.ascontiguousarray(np.asarray(v, np.float32).reshape(n, 128).T)


def make_consts():
    c = np.zeros((128, NCONST), np.float32)
    p = np.arange(128)[:, None]
    q = np.arange(128)[None, :]
    same = (p // 64) == (q // 64)
    c[:, CC["ident"]:CC["ident"] + 128] = np.eye(128)
    c[:, CC["onesblk"]:CC["onesblk"] + 128] = same
    c[:, CC["avgblk"]:CC["avgblk"] + 128] = same / 64.0
    c[:, CC["ones"]:CC["ones"] + 128] = 1.0
    c[:, CC["maskA"]:CC["maskA"] + 128] = same & ((p % 64) < (q % 64))
    q64 = np.arange(64)[None, :]
    c[:, CC["maskA"] + 128:CC["maskA"] + 192] = (p % 64) <= q64
    c[:, CC["maskSL"]:CC["maskSL"] + 128] = same & ((p % 64) > (q % 64))
    cc = np.ones((128, 256), np.float32)
    cc[:, 0::64] = 0.0
    c[:, CC["cumcoef"]:CC["cumcoef"] + 256] = cc
    sm = np.ones((128, 128), np.float32)
    sm[:, 0::8] = 0.0
    c[:, CC["startmask"]:CC["startmask"] + 128] = sm
    return c


def pack_params(inp, l):
    P = np.zeros((128, NPAR), np.float32)

    def put(name, arr, n):
        P[:, PC[name]:PC[name] + n] = _cols(arr, n)
    put("mu", inp["mu_a"][l], 14)
    for nm in ("w0", "a0", "k_k", "k_a", "lnx_w", "lnx_b"):
        put(nm, inp[nm][l], 4)
    put("r_k", inp["r_k"][l].reshape(512), 4)
    cw = np.asarray(inp["conv_w"][l], np.float32)
    for ct in range(2):
        for j in range(4):
            P[:, PC["conv_w"] + ct * 4 + j] = cw[j, ct * 128:(ct + 1) * 128]
    for nm, src in (("conv_b", "conv_b"), ("b_rg", "b_rg"), ("b_ig", "b_ig"), ("lam", "lru_lambda"),
                    ("g_out_b", "g_out_b"), ("s5_d", "s5_d"), ("b_glu", "b_glu"), ("g_out_c", "g_out_c")):
        put(nm, inp[src][l], 2)
    put("s5_lr", inp["s5_lam_re"][l].reshape(1024), 8)
    put("s5_li", inp["s5_lam_im"][l].reshape(1024), 8)
    put("s5_ldt", np.repeat(np.asarray(inp["s5_log_dt"][l], np.float32), 64), 8)
    put("g_mix", inp["g_mix"][l], 8)
    put("g_ffn", inp["g_ffn"][l], 8)
    put("g_ple", inp["g_ple"][l], 8)
    return P


def prep_shared(inp):
    f = lambda a: np.ascontiguousarray(np.asarray(a, np.float32))
    sh = {}
    sh["consts"] = make_consts()
    sh["params"] = np.stack([pack_params(inp, l) for l in range(NL)], 0)
    order = np.concatenate([np.arange(t * 128, (t + 1) * 128) for t in COL_ORDER])
    sh["w_in"] = f(np.asarray(inp["w_in"])[:, :, order])
    sh["w_out"] = f(inp["w_out"])
    wu = np.asarray(inp["w_ffn_up"], np.float32)
    uo = []
    for p in range(NF // 2):
        uo.append(np.arange(256 * p, 256 * p + 256))
        uo.append(np.arange(DFF + 256 * p, DFF + 256 * p + 256))
    sh["w_up"] = f(wu[:, :, np.concatenate(uo)])
    sh["w_down"] = f(inp["w_ffn_down"])
    sh["w_ple"] = f(inp["w_ple"])
    sh["w_gate"] = f(inp["w_ple_gate"])
    sh["g_final"] = f(inp["g_final"]).reshape(1, D)
    sh["lora1"] = f(np.concatenate([np.asarray(inp["w_dec2"]), np.asarray(inp["w_a2"])], axis=1))
    sh["w_g2"] = f(inp["w_g2"])
    def bd(w):
        w = np.asarray(w, np.float32)
        o = np.zeros((NL, 2, 128, 128), np.float32)
        for ct in range(2):
            for h in range(2):
                o[:, ct, h * 64:(h + 1) * 64, h * 64:(h + 1) * 64] = w[:, ct * 2 + h]
        return o
    sh["w_rg"] = bd(inp["w_rg"])
    sh["w_ig"] = bd(inp["w_ig"])
    sh["w_glu"] = f(inp["w_glu"])
    def bpad(b):
        b = np.asarray(b, np.float32)
        o = np.zeros((NL, 8, 128, 128), np.float32)
        for s in range(8):
            for g2 in range(2):
                g = 2 * s + g2
                k0 = (g % 8) * 16
                o[:, s, k0:k0 + 16, g2 * 64:(g2 + 1) * 64] = np.transpose(b[:, g], (0, 2, 1))
        return o
    def cpad(c):
        c = np.asarray(c, np.float32)
        o = np.zeros((NL, 8, 128, 128), np.float32)
        for s in range(8):
            for g2 in range(2):
                g = 2 * s + g2
                m0 = (g % 8) * 16
                o[:, s, g2 * 64:(g2 + 1) * 64, m0:m0 + 16] = np.transpose(c[:, g], (0, 2, 1))
        return o
    s5w = np.stack([bpad(inp["s5_b_re"]), bpad(inp["s5_b_im"]), cpad(inp["s5_c_re"]), cpad(inp["s5_c_im"])], 1)
    sh["s5w"] = f(np.transpose(s5w, (0, 1, 3, 2, 4)))
    return sh


def prep_core(inp, c):
    f = lambda a: np.ascontiguousarray(np.asarray(a, np.float32))
    b0, b1 = c * SB_PER, (c + 1) * SB_PER
    d = {}
    d["xp"] = f(inp["x_prompt"][c])
    d["xs"] = f(np.asarray(inp["x_sample"])[b0:b1].reshape(SB_PER * DEC_T, D))
    d["pp"] = f(np.asarray(inp["p_prompt"])[:, c])
    d["psm"] = f(np.asarray(inp["p_sample"])[:, b0:b1].reshape(NL, SB_PER * DEC_T, 256))
    ss = np.asarray(inp["state_shift"], np.float32)[:, b0:b1]
    d["st_shift"] = f(np.transpose(ss.reshape(NL, SB_PER, 14, 128), (0, 3, 2, 1)))
    d["st_wkv"] = f(np.asarray(inp["state_wkv"])[:, b0:b1].reshape(NL, SB_PER * 8, 4096))
    sc = np.asarray(inp["state_conv"], np.float32)[:, b0:b1]
    d["st_conv"] = f(np.transpose(sc.reshape(NL, SB_PER, 3, 2, 128), (0, 4, 3, 1, 2)))
    sl = np.asarray(inp["state_lru"], np.float32)[:, b0:b1]
    d["st_lru"] = f(np.transpose(sl.reshape(NL, SB_PER, 2, 128), (0, 3, 2, 1)))
    for nm, src in (("st_s5re", "state_s5_re"), ("st_s5im", "state_s5_im")):
        s5 = np.asarray(inp[src], np.float32)[:, b0:b1].reshape(NL, SB_PER, 8, 128)
        d[nm] = f(np.transpose(s5, (0, 3, 2, 1)))
    return d


class V:
    __slots__ = ("t", "ap")

    def __init__(self, t, ap):
        self.t = t
        self.ap = ap

    def __getitem__(self, k):
        return V(self.t, self.ap[k])

    def re(self, s, **kw):
        return V(self.t, self.ap.rearrange(s, **kw))

    def bc(self, shape):
        return V(self.t, self.ap.to_broadcast(shape))

    def un(self, axis):
        return V(self.t, self.ap.unsqueeze(axis))


class T:
    def __init__(self, ap, name="", scr=False):
        self.ap = ap
        self.name = name
        self.w = None
        self.r = []
        self.scr = scr
        self.psum = False

    def __getitem__(self, k):
        return V(self, self.ap[k])

    def v(self):
        return V(self, self.ap)

    def re(self, s, **kw):
        return V(self, self.ap.rearrange(s, **kw))

    def un(self, axis):
        return V(self, self.ap.unsqueeze(axis))

    def bc(self, shape):
        return V(self, self.ap.to_broadcast(shape))


class KB:
    def __init__(self, nc):
        self.nc = nc
        self.eng = {"pe": nc.tensor, "act": nc.scalar, "dve": nc.vector, "pool": nc.gpsimd, "sp": nc.sync}
        self.sems = {}
        self.semval = defaultdict(int)
        for e in self.eng:
            self.sems[e] = nc.alloc_semaphore(name=e + "_c")
        for q in ("sp", "act", "pool"):
            for i in range(NDSEM):
                self.sems[(q, i)] = nc.alloc_semaphore(name=f"{q}_d{i}")
        self.dma_i = defaultdict(int)
        self.waited = {e: defaultdict(int) for e in self.eng}
        self.ninst = defaultdict(int)
        self.out_dma = []
        self.scr_dma = []
        self._n = 0
        self.rr_i = 0
        self.rec = None

    def sb(self, shape, dt=F32, name=None):
        self._n += 1
        name = "s_" + (name or f"sb{self._n}")
        h = self.nc.alloc_sbuf_tensor(name, list(shape), dt)
        return T(h.ap(), name)

    def ps(self, shape, dt=F32, name=None):
        self._n += 1
        name = "p_" + (name or f"ps{self._n}")
        h = self.nc.alloc_psum_tensor(name, list(shape), dt)
        t = T(h.ap(), name)
        t.psum = True
        return t

    def _wait(self, e, deps):
        eng = self.eng[e]
        best = {}
        for (sk, val, de) in deps:
            if de == e and (e == "pe" or not SAME_ENGINE_SYNC):
                continue
            if val > best.get(sk, 0):
                best[sk] = val
        for sk, val in best.items():
            if self.waited[e][sk] < val:
                eng.wait_ge(self.sems[sk], val)
                self.waited[e][sk] = val
                self.ninst[e] += 1

    @staticmethod
    def _deps(reads, writes, e=None):
        deps = []
        for t in reads:
            if t.w is not None:
                deps.append(t.w)
            if t.psum:
                deps.extend(x for x in t.r if x[2] != e)
        for t in writes:
            if t.w is not None:
                deps.append(t.w)
            deps.extend(t.r)
        return deps

    @staticmethod
    def _commit(reads, writes, tok):
        for t in reads:
            if len(t.r) > 6:
                best = {}
                for x in t.r:
                    if x[1] > best.get(x[0], (None, 0, None))[1]:
                        best[x[0]] = x
                t.r = list(best.values())
            t.r.append(tok)
        for t in writes:
            t.w = tok
            t.r = []

    def op(self, e, meth, *args, reads=(), writes=(), **kw):
        if self.rec is not None:
            self.rec.append(lambda: self._op(e, meth, *args, reads=reads, writes=writes, **kw))
            return None
        return self._op(e, meth, *args, reads=reads, writes=writes, **kw)

    def _op(self, e, meth, *args, reads=(), writes=(), **kw):
        rd = list(reads)
        wr = list(writes)
        kw2 = {}
        for kk_, v in kw.items():
            if isinstance(v, V):
                (wr if kk_ in ("out", "accum_out") else rd).append(v.t)
                kw2[kk_] = v.ap
            elif isinstance(v, T):
                (wr if kk_ in ("out", "accum_out") else rd).append(v)
                kw2[kk_] = v.ap
            else:
                kw2[kk_] = v
        self._wait(e, self._deps(rd, wr, e))
        inst = getattr(self.eng[e], meth)(*args, **kw2)
        self.semval[e] += 1
        inst.then_inc(self.sems[e], 1)
        self.ninst[e] += 1
        self._commit(rd, wr, (e, self.semval[e], e))
        return inst

    def mm(self, out, lhsT, rhs, start=True, stop=True):
        return self.op("pe", "matmul", out=out, lhsT=lhsT, rhs=rhs, start=start, stop=stop)

    def tr(self, out, in_, ident):
        return self.op("pe", "transpose", out=out, in_=in_, identity=ident)

    def dma(self, q, out, in_, is_output=False, **kw):
        if self.rec is not None:
            self.rec.append(lambda: self._dma(q, out, in_, is_output=is_output, **kw))
            return None
        return self._dma(q, out, in_, is_output=is_output, **kw)

    def start_rec(self):
        assert self.rec is None
        self.rec = []

    def stop_rec(self):
        r = self.rec
        self.rec = None
        return r

    @staticmethod
    def interleave(*lists):
        lists = [x for x in lists if x]
        pos = [0] * len(lists)
        while True:
            best, bf = -1, 2.0
            for i, x in enumerate(lists):
                if pos[i] < len(x):
                    f = pos[i] / len(x)
                    if f < bf:
                        best, bf = i, f
            if best < 0:
                break
            lists[best][pos[best]]()
            pos[best] += 1

    def _dma(self, q, out, in_, is_output=False, **kw):
        rd, wr = [], []
        if isinstance(in_, (V, T)):
            rd.append(in_.t if isinstance(in_, V) else in_)
            in_ap = in_.ap
        else:
            in_ap = in_
        if isinstance(out, (V, T)):
            wr.append(out.t if isinstance(out, V) else out)
            out_ap = out.ap
        else:
            out_ap = out
        i = self.dma_i[q] % NDSEM
        self.dma_i[q] += 1
        sk = (q, i)
        deps = self._deps(rd, wr)
        if self.semval[sk] > 0:
            deps.append((sk, self.semval[sk], "dma"))
        self._wait(q, deps)
        inst = self.eng[q].dma_start(out=out_ap, in_=in_ap, allow_slow_non_contiguous=True, **kw)
        self.semval[sk] += 16
        inst.then_inc(self.sems[sk], 16)
        self.ninst[q] += 1
        tok = (sk, self.semval[sk], "dma")
        self._commit(rd, wr, tok)
        if is_output:
            self.out_dma.append(tok)
        if any(t.scr for t in rd + wr):
            self.scr_dma.append(tok)
        return inst

    def copy(self, out, in_, eng=None):
        if eng is None:
            self.rr_i += 1
            eng = "dve" if self.rr_i % 2 else "act"
        if eng == "act":
            return self.op("act", "copy", out=out, in_=in_)
        return self.op(eng, "tensor_copy", out=out, in_=in_)

    def tt(self, out, in0, in1, op, eng="dve"):
        return self.op(eng, "tensor_tensor", out=out, in0=in0, in1=in1, op=op)

    def ts(self, out, in0, s1, op0, s2=None, op1=None, eng="dve"):
        if op1 is None:
            return self.op(eng, "tensor_scalar", out=out, in0=in0, scalar1=s1, scalar2=None, op0=op0)
        return self.op(eng, "tensor_scalar", out=out, in0=in0, scalar1=s1, scalar2=s2, op0=op0, op1=op1)

    def stt(self, out, in0, scalar, in1, op0, op1, eng="dve"):
        return self.op(eng, "scalar_tensor_tensor", out=out, in0=in0, scalar=scalar, in1=in1, op0=op0, op1=op1)

    def act(self, out, in_, func, bias=None, scale=None, accum_out=None):
        kw = {}
        if bias is not None:
            kw["bias"] = bias
        if scale is not None:
            kw["scale"] = scale
        if accum_out is not None:
            kw["accum_out"] = accum_out
        return self.op("act", "activation", out=out, in_=in_, func=func, **kw)

    def memset(self, view, val, eng="dve"):
        if isinstance(view, T):
            view = view.v()
        return self.op(eng, "memset", view.ap, val, writes=[view.t])

    def barrier(self):
        assert self.rec is None
        toks = []
        for e in ("pe", "act", "dve", "pool"):
            if self.semval[e] > 0:
                toks.append((e, self.semval[e], "x"))
        toks.extend((sk, v, "dma") for (sk, v, _) in self.scr_dma)
        self.scr_dma = []
        for e in ("pe", "act", "dve", "pool", "sp"):
            self._wait(e, [(sk, v, "x") for (sk, v, _) in toks])

    def finish(self):
        deps = list(self.out_dma)
        for q in ("sp", "act", "pool"):
            for i in range(NDSEM):
                sk = (q, i)
                if self.semval[sk] > 0:
                    deps.append((sk, self.semval[sk], "dma"))
        for e in ("pe", "act", "dve", "pool"):
            if self.semval[e] > 0:
                deps.append((e, self.semval[e], "x"))
        self._wait("sp", deps)


class Scratch:
    def __init__(self, k, nbytes):
        self.k = k
        self.words = nbytes // 4
        self.base = k.nc.alloc_sbuf_tensor("scratch", [128, self.words], F32).ap()
        self.off = 0

    def reset(self):
        self.off = 0

    def alloc(self, shape, dt=F32, name=""):
        n = 1
        for s in shape[1:]:
            n *= s
        if dt == F32:
            w = n
        else:
            w = (n + 1) // 2
        w = (w + 7) // 8 * 8
        assert self.off + w <= self.words, f"scratch overflow {name} {self.off + w} > {self.words}"
        ap = self.base[:, self.off:self.off + w]
        self.off += w
        if dt != F32:
            ap = ap.bitcast(dt)
        ap = ap[:, 0:n]
        if len(shape) == 3:
            ap = ap.rearrange("p (a b) -> p a b", a=shape[1])
        elif len(shape) == 4:
            ap = ap.rearrange("p (a b c) -> p a b c", a=shape[1], b=shape[2])
        return T(ap, name, scr=True)


class Blk:
    def __init__(self, kind, idx):
        self.kind = kind
        self.idx = idx
        if kind == "p":
            self.nseq, self.T, self.ntok, self.nt = 1, TBP, TBP, TBP // 128
        else:
            self.nseq, self.T, self.ntok, self.nt = SB_PER, DEC_T, SB_PER * DEC_T, 1
        self.first = (kind == "s") or idx == 0
        self.last = (kind == "s") or idx == NPB - 1


def build_program(dbg=None):
    dbg = dbg or {}
    nc = bass.Bass("TRN2", target_bir_lowering=False)

    def din(name, shape):
        return nc.dram_tensor(name, list(shape), F32, kind="ExternalInput").ap()

    def dout(name, shape):
        return nc.dram_tensor(name, list(shape), F32, kind="ExternalOutput").ap()

    I = {}
    for nm, shp in [("xp", [SEQ, D]), ("xs", [128, D]), ("pp", [NL, SEQ, 256]), ("psm", [NL, 128, 256]),
                    ("st_shift", [NL, 128, 14, 16]), ("st_wkv", [NL, 128, 4096]), ("st_conv", [NL, 128, 2, 16, 3]),
                    ("st_lru", [NL, 128, 2, 16]), ("st_s5re", [NL, 128, 8, 16]), ("st_s5im", [NL, 128, 8, 16]),
                    ("consts", [128, NCONST]), ("params", [NL, 128, NPAR]), ("w_in", [NL, D, IN_COLS]),
                    ("w_out", [NL, D, D]), ("w_up", [NL, D, 2 * DFF]), ("w_down", [NL, DFF, D]),
                    ("w_ple", [NL, 256, D]), ("w_gate", [NL, D, D]), ("g_final", [1, D]),
                    ("lora1", [NL, 128, 512]), ("w_g2", [NL, 128, 512]), ("w_rg", [NL, 2, 128, 128]),
                    ("w_ig", [NL, 2, 128, 128]), ("w_glu", [NL, 256, 256]), ("s5w", [NL, 4, 128, 8, 128])]:
        I[nm] = din(nm, shp)
    O = {}
    for nm, shp in [("y_p", [SEQ, D]), ("y_s", [128, D]),
                    ("o_shift_p", [NL, 128, 14]), ("o_wkv_p", [NL, 8, 64, 64]), ("o_conv_p", [NL, 128, 2, 3]),
                    ("o_lru_p", [NL, 128, 2]), ("o_s5re_p", [NL, 128, 8]), ("o_s5im_p", [NL, 128, 8]),
                    ("o_shift_s", [NL, 128, 14, 16]), ("o_wkv_s", [NL, 128, 4096]), ("o_conv_s", [NL, 128, 2, 16, 3]),
                    ("o_lru_s", [NL, 128, 2, 16]), ("o_s5re_s", [NL, 128, 8, 16]), ("o_s5im_s", [NL, 128, 8, 16])]:
        O[nm] = dout(nm, shp)
    DBG = {}

    k = KB(nc)
    MUL, ADD, SUB = ALU.mult, ALU.add, ALU.subtract

    def dump(name, view, shape):
        if name in dbg:
            DBG[name] = dout("dbg_" + name, shape)
            isbf = (view.ap if isinstance(view, (V, T)) else view).dtype == BF16
            k.dma("pool" if isbf else "sp", DBG[name], view, is_output=True)

    consts = k.sb([128, NCONST], F32, "consts")
    C = lambda nm, n: consts[:, CC[nm]:CC[nm] + n]
    ident = C("ident", 128)
    idb = k.sb([128, 128], BF16, "idb")
    par = [k.sb([128, NPAR], F32, f"par{l}") for l in range(NL)]
    P = lambda l, nm, i=0, n=1: par[l][:, PC[nm] + i:PC[nm] + i + n]
    lora1 = [k.sb([128, 512], F32, f"lora1_{l}") for l in range(NL)]
    wg2 = [k.sb([128, 512], F32, f"wg2_{l}") for l in range(NL)]
    wrg = [k.sb([128, 2, 128], F32, f"wrg{l}") for l in range(NL)]
    wig = [k.sb([128, 2, 128], F32, f"wig{l}") for l in range(NL)]
    wglu = [k.sb([128, 2, 256], F32, f"wglu{l}") for l in range(NL)]
    gfin = k.sb([128, D], F32, "gfin")
    s5d = [k.sb([128, 16, 8], F32, f"s5d{l}") for l in range(NL)]
    S5N = {n: i for i, n in enumerate(["dt", "mag", "ang", "ar", "ai", "nai", "cr", "ci", "c1", "s1", "t0", "t1", "Qr", "Qi", "nQi"])}
    S5 = lambda l, nm: s5d[l][:, S5N[nm], :]
    Tc = [k.sb([128, 8, 128], F32, f"Tc{l}") for l in range(NL)]
    Ts = [k.sb([128, 8, 128], F32, f"Ts{l}") for l in range(NL)]
    c8 = [k.sb([128, 2], F32, f"c8_{l}") for l in range(NL)]
    h = k.sb([128, 4, D], F32, "h")
    xn = [k.sb([128, D], BF16, f"xn{i}") for i in range(2)]
    junk = k.sb([128, D], BF16, "junk")
    stat = [k.sb([128, 4], F32, f"stat{i}") for i in range(2)]
    xnT = k.sb([128, 8, TBP], BF16, "xnT")
    wring = [k.sb([128, 4096], BF16, f"wring{i}") for i in range(NW)]
    mixed = k.sb([128, 8, TBP], BF16, "mixed")
    ptile = k.sb([128, 4, 256], F32, "ptile")
    pbf = k.sb([128, 4, 256], BF16, "pbf")
    pT = k.sb([128, 2, TBP], BF16, "pT")
    tmpt = [k.sb([128, 512], F32, f"tmpt{i}") for i in range(2)]
    wple_t = k.sb([128, 2, D], BF16, "wple_t")
    histA = [k.sb([128, 14], F32, f"histA{l}") for l in range(NL)]
    histB = [k.sb([128, 2, 3], F32, f"histB{l}") for l in range(NL)]
    lruc = [k.sb([128, 2], F32, f"lruc{l}") for l in range(NL)]
    s5cr = [k.sb([128, 8], F32, f"s5cr{l}") for l in range(NL)]
    s5ci = [k.sb([128, 8], F32, f"s5ci{l}") for l in range(NL)]
    Pbd = [[k.sb([128, 128], F32, f"Pbd{l}_{hp}") for hp in range(4)] for l in range(NL)]
    small = k.sb([128, 64], F32, "small")
    PF = [k.ps([128, 512], F32, f"PF{i}") for i in range(4)]
    PHb = [k.ps([128, 512], F32, f"PHb{i}") for i in range(2)]
    PT = [k.ps([128, 1024], BF16, f"PTb{i}") for i in range(2)]
    scr = Scratch(k, min(nc.sbuf_bytes_remaining - 1024, 77 * 1024))
    ps_i = [0]

    pf_pool = [[PF[0], PF[1], PF[2]]]

    def pf():
        ps_i[0] += 1
        return pf_pool[0][ps_i[0] % len(pf_pool[0])]
    ph_i = [0]
    PHpool = [PHb[0], PHb[1], PF[0], PF[1], PF[2]]

    ph_pool = [PHpool]

    def ph():
        ph_i[0] += 1
        return ph_pool[0][ph_i[0] % len(ph_pool[0])]

    blocks = [Blk("p", i) for i in range(NPB)] + [Blk("s", 0)]
    sched = []
    NCH = 32
    for bi, B in enumerate(blocks):
        for l in range(NL):
            for c in range(5):
                sched.append((I["w_in"][l][:, c * 512:(c + 1) * 512].rearrange("(c p) n -> p c n", p=128), (8, 512)))
            for j in range(2):
                sched.append((I["w_out"][l][:, j * 512:(j + 1) * 512].rearrange("(c p) n -> p c n", p=128), (8, 512)))
            for p_ in range(11):
                sched.append((I["w_up"][l][:, p_ * 512:(p_ + 1) * 512].rearrange("(c p) n -> p c n", p=128), (8, 512)))
            for j in range(2):
                for g in range(6):
                    nfc = 4 if g < 5 else 2
                    sched.append((I["w_down"][l][g * 512:g * 512 + nfc * 128, j * 512:(j + 1) * 512]
                                  .rearrange("(c p) n -> p c n", p=128), (nfc, 512)))
            for j in range(2):
                sched.append((I["w_gate"][l][:, j * 512:(j + 1) * 512].rearrange("(c p) n -> p c n", p=128), (8, 512)))
    issued = [0]

    def wget(idx):
        while issued[0] < min(idx + NW, len(sched)):
            i = issued[0]
            src, (a, b) = sched[i]
            dst = wring[i % NW][:, 0:a * b].re("p (a b) -> p a b", a=a)
            k.dma("pool", dst, src)
            issued[0] += 1
        a, b = sched[idx][1]
        return wring[idx % NW][:, 0:a * b].re("p (a b) -> p a b", a=a)

    k.dma("sp", consts, I["consts"])
    for l in range(NL):
        k.dma("sp", par[l], I["params"][l])
    k.dma("sp", gfin, I["g_final"].to_broadcast([128, D]))
    for l in range(NL):
        k.dma("sp", lora1[l], I["lora1"][l])
        k.dma("sp", wg2[l], I["w_g2"][l])
        k.dma("sp", wrg[l], I["w_rg"][l].rearrange("c p n -> p c n"))
        k.dma("sp", wig[l], I["w_ig"][l].rearrange("c p n -> p c n"))
        k.dma("sp", wglu[l], I["w_glu"][l].rearrange("(c p) n -> p c n", p=128))
    k.copy(idb, ident, "dve")
    wget(0)
    for l in range(NL):
        k.memset(histA[l], 0.0)
        k.memset(histB[l], 0.0)
        k.memset(lruc[l], 0.0)
        k.memset(s5cr[l], 0.0)
        k.memset(s5ci[l], 0.0)
        for hp in range(4):
            k.memset(Pbd[l][hp], 0.0)

    def range_reduce(v):
        tq = S5(0, "t1") if False else small[:, 0:8]
        for m in range(6):
            k.ts(tq, v, math.pi, ALU.is_ge, -2 * math.pi, MUL)
            k.tt(v, v, tq, ADD)

    for l in range(NL):
        lr = P(l, "s5_lr", 0, 8)
        li = P(l, "s5_li", 0, 8)
        k.act(S5(l, "dt"), P(l, "s5_ldt", 0, 8), AF.Exp)
        k.tt(S5(l, "t0"), lr, S5(l, "dt"), MUL)
        k.act(S5(l, "mag"), S5(l, "t0"), AF.Exp)
        k.tt(S5(l, "ang"), li, S5(l, "dt"), MUL)
        k.copy(S5(l, "t0"), S5(l, "ang"), "dve")
        range_reduce(S5(l, "t0"))
        k.act(S5(l, "s1"), S5(l, "t0"), AF.Sin)
        k.ts(S5(l, "t0"), S5(l, "ang"), math.pi / 2, ADD)
        range_reduce(S5(l, "t0"))
        k.act(S5(l, "c1"), S5(l, "t0"), AF.Sin)
        k.tt(S5(l, "ar"), S5(l, "mag"), S5(l, "c1"), MUL)
        k.tt(S5(l, "ai"), S5(l, "mag"), S5(l, "s1"), MUL)
        k.ts(S5(l, "nai"), S5(l, "ai"), -1.0, MUL)
        k.tt(S5(l, "t0"), lr, lr, MUL)
        k.tt(S5(l, "t1"), li, li, MUL)
        k.tt(S5(l, "t0"), S5(l, "t0"), S5(l, "t1"), ADD)
        k.op("dve", "reciprocal", out=S5(l, "t0"), in_=S5(l, "t0"))
        k.ts(S5(l, "t1"), S5(l, "ar"), -1.0, ADD)
        k.tt(S5(l, "cr"), S5(l, "t1"), lr, MUL)
        k.tt(S5(l, "ci"), S5(l, "ai"), li, MUL)
        k.tt(S5(l, "cr"), S5(l, "cr"), S5(l, "ci"), ADD)
        k.tt(S5(l, "cr"), S5(l, "cr"), S5(l, "t0"), MUL)
        k.tt(S5(l, "ci"), S5(l, "ai"), lr, MUL)
        k.tt(S5(l, "t1"), S5(l, "t1"), li, MUL)
        k.tt(S5(l, "ci"), S5(l, "ci"), S5(l, "t1"), SUB)
        k.tt(S5(l, "ci"), S5(l, "ci"), S5(l, "t0"), MUL)
        k.memset(Tc[l][:, :, 0:1], 1.0)
        k.memset(Ts[l][:, :, 0:1], 0.0)
        cn, sn = S5(l, "c1"), S5(l, "s1")
        ta = tmpt[0][:, 0:8 * 64].re("p (a b) -> p a b", a=8)
        tb = tmpt[1][:, 0:8 * 64].re("p (a b) -> p a b", a=8)
        n = 1
        while n < 128:
            cb = cn.un(2).bc([128, 8, n])
            sb_ = sn.un(2).bc([128, 8, n])
            k.tt(ta[:, :, 0:n], Ts[l][:, :, 0:n], sb_, MUL)
            k.tt(tb[:, :, 0:n], Tc[l][:, :, 0:n], sb_, MUL)
            k.tt(Tc[l][:, :, n:2 * n], Tc[l][:, :, 0:n], cb, MUL)
            k.tt(Tc[l][:, :, n:2 * n], Tc[l][:, :, n:2 * n], ta[:, :, 0:n], SUB)
            k.tt(Ts[l][:, :, n:2 * n], Ts[l][:, :, 0:n], cb, MUL)
            k.tt(Ts[l][:, :, n:2 * n], Ts[l][:, :, n:2 * n], tb[:, :, 0:n], ADD)
            k.tt(S5(l, "t0"), cn, cn, MUL)
            k.tt(S5(l, "t1"), sn, sn, MUL)
            k.tt(sn, cn, sn, MUL)
            k.ts(sn, sn, 2.0, MUL)
            k.tt(cn, S5(l, "t0"), S5(l, "t1"), SUB)
            n *= 2
        c127, s127 = Tc[l][:, :, 127], Ts[l][:, :, 127]
        k.tt(S5(l, "Qr"), S5(l, "ar"), c127, MUL)
        k.tt(S5(l, "t0"), S5(l, "ai"), s127, MUL)
        k.tt(S5(l, "Qr"), S5(l, "Qr"), S5(l, "t0"), SUB)
        k.tt(S5(l, "Qi"), S5(l, "ar"), s127, MUL)
        k.tt(S5(l, "t0"), S5(l, "ai"), c127, MUL)
        k.tt(S5(l, "Qi"), S5(l, "Qi"), S5(l, "t0"), ADD)
        k.ts(S5(l, "nQi"), S5(l, "Qi"), -1.0, MUL)
        k.act(c8[l], P(l, "lam", 0, 2), AF.Exp, scale=-1.0)
        k.act(c8[l], c8[l], AF.Ln, bias=1.0)
        k.ts(c8[l], c8[l], -8.0, MUL)

    def rstd_of(ss_view, out_view, scale, eps):
        k.ts(out_view, ss_view, scale, MUL, eps, ADD)
        k.act(out_view, out_view, AF.Ln)
        k.act(out_view, out_view, AF.Exp, scale=-0.5)

    def norm_T(B, l, gname):
        for i in range(B.nt):
            st = stat[i % 2]
            xb = xn[i % 2]
            k.act(junk, h[:, i, :], AF.Square, accum_out=st[:, 0:1])
            rstd_of(st[:, 0:1], st[:, 1:2], 1.0 / D, RMS_EPS)
            k.act(xb, h[:, i, :], AF.Copy, scale=st[:, 1:2])
            for half in range(2):
                pt = PT[half]
                for c4 in range(4):
                    c = half * 4 + c4
                    k.tr(pt[:, c4 * 128:(c4 + 1) * 128], xb[:, c * 128:(c + 1) * 128], idb)
                for c4 in range(4):
                    c = half * 4 + c4
                    dst = xnT[:, c, i * 128:(i + 1) * 128]
                    if half == 0:
                        k.act(dst, pt[:, c4 * 128:(c4 + 1) * 128], AF.Copy, scale=P(l, gname, c))
                    else:
                        k.ts(dst, pt[:, c4 * 128:(c4 + 1) * 128], P(l, gname, c), MUL)

    def inproj(B, l, wbase, tile, evac):
        pos = COL_ORDER.index(tile)
        wc = wget(wbase + pos // 4)
        w0_ = (pos % 4) * 128
        ps = pf()
        for c in range(8):
            k.mm(ps[:, 0:B.ntok], wc[:, c, w0_:w0_ + 128], xnT[:, c, 0:B.ntok], start=(c == 0), stop=(c == 7))
        evac(ps[:, 0:B.ntok])

    def seq3(v, B):
        return v.re("p (s t) -> p s t", t=B.T)

    def rms_pair(B, l, ytiles, gname, mix0):
        nt_ = B.ntok
        ps = pf()
        for ct in range(2):
            sq = tmpt[ct]
            k.act(sq[:, 0:nt_], ytiles[ct], AF.Square)
            k.mm(ps[:, 0:nt_], C("ones", 128), sq[:, 0:nt_], start=(ct == 0), stop=(ct == 1))
        rs = tmpt[0]
        rstd_of(ps[:, 0:nt_], rs[:, 0:nt_], 1.0 / 256, RMS_EPS)
        for ct in range(2):
            k.stt(mixed[:, mix0 + ct, 0:nt_], ytiles[ct], P(l, gname, ct), rs[:, 0:nt_], MUL, MUL)

    small2 = k.sb([128, 64], F32, "small2")

    def mixer_c(B, l, wbase):
        nt_ = B.ntok
        scr.reset()
        s5w = [scr.alloc([128, 8, 128], F32, f"s5w{i}") for i in range(4)]
        u = scr.alloc([128, 2, TBP], F32, "u")
        bufs = []
        for si in range(2):
            bufs.append([scr.alloc([128, TBP], F32, f"{nm}{si}") for nm in ("t1", "t2", "bur", "bui", "gr", "gi")])
        yc = scr.alloc([128, 2, TBP], F32, "yc")
        z = scr.alloc([128, 2, TBP], F32, "z")
        coefs = [scr.alloc([128, 128], F32, f"coef{si}") for si in range(2)]
        st_r = scr.alloc([128, 8, 16], F32, "st_r")
        st_i = scr.alloc([128, 8, 16], F32, "st_i")
        so_r = scr.alloc([128, 8, 16], F32, "so_r")
        so_i = scr.alloc([128, 8, 16], F32, "so_i")
        for kind in range(4):
            k.dma("sp", s5w[kind], I["s5w"][l, kind])
        k.ts(s5w[3], s5w[3], -1.0, MUL)
        if B.kind == "s":
            k.dma("sp", st_r, I["st_s5re"][l])
            k.dma("sp", st_i, I["st_s5im"][l])
        for ct in range(2):
            inproj(B, l, wbase, 18 + ct, lambda ps, ct=ct: k.copy(u[:, ct, 0:nt_], ps))
        dump(f"u{l}", u[:, :, 0:nt_], [128, 2, nt_])
        if B.kind == "p":
            nseg, sl = 4, 128
        else:
            nseg, sl = 16, 8
        v3 = lambda T_: T_[:, 0:nt_].re("p (s t) -> p s t", t=sl)
        Yps = [PF[0], PF[1]]
        banks = [(PF[2], PF[3]), (PHb[0], PHb[1])]
        smalls = [small, small2]

        def tile_body(s, si):
            t1, t2, bur, bui, gr, gi = bufs[si]
            bre, bim = banks[si]
            sm = smalls[si]
            coef = coefs[si]
            ct = s // 4
            cr_, ci_ = S5(l, "cr")[:, s:s + 1], S5(l, "ci")[:, s:s + 1]
            ar_, ai_, nai_ = S5(l, "ar")[:, s:s + 1], S5(l, "ai")[:, s:s + 1], S5(l, "nai")[:, s:s + 1]
            Qr_, Qi_, nQi_ = S5(l, "Qr")[:, s:s + 1], S5(l, "Qi")[:, s:s + 1], S5(l, "nQi")[:, s:s + 1]
            k.mm(bre[:, 0:nt_], s5w[0][:, s, :], u[:, ct, 0:nt_])
            k.mm(bim[:, 0:nt_], s5w[1][:, s, :], u[:, ct, 0:nt_])
            k.act(t1[:, 0:nt_], bim[:, 0:nt_], AF.Copy, scale=ci_)
            k.stt(bur[:, 0:nt_], bre[:, 0:nt_], cr_, t1[:, 0:nt_], MUL, SUB)
            k.act(t2[:, 0:nt_], bre[:, 0:nt_], AF.Copy, scale=ci_)
            k.stt(bui[:, 0:nt_], bim[:, 0:nt_], cr_, t2[:, 0:nt_], MUL, ADD)
            tcb = Tc[l][:, s, 0:sl].un(1).bc([128, nseg, sl])
            tsb = Ts[l][:, s, 0:sl].un(1).bc([128, nseg, sl])
            k.tt(v3(t1), v3(bur), tcb, MUL)
            k.tt(v3(t2), v3(bui), tsb, MUL)
            k.tt(gr[:, 0:nt_], t1[:, 0:nt_], t2[:, 0:nt_], ADD)
            k.tt(v3(t1), v3(bui), tcb, MUL)
            k.tt(v3(t2), v3(bur), tsb, MUL)
            k.tt(gi[:, 0:nt_], t1[:, 0:nt_], t2[:, 0:nt_], SUB)
            magb = S5(l, "mag")[:, s:s + 1]
            if B.kind == "p":
                cre, cim = s5cr[l][:, s:s + 1], s5ci[l][:, s:s + 1]
                for sg in range(nseg):
                    c0 = sg * sl
                    if sg == 0:
                        k.stt(sm[:, 0:1], cre, ar_, gr[:, c0:c0 + 1], MUL, ADD)
                        k.stt(gr[:, c0:c0 + 1], cim, nai_, sm[:, 0:1], MUL, ADD)
                        k.stt(sm[:, 1:2], cim, ar_, gi[:, c0:c0 + 1], MUL, ADD)
                        k.stt(gi[:, c0:c0 + 1], cre, ai_, sm[:, 1:2], MUL, ADD)
                    else:
                        glr, gli = t1[:, c0 - 1:c0], t2[:, c0 - 1:c0]
                        k.stt(sm[:, 0:1], glr, Qr_, gr[:, c0:c0 + 1], MUL, ADD)
                        k.stt(gr[:, c0:c0 + 1], gli, nQi_, sm[:, 0:1], MUL, ADD)
                        k.stt(sm[:, 1:2], gli, Qr_, gi[:, c0:c0 + 1], MUL, ADD)
                        k.stt(gi[:, c0:c0 + 1], glr, Qi_, sm[:, 1:2], MUL, ADD)
                    k.op("dve", "tensor_tensor_scan", out=t1[:, c0:c0 + sl], data0=magb.bc([128, sl]),
                         data1=gr[:, c0:c0 + sl], initial=0.0, op0=MUL, op1=ADD)
                    k.op("dve", "tensor_tensor_scan", out=t2[:, c0:c0 + sl], data0=magb.bc([128, sl]),
                         data1=gi[:, c0:c0 + sl], initial=0.0, op0=MUL, op1=ADD)
                cl, sl_ = Tc[l][:, s, sl - 1:sl], Ts[l][:, s, sl - 1:sl]
                hrl, hil = t1[:, nt_ - 1:nt_], t2[:, nt_ - 1:nt_]
                k.ts(sm[:, 2:3], hil, sl_, MUL)
                k.stt(cre, hrl, cl, sm[:, 2:3], MUL, SUB)
                k.ts(sm[:, 3:4], hrl, sl_, MUL)
                k.stt(cim, hil, cl, sm[:, 3:4], MUL, ADD)
            else:
                g3r, g3i = v3(gr), v3(gi)
                hr0, hi0 = st_r[:, s, :], st_i[:, s, :]
                k.stt(sm[:, 0:16], hr0, ar_, g3r[:, :, 0], MUL, ADD)
                k.stt(g3r[:, :, 0], hi0, nai_, sm[:, 0:16], MUL, ADD)
                k.stt(sm[:, 16:32], hi0, ar_, g3i[:, :, 0], MUL, ADD)
                k.stt(g3i[:, :, 0], hr0, ai_, sm[:, 16:32], MUL, ADD)
                k.ts(coef, C("startmask", 128), magb, MUL)
                k.op("dve", "tensor_tensor_scan", out=t1[:, 0:nt_], data0=coef[:, 0:nt_],
                     data1=gr[:, 0:nt_], initial=0.0, op0=MUL, op1=ADD)
                k.op("dve", "tensor_tensor_scan", out=t2[:, 0:nt_], data0=coef[:, 0:nt_],
                     data1=gi[:, 0:nt_], initial=0.0, op0=MUL, op1=ADD)
            RB = dbg.get("rb_eng", "dve")
            k.tt(v3(gr), v3(t1), tcb, MUL, eng=RB)
            k.tt(v3(gi), v3(t2), tsb, MUL, eng=RB)
            k.tt(bur[:, 0:nt_], gr[:, 0:nt_], gi[:, 0:nt_], SUB, eng=RB)
            k.tt(v3(gr), v3(t2), tcb, MUL, eng=RB)
            k.tt(v3(gi), v3(t1), tsb, MUL, eng=RB)
            k.tt(bui[:, 0:nt_], gr[:, 0:nt_], gi[:, 0:nt_], ADD, eng=RB)
            if B.kind == "s":
                k.copy(so_r[:, s, :], v3(bur)[:, :, sl - 1], "dve")
                k.copy(so_i[:, s, :], v3(bui)[:, :, sl - 1], "dve")
            k.mm(Yps[ct][:, 0:nt_], s5w[2][:, s, :], bur[:, 0:nt_], start=(s % 4 == 0), stop=False)
            k.mm(Yps[ct][:, 0:nt_], s5w[3][:, s, :], bui[:, 0:nt_], start=False, stop=(s % 4 == 3))

        for s0 in range(0, 8, 2):
            k.start_rec()
            tile_body(s0, 0)
            sa = k.stop_rec()
            k.start_rec()
            tile_body(s0 + 1, 1)
            sb_ = k.stop_rec()
            k.interleave(sa, sb_)
        if B.kind == "s":
            k.dma("sp", O["o_s5re_s"][l], so_r, is_output=True)
            k.dma("sp", O["o_s5im_s"][l], so_i, is_output=True)
        elif B.last:
            k.dma("sp", O["o_s5re_p"][l], s5cr[l], is_output=True)
            k.dma("sp", O["o_s5im_p"][l], s5ci[l], is_output=True)
        for ct in range(2):
            k.stt(yc[:, ct, 0:nt_], u[:, ct, 0:nt_], P(l, "s5_d", ct), Yps[ct][:, 0:nt_], MUL, ADD)
            k.act(z[:, ct, 0:nt_], yc[:, ct, 0:nt_], AF.Gelu)
        for co in range(2):
            ps = PF[2 + co]
            for ci2 in range(2):
                k.mm(ps[:, 0:nt_], wglu[l][:, ci2, co * 128:(co + 1) * 128], z[:, ci2, 0:nt_],
                     start=(ci2 == 0), stop=(ci2 == 1))
            k.act(yc[:, co, 0:nt_], ps[:, 0:nt_], AF.Sigmoid, bias=P(l, "b_glu", co))
            k.tt(yc[:, co, 0:nt_], yc[:, co, 0:nt_], z[:, co, 0:nt_], MUL)
        dump(f"yc{l}", yc[:, :, 0:nt_], [128, 2, nt_])
        rms_pair(B, l, [yc[:, 0, 0:nt_], yc[:, 1, 0:nt_]], "g_out_c", 6)

    def mixer_b(B, l, wbase):
        nt_ = B.ntok
        k.barrier()
        scr.reset()
        Tn = B.T
        gate = scr.alloc([128, 2, TBP], F32, "gate")
        xbuf = [scr.alloc([128, B.nseq, 3 + Tn], F32, f"xbuf{ct}") for ct in range(2)]
        xc = scr.alloc([128, TBP], F32, "xc")
        gr_ = scr.alloc([128, TBP], F32, "gr_")
        gi_ = scr.alloc([128, TBP], F32, "gi_")
        a_ = scr.alloc([128, TBP], F32, "a_")
        b_ = scr.alloc([128, TBP], F32, "b_")
        yb = scr.alloc([128, 2, TBP], F32, "yb")
        so = scr.alloc([128, 2, 16], F32, "so")
        st = scr.alloc([128, 2, 16], F32, "st")
        sconv = scr.alloc([128, 2, 16, 3], F32, "sconv")
        if B.kind == "s":
            k.dma("sp", st, I["st_lru"][l])
            k.dma("sp", sconv, I["st_conv"][l])
        for ct in range(2):
            inproj(B, l, wbase, 14 + ct, lambda ps, ct=ct: k.copy(gate[:, ct, 0:nt_], ps))
        for ct in range(2):
            if B.kind == "p":
                k.copy(xbuf[ct][:, 0, 0:3], histB[l][:, ct, :], "dve")
            else:
                k.copy(xbuf[ct][:, :, 0:3], sconv[:, ct, :, :], "dve")
            inproj(B, l, wbase, 16 + ct, lambda ps, ct=ct: k.copy(xbuf[ct][:, :, 3:3 + Tn], seq3(ps, B)))
        for ct in range(2):
            xb = xbuf[ct]
            if B.kind == "p":
                k.copy(histB[l][:, ct, :], xb[:, 0, Tn:Tn + 3], "dve")
                if B.last:
                    k.dma("sp", O["o_conv_p"][l][:, ct, :], histB[l][:, ct, :], is_output=True)
            else:
                k.dma("sp", O["o_conv_s"][l][:, ct], xb[:, :, Tn:Tn + 3], is_output=True)
            xc3 = seq3(xc[:, 0:nt_], B)
            k.ts(xc3, xb[:, :, 0:Tn], P(l, "conv_w", ct * 4 + 0), MUL, P(l, "conv_b", ct), ADD)
            for j in range(1, 4):
                k.stt(xc3, xb[:, :, j:j + Tn], P(l, "conv_w", ct * 4 + j), xc3, MUL, ADD)
            ps1, ps2 = pf(), pf()
            k.mm(ps1[:, 0:nt_], wrg[l][:, ct, :], xc[:, 0:nt_])
            k.mm(ps2[:, 0:nt_], wig[l][:, ct, :], xc[:, 0:nt_])
            k.act(gr_[:, 0:nt_], ps1[:, 0:nt_], AF.Sigmoid, bias=P(l, "b_rg", ct))
            k.act(gi_[:, 0:nt_], ps2[:, 0:nt_], AF.Sigmoid, bias=P(l, "b_ig", ct))
            k.act(a_[:, 0:nt_], gr_[:, 0:nt_], AF.Exp, scale=c8[l][:, ct:ct + 1])
            k.tt(b_[:, 0:nt_], a_[:, 0:nt_], a_[:, 0:nt_], MUL)
            k.ts(b_[:, 0:nt_], b_[:, 0:nt_], -1.0, MUL, 1.0, ADD)
            k.act(b_[:, 0:nt_], b_[:, 0:nt_], AF.Sqrt)
            k.tt(b_[:, 0:nt_], b_[:, 0:nt_], gi_[:, 0:nt_], MUL)
            k.tt(b_[:, 0:nt_], b_[:, 0:nt_], xc[:, 0:nt_], MUL)
            hs = gr_
            if B.kind == "p":
                k.op("dve", "tensor_tensor_scan", out=hs[:, 0:nt_], data0=a_[:, 0:nt_], data1=b_[:, 0:nt_],
                     initial=lruc[l][:, ct:ct + 1], op0=MUL, op1=ADD)
                k.copy(lruc[l][:, ct:ct + 1], hs[:, nt_ - 1:nt_], "dve")
                if B.last and ct == 1:
                    k.dma("sp", O["o_lru_p"][l], lruc[l], is_output=True)
            else:
                a3, b3 = seq3(a_[:, 0:nt_], B), seq3(b_[:, 0:nt_], B)
                k.tt(small[:, 0:16], a3[:, :, 0], st[:, ct, :], MUL)
                k.tt(b3[:, :, 0], b3[:, :, 0], small[:, 0:16], ADD)
                k.tt(a_[:, 0:nt_], a_[:, 0:nt_], C("startmask", 128), MUL)
                k.op("dve", "tensor_tensor_scan", out=hs[:, 0:nt_], data0=a_[:, 0:nt_], data1=b_[:, 0:nt_],
                     initial=0.0, op0=MUL, op1=ADD)
                k.copy(so[:, ct, :], seq3(hs[:, 0:nt_], B)[:, :, Tn - 1], "dve")
            k.act(gi_[:, 0:nt_], gate[:, ct, 0:nt_], AF.Gelu)
            k.tt(yb[:, ct, 0:nt_], hs[:, 0:nt_], gi_[:, 0:nt_], MUL)
        if B.kind == "s":
            k.dma("sp", O["o_lru_s"][l], so, is_output=True)
        dump(f"yb{l}", yb[:, :, 0:nt_], [128, 2, nt_])
        rms_pair(B, l, [yb[:, 0, 0:nt_], yb[:, 1, 0:nt_]], "g_out_b", 4)

    D1 = [T(nc.dram_tensor(f"d1_{l}", [128, 6 * 512], F32, kind="Internal").ap(), f"d1_{l}") for l in range(NL)]
    D2 = [T(nc.dram_tensor(f"d2_{l}", [128, 512], F32, kind="Internal").ap(), f"d2_{l}") for l in range(NL)]
    EXPM05 = math.exp(-0.5)

    def mixer_a(B, l, wbase):
        nt_ = B.ntok
        Tn, ns = B.T, B.nseq
        prompt = B.kind == "p"
        k.barrier()
        scr.reset()
        lo = [scr.alloc([128, ns, 1 + Tn], F32, f"lo{j}") for j in range(2)]
        rkv = [scr.alloc([128, ns, 1 + Tn], F32, f"rkv{j}") for j in range(3)]
        dtmp = scr.alloc([128, nt_], F32, "dtmp")
        l12 = scr.alloc([128, nt_], F32, "l12")
        sgc = scr.alloc([128, nt_], F32, "sgc")
        xs = [scr.alloc([128, nt_], F32, f"xs{j}") for j in range(3)]
        UN = 256 if prompt else 128
        lw, cs, a_, g_, kk, kmod, beta, tA, tB, bonus, ysb, yc = [
            scr.alloc([128, UN], F32, nm) for nm in
            ("lw", "cs", "a_", "g_", "kk", "kmod", "beta", "tA", "tB", "bonus", "ysb", "yc")]
        gam = scr.alloc([128, 8], F32, "gam")
        if prompt:
            ARt = scr.alloc([128, 4, 192], BF16, "ARt")
            Bt_bd = scr.alloc([128, 4, 128], BF16, "Bt_bd")
            Kt_bd = scr.alloc([128, 4, 128], BF16, "Kt_bd")
            Vt_bd = scr.alloc([128, 4, 128], BF16, "Vt_bd")
            XM = scr.alloc([128, 4, 192], BF16, "XM")
            KM = scr.alloc([128, 4, 192], BF16, "KM")
            Btok = scr.alloc([128, 4, 128], BF16, "Btok")
            Ktok = scr.alloc([128, 4, 128], BF16, "Ktok")
            Vbd = scr.alloc([128, 4, 128], BF16, "Vbd")
            TT5 = scr.alloc([128, 4, 128], BF16, "TT5")
            Xr = [scr.alloc([128, 4, 128], BF16, f"Xr{i}") for i in range(2)]
            Zr = [scr.alloc([128, 4, 128], BF16, f"Zr{i}") for i in range(2)]
            TTt = [scr.alloc([128, 4, 128], BF16, f"TTt{i}") for i in range(2)]
            Wsb = scr.alloc([128, 128], BF16, "Wsb")
            Ubd = scr.alloc([128, 128], BF16, "Ubd")
            Pb16 = scr.alloc([128, 128], BF16, "Pb16")
            for t_ in (ARt, Bt_bd, Kt_bd, Vt_bd):
                k.memset(t_, 0.0)
        else:
            S = scr.alloc([128, 4096], F32, "S")
            tmp3 = scr.alloc([128, 4096], F32, "tmp3")
            RW = scr.alloc([128, 8, 6, 64], F32, "RW")
            KKA = scr.alloc([128, 8, 64], F32, "KKA")
            Ys = scr.alloc([128, 8, 64], F32, "Ys")
            TM = T(tmp3.ap[:, 0:3072].rearrange("p (q h j) -> p q h j", q=6, h=4), "TM", scr=True)
            YT = scr.alloc([128, 8, 64], F32, "YT")
            skk = scr.alloc([128, 64], F32, "skk")
            bonus_s = scr.alloc([128, 4, 128], F32, "bonus_s")
            g_s = scr.alloc([128, 4, 128], F32, "g_s")

        def load_shift(raw, tile):
            if prompt:
                k.copy(raw[:, 0, 0:1], histA[l][:, tile:tile + 1], "dve")
            else:
                k.dma("sp", raw[:, :, 0], I["st_shift"][l][:, tile, :])
            inproj(B, l, wbase, tile, lambda ps: k.copy(raw[:, :, 1:1 + Tn], seq3(ps, B)))
            if prompt:
                k.copy(histA[l][:, tile:tile + 1], raw[:, 0, Tn:Tn + 1], "dve")
            else:
                k.dma("sp", O["o_shift_s"][l][:, tile, :], raw[:, :, Tn], is_output=True)

        def shift(raw, tile, dst):
            k.tt(seq3(dtmp[:, 0:nt_], B), raw[:, :, 0:Tn], raw[:, :, 1:1 + Tn], SUB)
            k.stt(seq3(dst[:, 0:nt_], B), seq3(dtmp[:, 0:nt_], B), P(l, "mu", tile), raw[:, :, 1:1 + Tn], MUL, ADD)

        load_shift(lo[0], 12)
        shift(lo[0], 12, l12)
        load_shift(lo[1], 13)
        shift(lo[1], 13, sgc)
        k.act(l12[0:64, 0:nt_], l12[0:64, 0:nt_], AF.Tanh)
        k.act(sgc[:, 0:nt_], sgc[:, 0:nt_], AF.Sigmoid)

        units = [(0, 256), (256, 256)] if prompt else [(0, 128)]
        for hp in range(4):
            Pb = Pbd[l][hp]
            if prompt:
                k.copy(Pb16, Pb, "dve")
            for j, tile in enumerate((hp, 4 + hp, 8 + hp)):
                load_shift(rkv[j], tile)
                shift(rkv[j], tile, xs[j])
            if l == 0 and hp == 0:
                dump("xs0", xs[0][:, 0:nt_], [128, nt_])
            hsl = slice(hp * 128, (hp + 1) * 128)
            for (c0, n) in units:
                sl = slice(c0, c0 + n)
                xr_, xk_, xv_ = xs[0][:, sl], xs[1][:, sl], xs[2][:, sl]
                N_ = slice(0, n)
                ps = ph()
                k.mm(ps[:, N_], lora1[l][0:64, hsl], l12[0:64, sl])
                k.act(lw[:, N_], ps[:, N_], AF.Sigmoid, bias=P(l, "w0", hp))
                k.ts(lw[:, N_], lw[:, N_], -EXPM05, MUL)
                ps = ph()
                k.mm(ps[:, N_], lora1[l][64:128, hsl], l12[64:128, sl])
                k.act(a_[:, N_], ps[:, N_], AF.Sigmoid, bias=P(l, "a0", hp))
                ps = ph()
                k.mm(ps[:, N_], wg2[l][:, hsl], sgc[:, sl])
                gdst = g_[:, N_] if prompt else g_s[:, hp, :]
                k.copy(gdst, ps[:, N_])
                k.ts(kk[:, N_], xk_, P(l, "k_k", hp), MUL)
                k.tt(tA[:, N_], kk[:, N_], kk[:, N_], MUL)
                ps = ph()
                k.mm(ps[:, N_], C("onesblk", 128), tA[:, N_])
                k.ts(tB[:, N_], ps[:, N_], 1e-24, ALU.max)
                k.act(tB[:, N_], tB[:, N_], AF.Ln)
                k.act(tB[:, N_], tB[:, N_], AF.Exp, scale=-0.5)
                k.tt(kk[:, N_], kk[:, N_], tB[:, N_], MUL)
                k.ts(tA[:, N_], a_[:, N_], -1.0, ADD, P(l, "k_a", hp), MUL)
                k.stt(kmod[:, N_], tA[:, N_], 1.0, xk_, ADD, MUL)
                k.tt(beta[:, N_], kk[:, N_], a_[:, N_], MUL)
                k.stt(tA[:, N_], xr_, P(l, "r_k", hp), kmod[:, N_], MUL, MUL)
                ps = ph()
                k.mm(ps[:, N_], C("onesblk", 128), tA[:, N_])
                bdst = bonus[:, N_] if prompt else bonus_s[:, hp, :]
                k.tt(bdst, ps[:, N_], xv_, MUL)
                if prompt:
                    ch = lambda v: v.re("p (c t) -> p c t", t=64)
                    k.op("dve", "tensor_tensor_scan", out=cs[:, N_], data0=C("cumcoef", 256), data1=lw[:, N_],
                         initial=0.0, op0=MUL, op1=ADD)
                    k.tt(lw[:, N_], cs[:, N_], lw[:, N_], SUB)
                    k.act(tA[:, N_], cs[:, N_], AF.Exp)
                    k.copy(gam[:, 0:4], tA[:, 63:256:64], "dve")
                    k.tt(ARt[:, :, 128:192], ch(xr_), ch(tA[:, N_]), MUL)
                    k.act(tB[:, N_], cs[:, N_], AF.Exp, scale=-1.0)
                    for hh in range(2):
                        rows = slice(hh * 64, hh * 64 + 64)
                        cols = slice(hh * 64, hh * 64 + 64)
                        k.tt(Bt_bd[rows, :, cols], ch(beta[rows, N_]), ch(tB[rows, N_]), MUL)
                        k.tt(Kt_bd[rows, :, cols], ch(kmod[rows, N_]), ch(tB[rows, N_]), MUL)
                        k.copy(Vt_bd[rows, :, cols], ch(xs[2][rows, sl]), "act")
                    k.act(tA[:, N_], lw[:, N_], AF.Exp)
                    for hh in range(2):
                        rows = slice(hh * 64, hh * 64 + 64)
                        cols = slice(hh * 64, hh * 64 + 64)
                        k.stt(ARt[rows, :, cols], ch(kk[rows, N_]), -1.0, ch(tA[rows, N_]), MUL, MUL)
                    for c in range(4):
                        psa = ph()
                        k.mm(psa[:, 0:192], Bt_bd[:, c, :], ARt[:, c, :])
                        k.tt(XM[:, c, :], psa[:, 0:192], C("maskA", 192), MUL)
                        psb = ph()
                        k.mm(psb[:, 0:192], Kt_bd[:, c, :], ARt[:, c, :])
                        k.tt(KM[:, c, :], psb[:, 0:192], C("maskA", 192), MUL)
                        psc = ph()
                        k.mm(psc[:, 0:128], ARt[:, c, 0:128], Bt_bd[:, c, :])
                        k.tt(Zr[0][:, c, :], psc[:, 0:128], C("maskSL", 128), MUL)
                        for (src, dst) in ((Bt_bd, Btok), (Kt_bd, Ktok), (Vt_bd, Vbd)):
                            pst = ph()
                            pst16 = V(pst, pst.ap.bitcast(BF16))
                            k.tr(pst16[:, 0:128], src[:, c, :], idb)
                            k.copy(dst[:, c, :], pst16[:, 0:128])
                        k.tt(TTt[0][:, c, :], XM[:, c, 0:128], ident, ADD)
                    for r in range(1, 7):
                        for c in range(4):
                            Zp = Zr[(r - 1) % 2][:, c, :]
                            Xp = XM[:, c, 0:128] if r == 1 else Xr[(r - 1) % 2][:, c, :]
                            if r <= 4:
                                ps1 = ph()
                                k.mm(ps1[:, 0:128], Zp, Xp)
                                k.copy(Xr[r % 2][:, c, :], ps1[:, 0:128])
                            if 2 <= r <= 5:
                                psT = ph()
                                k.mm(psT[:, 0:128], Zp, TTt[(r - 2) % 2][:, c, :])
                                k.tt(TTt[(r - 1) % 2][:, c, :], psT[:, 0:128], TTt[(r - 2) % 2][:, c, :], ADD)
                            if r <= 5:
                                ps2 = ph()
                                k.mm(ps2[:, 0:128], Xp, Zp)
                                k.copy(Zr[r % 2][:, c, :], ps2[:, 0:128])
                            if r == 6:
                                psT = ph()
                                k.mm(psT[:, 0:128], Zp, TTt[0][:, c, :])
                                k.tt(TT5[:, c, :], psT[:, 0:128], TTt[0][:, c, :], ADD)
                    Yps = PF[3]
                    for c in range(4):
                        Wps = ph()
                        k.mm(Wps[:, 0:128], ARt[:, c, 0:128], Pb16, True, False)
                        k.mm(Wps[:, 0:128], KM[:, c, 0:128], Vbd[:, c, :], False, True)
                        k.copy(Wsb, Wps[:, 0:128])
                        k.ts(Pb, Pb, gam[:, c:c + 1], MUL)
                        Ups = ph()
                        k.mm(Ups[:, 0:128], TT5[:, c, :], Wsb)
                        k.copy(Ubd, Ups[:, 0:128])
                        ycol = Yps[:, c * 64:(c + 1) * 64]
                        k.mm(ycol, Pb16, ARt[:, c, 128:192], True, False)
                        k.mm(ycol, Ubd, XM[:, c, 128:192], False, False)
                        k.mm(ycol, Vbd[:, c, :], KM[:, c, 128:192], False, True)
                        Pps = ph()
                        k.mm(Pps[:, 0:128], Btok[:, c, :], Ubd, True, False)
                        k.mm(Pps[:, 0:128], Ktok[:, c, :], Vbd[:, c, :], False, True)
                        k.stt(Pb, Pps[:, 0:128], gam[:, c:c + 1], Pb, MUL, ADD)
                        k.copy(Pb16, Pb, "dve")
                    k.copy(ysb[:, N_], Yps[:, N_])
                    finish_unit(l, hp, sl, N_, ysb, yc, tA, tB, bonus[:, N_], g_[:, N_])
                else:
                    k.act(tA[:, N_], lw[:, N_], AF.Exp)
                    for qi, src in enumerate((xr_, tA[:, N_], kmod[:, N_], xv_, kk[:, N_], a_[:, N_])):
                        pst = ph()
                        k.tr(pst[:, 0:128], src, ident)
                        k.copy(TM[:, qi, hp, :], pst[:, 0:128])
            if prompt and B.last:
                for hh in range(2):
                    rows = slice(hh * 64, hh * 64 + 64)
                    k.dma("sp", O["o_wkv_p"][l][2 * hp + hh], Pb[rows, rows], is_output=True)
        if prompt:
            if B.last:
                k.dma("sp", O["o_shift_p"][l], histA[l], is_output=True)
            return
        k.dma("sp", D1[l], TM.re("p q h j -> p (q h j)"))
        for b in range(SB_PER):
            src = D1[l][b * 8:(b + 1) * 8, :].re("t (q h j) -> h (t q) j", q=6, h=8)
            k.dma("sp", RW[b * 8:(b + 1) * 8].re("p t q j -> p (t q) j"), src)
        k.tt(KKA, RW[:, :, 4, :], RW[:, :, 5, :], MUL)
        ISPL = dbg.get("ispl", 64)
        parts = []
        for (i0, i1, eng) in ((0, ISPL, "dve"), (ISPL, 64, "pool")):
            if i1 <= i0:
                continue
            ni = i1 - i0
            S_ = T(S.ap[:, i0 * 64:i1 * 64].rearrange("p (i j) -> p i j", j=64), f"S_{eng}", scr=True)
            T_ = T(tmp3.ap[:, i0 * 64:i1 * 64].rearrange("p (i j) -> p i j", j=64), f"T3_{eng}", scr=True)
            k_ = T(skk.ap[:, i0:i1], f"skk_{eng}", scr=True)
            Y_ = T(Ys.ap[:, :, i0:i1], f"Ys_{eng}", scr=True)
            k.dma("sp", S_.re("p i j -> p (i j)"), I["st_wkv"][l][:, i0 * 64:i1 * 64])
            parts.append((i0, i1, ni, eng, S_, T_, k_, Y_))
        for t in range(DEC_T):
            for (i0, i1, ni, eng, S_, T_, k_, Y_) in parts:
                bi = lambda v: v.un(1).bc([128, ni, 64])
                bj = lambda v: v.un(2).bc([128, ni, 64])
                k.tt(T_, S_, bi(RW[:, t, 4, :]), MUL, eng=eng)
                k.op("dve", "tensor_reduce", out=k_, in_=T_, axis=AX.X, op=ADD)
                k.tt(S_, S_, bi(RW[:, t, 1, :]), MUL, eng=eng)
                k.tt(T_, bj(k_), bi(KKA[:, t, :]), MUL, eng=eng)
                k.tt(S_, S_, T_, SUB, eng=eng)
                k.tt(T_, bj(RW[:, t, 3, i0:i1]), bi(RW[:, t, 2, :]), MUL, eng=eng)
                k.tt(S_, S_, T_, ADD, eng=eng)
                k.tt(T_, S_, bi(RW[:, t, 0, :]), MUL, eng=eng)
                k.op("dve", "tensor_reduce", out=Y_[:, t, :], in_=T_, axis=AX.X, op=ADD)
        for (i0, i1, ni, eng, S_, T_, k_, Y_) in parts:
            k.dma("sp", O["o_wkv_s"][l][:, i0 * 64:i1 * 64], S_.re("p i j -> p (i j)"), is_output=True)
        Ysd = [p_[7] for p_ in parts]
        for (i0, i1, ni, eng, S_, T_, k_, Y_) in parts:
            k.dma("sp", D2[l].re("p (t i) -> p t i", t=8)[:, :, i0:i1], Y_)
        for b in range(SB_PER):
            src = D2[l][b * 8:(b + 1) * 8, :].re("h (t i) -> t h i", t=8)
            k.dma("sp", YT[b * 8:(b + 1) * 8], src)
        N_ = slice(0, 128)
        for hp in range(4):
            pst = ph()
            k.tr(pst[:, 0:128], YT[:, 2 * hp:2 * hp + 2, :].re("p h i -> p (h i)"), ident)
            k.copy(ysb[:, N_], pst[:, 0:128])
            finish_unit(l, hp, slice(0, 128), N_, ysb, yc, tA, tB, bonus_s[:, hp, :], g_s[:, hp, :])

    def mixer_a_prompt(B, l, wbase):
        nt_, Tn = B.ntok, B.T
        k.barrier()
        scr.reset()
        lo = [scr.alloc([128, 1, 1 + Tn], F32, f"lo{j}") for j in range(2)]
        rkv = [scr.alloc([128, 1, 1 + Tn], F32, f"rkv{j}") for j in range(3)]
        dtmp = scr.alloc([128, nt_], F32, "dtmp")
        l12 = scr.alloc([128, nt_], F32, "l12")
        sgc = scr.alloc([128, nt_], F32, "sgc")
        xs = [scr.alloc([128, nt_], F32, f"xs{j}") for j in range(3)]
        UN = 256
        lw, cs, a_, kk, kmod, beta, tA, tB, ysb, yc, tAf, tBf = [
            scr.alloc([128, UN], F32, nm) for nm in
            ("lw", "cs", "a_", "kk", "kmod", "beta", "tA", "tB", "ysb", "yc", "tAf", "tBf")]
        sets3, sets2e, sets2g = [], [], []
        for si in range(3):
            d = {}
            d["ARt"] = scr.alloc([128, 4, 192], BF16, f"ARt{si}")
            d["gam"] = scr.alloc([128, 8], F32, f"gam{si}")
            d["bonus"] = scr.alloc([128, UN], F32, f"bonus{si}")
            d["g_"] = scr.alloc([128, UN], F32, f"g_{si}")
            sets3.append(d)
        for si in range(2):
            d = {}
            for nm in ("Bt_bd", "Kt_bd", "Vt_bd"):
                d[nm] = scr.alloc([128, 4, 128], BF16, f"{nm}{si}")
            sets2e.append(d)
            d = {}
            d["MrT"] = scr.alloc([128, 8, 64], BF16, f"MrT{si}")
            for nm in ("Xs", "MakT", "Btok", "Ktok", "Vbd", "TT5"):
                d[nm] = scr.alloc([128, 4, 128], BF16, f"{nm}{si}")
            sets2g.append(d)
        Xr = [scr.alloc([128, 4, 128], BF16, f"Xr{i}") for i in range(2)]
        Zr = [scr.alloc([128, 4, 128], BF16, f"Zr{i}") for i in range(2)]
        TTt = [scr.alloc([128, 4, 128], BF16, f"TTt{i}") for i in range(2)]
        Wsb = scr.alloc([128, 128], BF16, "Wsb")
        Ubd = scr.alloc([128, 128], BF16, "Ubd")
        Pb16 = scr.alloc([128, 128], BF16, "Pb16")
        for d in sets3:
            k.memset(d["ARt"], 0.0)
        for d in sets2e:
            for nm in ("Bt_bd", "Kt_bd", "Vt_bd"):
                k.memset(d[nm], 0.0)

        def load_shift(raw, tile):
            k.copy(raw[:, 0, 0:1], histA[l][:, tile:tile + 1], "dve")
            inproj(B, l, wbase, tile, lambda ps: k.copy(raw[:, :, 1:1 + Tn], seq3(ps, B)))
            k.copy(histA[l][:, tile:tile + 1], raw[:, 0, Tn:Tn + 1], "dve")

        def shift(raw, tile, dst):
            k.tt(seq3(dtmp[:, 0:nt_], B), raw[:, :, 0:Tn], raw[:, :, 1:1 + Tn], SUB)
            k.stt(seq3(dst[:, 0:nt_], B), seq3(dtmp[:, 0:nt_], B), P(l, "mu", tile), raw[:, :, 1:1 + Tn], MUL, ADD)

        load_shift(lo[0], 12)
        shift(lo[0], 12, l12)
        load_shift(lo[1], 13)
        shift(lo[1], 13, sgc)
        k.act(l12[0:64, 0:nt_], l12[0:64, 0:nt_], AF.Tanh)
        k.act(sgc[:, 0:nt_], sgc[:, 0:nt_], AF.Sigmoid)
        ch = lambda v: v.re("p (c t) -> p c t", t=64)

        def ew(ui):
            hp, sub = ui // 2, ui % 2
            D3, D2e = sets3[ui % 3], sets2e[ui % 2]
            ARt, gam, bonus, g_ = D3["ARt"], D3["gam"], D3["bonus"], D3["g_"]
            Bt_bd, Kt_bd, Vt_bd = D2e["Bt_bd"], D2e["Kt_bd"], D2e["Vt_bd"]
            if sub == 0:
                for j, tile in enumerate((hp, 4 + hp, 8 + hp)):
                    load_shift(rkv[j], tile)
                    shift(rkv[j], tile, xs[j])
            hsl = slice(hp * 128, (hp + 1) * 128)
            c0, n = sub * 256, 256
            sl = slice(c0, c0 + n)
            xr_, xk_, xv_ = xs[0][:, sl], xs[1][:, sl], xs[2][:, sl]
            N_ = slice(0, n)
            ps = ph()
            k.mm(ps[:, N_], lora1[l][0:64, hsl], l12[0:64, sl])
            k.act(lw[:, N_], ps[:, N_], AF.Sigmoid, bias=P(l, "w0", hp))
            k.act(lw[:, N_], lw[:, N_], AF.Copy, scale=-EXPM05)
            ps = ph()
            k.mm(ps[:, N_], lora1[l][64:128, hsl], l12[64:128, sl])
            k.act(a_[:, N_], ps[:, N_], AF.Sigmoid, bias=P(l, "a0", hp))
            ps = ph()
            k.mm(ps[:, N_], wg2[l][:, hsl], sgc[:, sl])
            k.copy(g_[:, N_], ps[:, N_])
            k.act(kk[:, N_], xk_, AF.Copy, scale=P(l, "k_k", hp))
            k.tt(tA[:, N_], kk[:, N_], kk[:, N_], MUL)
            ps = ph()
            k.mm(ps[:, N_], C("onesblk", 128), tA[:, N_])
            k.ts(tB[:, N_], ps[:, N_], 1e-24, ALU.max)
            k.act(tB[:, N_], tB[:, N_], AF.Ln)
            k.act(tB[:, N_], tB[:, N_], AF.Exp, scale=-0.5)
            k.tt(kk[:, N_], kk[:, N_], tB[:, N_], MUL)
            k.ts(tA[:, N_], a_[:, N_], -1.0, ADD, P(l, "k_a", hp), MUL)
            k.stt(kmod[:, N_], tA[:, N_], 1.0, xk_, ADD, MUL)
            k.tt(beta[:, N_], kk[:, N_], a_[:, N_], MUL)
            k.stt(tA[:, N_], xr_, P(l, "r_k", hp), kmod[:, N_], MUL, MUL)
            ps = ph()
            k.mm(ps[:, N_], C("onesblk", 128), tA[:, N_])
            k.tt(bonus[:, N_], ps[:, N_], xv_, MUL)
            k.op("dve", "tensor_tensor_scan", out=cs[:, N_], data0=C("cumcoef", 256), data1=lw[:, N_],
                 initial=0.0, op0=MUL, op1=ADD)
            k.tt(lw[:, N_], cs[:, N_], lw[:, N_], SUB)
            k.act(tA[:, N_], cs[:, N_], AF.Exp)
            k.copy(gam[:, 0:4], tA[:, 63:256:64], "dve")
            k.tt(ARt[:, :, 128:192], ch(xr_), ch(tA[:, N_]), MUL)
            k.act(tB[:, N_], cs[:, N_], AF.Exp, scale=-1.0)
            for hh in range(2):
                rows = slice(hh * 64, hh * 64 + 64)
                cols = slice(hh * 64, hh * 64 + 64)
                k.tt(Bt_bd[rows, :, cols], ch(beta[rows, N_]), ch(tB[rows, N_]), MUL)
                k.tt(Kt_bd[rows, :, cols], ch(kmod[rows, N_]), ch(tB[rows, N_]), MUL)
                k.copy(Vt_bd[rows, :, cols], ch(xs[2][rows, sl]), "act")
            k.act(tA[:, N_], lw[:, N_], AF.Exp)
            for hh in range(2):
                rows = slice(hh * 64, hh * 64 + 64)
                cols = slice(hh * 64, hh * 64 + 64)
                k.stt(ARt[rows, :, cols], ch(kk[rows, N_]), -1.0, ch(tA[rows, N_]), MUL, MUL)

        def gn(ui):
            D3, D2e, D2g = sets3[ui % 3], sets2e[ui % 2], sets2g[ui % 2]
            ARt = D3["ARt"]
            Bt_bd, Kt_bd, Vt_bd = D2e["Bt_bd"], D2e["Kt_bd"], D2e["Vt_bd"]
            Xs, MakT, MrT, Btok, Ktok, Vbd, TT5 = (D2g[n] for n in ("Xs", "MakT", "MrT", "Btok", "Ktok", "Vbd", "TT5"))
            c4 = lambda v: v[:, 0:512].re("p (c t) -> p c t", c=4)
            mSU = C("maskA", 128).un(1).bc([128, 4, 128])
            mSL = C("maskSL", 128).un(1).bc([128, 4, 128])
            mIU = consts[:, CC["maskA"] + 128:CC["maskA"] + 192].un(1).bc([128, 8, 64])
            idb4 = ident.un(1).bc([128, 4, 128])
            bX = ph()
            for c in range(4):
                k.mm(bX[:, c * 128:(c + 1) * 128], Bt_bd[:, c, :], ARt[:, c, 0:128])
            k.tt(Xs, c4(bX), mSU, MUL)
            bZ = ph()
            for c in range(4):
                k.mm(bZ[:, c * 128:(c + 1) * 128], ARt[:, c, 0:128], Bt_bd[:, c, :])
            k.tt(Zr[0], c4(bZ), mSL, MUL)
            bK = ph()
            for c in range(4):
                k.mm(bK[:, c * 128:(c + 1) * 128], Kt_bd[:, c, :], ARt[:, c, 0:128])
            k.tt(MakT, c4(bK), mSU, MUL)
            bR = ph()
            for c in range(4):
                k.mm(bR[:, c * 64:(c + 1) * 64], Bt_bd[:, c, :], ARt[:, c, 128:192])
            for c in range(4):
                k.mm(bR[:, 256 + c * 64:256 + (c + 1) * 64], Kt_bd[:, c, :], ARt[:, c, 128:192])
            k.tt(MrT, bR[:, 0:512].re("p (c t) -> p c t", c=8), mIU, MUL)
            k.tt(TTt[0], Xs, idb4, ADD)
            for (src, dst) in ((Bt_bd, Btok), (Kt_bd, Ktok), (Vt_bd, Vbd)):
                pst = ph()
                pt_ = pst.t if isinstance(pst, V) else pst
                pst16 = V(pt_, pst.ap.bitcast(BF16))
                for c in range(4):
                    k.tr(pst16[:, c * 128:(c + 1) * 128], src[:, c, :], idb)
                k.copy(dst, pst16[:, 0:512].re("p (c t) -> p c t", c=4))
            for r in range(1, 7):
                Zp = Zr[(r - 1) % 2]
                Xp = Xs if r == 1 else Xr[(r - 1) % 2]
                if r <= 4:
                    bA = ph()
                    for c in range(4):
                        k.mm(bA[:, c * 128:(c + 1) * 128], Zp[:, c, :], Xp[:, c, :])
                    k.copy(Xr[r % 2], c4(bA), "act")
                if 2 <= r <= 5:
                    bB = ph()
                    for c in range(4):
                        k.mm(bB[:, c * 128:(c + 1) * 128], idb, TTt[(r - 2) % 2][:, c, :], True, False)
                        k.mm(bB[:, c * 128:(c + 1) * 128], Zp[:, c, :], TTt[(r - 2) % 2][:, c, :], False, True)
                    k.copy(TTt[(r - 1) % 2], c4(bB), "act")
                if r <= 5:
                    bC = ph()
                    for c in range(4):
                        k.mm(bC[:, c * 128:(c + 1) * 128], Xp[:, c, :], Zp[:, c, :])
                    k.copy(Zr[r % 2], c4(bC), "act")
                if r == 6:
                    bB = ph()
                    for c in range(4):
                        k.mm(bB[:, c * 128:(c + 1) * 128], idb, TTt[0][:, c, :], True, False)
                        k.mm(bB[:, c * 128:(c + 1) * 128], Zp[:, c, :], TTt[0][:, c, :], False, True)
                    k.copy(TT5, c4(bB), "act")

        def tail(ui):
            hp, sub = ui // 2, ui % 2
            D3, D2g = sets3[ui % 3], sets2g[ui % 2]
            ARt, gam, bonus, g_ = D3["ARt"], D3["gam"], D3["bonus"], D3["g_"]
            MakT, MrT, Btok, Ktok, Vbd, TT5 = (D2g[n] for n in ("MakT", "MrT", "Btok", "Ktok", "Vbd", "TT5"))
            Pb = Pbd[l][hp]
            c0, n = sub * 256, 256
            sl = slice(c0, c0 + n)
            N_ = slice(0, n)
            if sub == 0:
                k.copy(Pb16, Pb, "dve")
            Yps = PF[3]
            for c in range(4):
                Wps = ph()
                k.mm(Wps[:, 0:128], ARt[:, c, 0:128], Pb16, True, False)
                k.mm(Wps[:, 0:128], MakT[:, c, :], Vbd[:, c, :], False, True)
                k.copy(Wsb, Wps[:, 0:128], "act")
                k.ts(Pb, Pb, gam[:, c:c + 1], MUL)
                Ups = ph()
                k.mm(Ups[:, 0:128], TT5[:, c, :], Wsb)
                k.copy(Ubd, Ups[:, 0:128], "act")
                ycol = Yps[:, c * 64:(c + 1) * 64]
                k.mm(ycol, Pb16, ARt[:, c, 128:192], True, False)
                k.mm(ycol, Ubd, MrT[:, c, :], False, False)
                k.mm(ycol, Vbd[:, c, :], MrT[:, 4 + c, :], False, True)
                Pps = ph()
                k.mm(Pps[:, 0:128], Btok[:, c, :], Ubd, True, False)
                k.mm(Pps[:, 0:128], Ktok[:, c, :], Vbd[:, c, :], False, True)
                k.stt(Pb, Pps[:, 0:128], gam[:, c:c + 1], Pb, MUL, ADD)
                k.copy(Pb16, Pb, "dve")
            k.copy(ysb[:, N_], Yps[:, N_])
            finish_unit(l, hp, sl, N_, ysb, yc, tAf, tBf, bonus[:, N_], g_[:, N_])
            if sub == 1 and B.last:
                for hh in range(2):
                    rows = slice(hh * 64, hh * 64 + 64)
                    k.dma("sp", O["o_wkv_p"][l][2 * hp + hh], Pb[rows, rows], is_output=True)

        PTf = [V(PT[i], PT[i].ap.bitcast(F32)) for i in range(2)]
        pools = {"tail": ([PHb[0]], None), "gn": ([PHb[1], PF[0], PTf[0], PTf[1]], None),
                 "ew": ([PF[1], PF[2]], [PF[1], PF[2]])}
        pf_save = pf_pool[0]

        def rec(kind, fnc, ui):
            if ui is None or ui > 7:
                return []
            ph_pool[0] = pools[kind][0]
            if pools[kind][1] is not None:
                pf_pool[0] = pools[kind][1]
            k.start_rec()
            fnc(ui)
            return k.stop_rec()

        k.interleave(rec("ew", ew, 0))
        k.interleave(rec("gn", gn, 0), rec("ew", ew, 1))
        for ui in range(8):
            k.interleave(rec("tail", tail, ui), rec("gn", gn, ui + 1), rec("ew", ew, ui + 2))
        pf_pool[0] = pf_save
        ph_pool[0] = PHpool
        if B.last:
            k.dma("sp", O["o_shift_p"][l], histA[l], is_output=True)

    def finish_unit(l, hp, sl, N_, ysb, yc, tA, tB, bonus_v, g_v):
        ps = ph()
        k.mm(ps[:, N_], C("avgblk", 128), ysb[:, N_])
        k.tt(yc[:, N_], ysb[:, N_], ps[:, N_], SUB)
        k.act(tA[:, N_], yc[:, N_], AF.Square)
        ps = ph()
        k.mm(ps[:, N_], C("avgblk", 128), tA[:, N_])
        rstd_of(ps[:, N_], tB[:, N_], 1.0, GN_EPS)
        k.tt(yc[:, N_], yc[:, N_], tB[:, N_], MUL)
        k.act(yc[:, N_], yc[:, N_], AF.Identity, scale=P(l, "lnx_w", hp), bias=P(l, "lnx_b", hp))
        k.tt(yc[:, N_], yc[:, N_], bonus_v, ADD)
        k.tt(mixed[:, hp, sl], yc[:, N_], g_v, MUL)

    def outproj(B, l, wbase):
        for j in range(2):
            wc = wget(wbase + 5 + j)
            for i in range(B.nt):
                ps = pf()
                for m in range(8):
                    k.mm(ps, mixed[:, m, i * 128:(i + 1) * 128], wc[:, m, :], start=(m == 0), stop=(m == 7))
                hv = h[:, i, j * 512:(j + 1) * 512]
                k.tt(hv, hv, ps, ADD)

    ffn_bufs = {}

    def ffn_prepare():
        k.barrier()
        scr.reset()
        ffn_bufs["hT"] = scr.alloc([128, NF, TBP], BF16, "hT")
        ffn_bufs["sgt"] = [scr.alloc([128, TBP], F32, f"sgt{i}") for i in range(2)]

    def ffn(B, l, wbase):
        nt_ = B.ntok
        if "hT" not in ffn_bufs:
            ffn_prepare()
        hT, sgt = ffn_bufs.pop("hT"), ffn_bufs.pop("sgt")
        norm_T(B, l, "g_ffn")
        for p_ in range(11):
            wc = wget(wbase + 7 + p_)
            for q in range(2):
                f_ = 2 * p_ + q
                gps, ups = PF[2 * (f_ % 2)], PF[2 * (f_ % 2) + 1]
                for c in range(8):
                    k.mm(gps[:, 0:nt_], wc[:, c, q * 128:(q + 1) * 128], xnT[:, c, 0:nt_], start=(c == 0), stop=(c == 7))
                for c in range(8):
                    k.mm(ups[:, 0:nt_], wc[:, c, 256 + q * 128:256 + (q + 1) * 128], xnT[:, c, 0:nt_],
                         start=(c == 0), stop=(c == 7))
                sg_ = sgt[f_ % 2]
                k.act(sg_[:, 0:nt_], gps[:, 0:nt_], AF.Silu)
                k.tt(hT[:, f_, 0:nt_], sg_[:, 0:nt_], ups[:, 0:nt_], MUL)
        for j in range(2):
            for g in range(6):
                nfc = 4 if g < 5 else 2
                wc = wget(wbase + 18 + j * 6 + g)
                for fc in range(nfc):
                    f_ = g * 4 + fc
                    for i in range(B.nt):
                        k.mm(PF[i], hT[:, f_, i * 128:(i + 1) * 128], wc[:, fc, :], start=(f_ == 0), stop=(f_ == NF - 1))
            for i in range(B.nt):
                hv = h[:, i, j * 512:(j + 1) * 512]
                k.tt(hv, hv, PF[i], ADD)

    def ple(B, l, wbase, tok0):
        src = (I["pp"][l][tok0:tok0 + B.ntok, :] if B.kind == "p" else I["psm"][l]).rearrange("(i p) q -> p i q", p=128)
        k.dma("sp", ptile[:, 0:B.nt, :], src)
        k.copy(pbf[:, 0:B.nt, :], ptile[:, 0:B.nt, :], "dve")
        for i in range(B.nt):
            pt = PT[i % 2]
            for q in range(2):
                k.tr(pt[:, q * 128:(q + 1) * 128], pbf[:, i, q * 128:(q + 1) * 128], idb)
            for q in range(2):
                k.copy(pT[:, q, i * 128:(i + 1) * 128], pt[:, q * 128:(q + 1) * 128], "act" if i % 2 == 0 else "dve")
        norm_T(B, l, "g_ple")
        wp = wple_t
        k.dma("pool", wple_t, I["w_ple"][l].rearrange("(c p) n -> p c n", p=128))
        for j in range(2):
            wc = wget(wbase + 30 + j)
            for i in range(B.nt):
                gps, pps = PF[2 * (i % 2)], PF[2 * (i % 2) + 1]
                for c in range(8):
                    k.mm(gps, xnT[:, c, i * 128:(i + 1) * 128], wc[:, c, :], start=(c == 0), stop=(c == 7))
                for q in range(2):
                    k.mm(pps, pT[:, q, i * 128:(i + 1) * 128], wp[:, q, j * 512:(j + 1) * 512], start=(q == 0), stop=(q == 1))
                tg = tmpt[i % 2]
                k.act(tg, gps, AF.Sigmoid)
                k.tt(tg, tg, pps, MUL)
                hv = h[:, i, j * 512:(j + 1) * 512]
                k.tt(hv, hv, tg, ADD)

    def final_norm(B, tok0):
        ydst = O["y_p"][tok0:tok0 + B.ntok, :] if B.kind == "p" else O["y_s"]
        ydst = ydst.rearrange("(i p) d -> p i d", p=128)
        for i in range(B.nt):
            st = stat[i % 2]
            k.act(junk, h[:, i, :], AF.Square, accum_out=st[:, 0:1])
            rstd_of(st[:, 0:1], st[:, 1:2], 1.0 / D, RMS_EPS)
            k.stt(h[:, i, :], h[:, i, :], st[:, 1:2], gfin, MUL, MUL)
        k.dma("sp", ydst, h[:, 0:B.nt, :], is_output=True)

    stop_after = dbg.get("stop")
    for bi, B in enumerate(blocks):
        if dbg.get("blocks") is not None and bi not in dbg["blocks"]:
            continue
        tok0 = B.idx * TBP
        src = (I["xp"][tok0:tok0 + B.ntok, :] if B.kind == "p" else I["xs"]).rearrange("(i p) d -> p i d", p=128)
        k.dma("sp", h[:, 0:B.nt, :], src)
        stages = dbg.get("stages", "nCBAoFP")
        for l in range(NL):
            if dbg.get("layers") is not None and l not in dbg["layers"]:
                continue
            wbase = (bi * NL + l) * NCH
            if "n" in stages:
                norm_T(B, l, "g_mix")
            k.barrier()
            if "C" in stages:
                mixer_c(B, l, wbase)
            if "B" in stages:
                mixer_b(B, l, wbase)
            if "A" in stages:
                if B.kind == "p" and not dbg.get("nopipe"):
                    mixer_a_prompt(B, l, wbase)
                else:
                    mixer_a(B, l, wbase)
            dump(f"mixed{l}", mixed[:, :, 0:B.ntok], [128, 8, B.ntok])
            if "o" in stages:
                if "F" in stages:
                    ffn_prepare()
                outproj(B, l, wbase)
            dump(f"h_mix{l}", h[:, 0:B.nt, :], [128, B.nt, D])
            if "F" in stages:
                ffn(B, l, wbase)
            dump(f"h_ffn{l}", h[:, 0:B.nt, :], [128, B.nt, D])
            if "P" in stages:
                ple(B, l, wbase, tok0)
            dump(f"h_ple{l}", h[:, 0:B.nt, :], [128, B.nt, D])
        final_norm(B, tok0)
    k.finish()
    nc._kb = (k, I, O, DBG)
    return nc


_CACHE = {}


def kernel(**inputs):
    dbg = dict(DEBUG)
    sh = prep_shared(inputs)
    in_maps = []
    for c in range(NCORES):
        d = dict(sh)
        d.update(prep_core(inputs, c))
        in_maps.append(d)
    nc = build_program(dbg)
    res = run_bass_kernel_spmd(nc, in_maps, core_ids=list(range(NCORES)))
    R = res.results
    if dbg:
        _CACHE["res"] = R
    g = lambda nm: np.stack([np.asarray(R[c][nm], np.float32) for c in range(NCORES)], 0)
    y_prompt = g("y_p")
    y_sample = g("y_s").reshape(DEC_B, DEC_T, D)
    shift_p = np.transpose(g("o_shift_p"), (1, 0, 3, 2)).reshape(NL, NCORES, A_COLS)
    wkv_p = np.transpose(g("o_wkv_p"), (1, 0, 2, 4, 3))
    conv_p = np.transpose(g("o_conv_p"), (1, 0, 4, 3, 2)).reshape(NL, NCORES, 3, 256)
    lru_p = np.transpose(g("o_lru_p"), (1, 0, 3, 2)).reshape(NL, NCORES, 256)
    s5re_p = np.transpose(g("o_s5re_p"), (1, 0, 3, 2)).reshape(NL, NCORES, 16, 64)
    s5im_p = np.transpose(g("o_s5im_p"), (1, 0, 3, 2)).reshape(NL, NCORES, 16, 64)
    t_ = g("o_shift_s")
    shift_s = np.transpose(t_, (1, 0, 4, 3, 2)).reshape(NL, DEC_B, A_COLS)
    wkv_s = np.transpose(g("o_wkv_s"), (1, 0, 2, 3)).reshape(NL, DEC_B, 8, 64, 64)
    t_ = g("o_conv_s")
    conv_s = np.transpose(t_, (1, 0, 4, 5, 3, 2)).reshape(NL, DEC_B, 3, 256)
    t_ = g("o_lru_s")
    lru_s = np.transpose(t_, (1, 0, 4, 3, 2)).reshape(NL, DEC_B, 256)
    t_ = g("o_s5re_s")
    s5re_s = np.transpose(t_, (1, 0, 4, 3, 2)).reshape(NL, DEC_B, 16, 64)
    t_ = g("o_s5im_s")
    s5im_s = np.transpose(t_, (1, 0, 4, 3, 2)).reshape(NL, DEC_B, 16, 64)
    outs = (y_prompt, y_sample, shift_p, wkv_p, conv_p, lru_p, s5re_p, s5im_p,
            shift_s, wkv_s, conv_s, lru_s, s5re_s, s5im_s)
    return tuple(np.ascontiguousarray(o, dtype=np.float32) for o in outs)
```

```python
import math
import numpy as np
from collections import defaultdict
import concourse.bass as bass
import concourse.mybir as mybir
from concourse.bass_utils import run_bass_kernel_spmd

F32 = mybir.dt.float32
BF16 = mybir.dt.bfloat16
ALU = mybir.AluOpType
AF = mybir.ActivationFunctionType
AX = mybir.AxisListType

NCORES = 8
D = 1024
NL = 2
SEQ = 2048
TBP = 512
NPB = SEQ // TBP
DEC_B = 128
DEC_T = 8
SB_PER = DEC_B // NCORES
A_COLS = 1792
IN_COLS = 2560
DFF = 2816
NF = DFF // 128
COL_ORDER = [18, 19, 14, 15, 16, 17, 12, 13, 0, 4, 8, 1, 5, 9, 2, 6, 10, 3, 7, 11]
RMS_EPS = 1e-6
GN_EPS = 64e-5
DEBUG = {}

SAME_ENGINE_SYNC = True
NDSEM = 8
NW = 4

PC = {}
_off = 0
for _n, _c in [("mu", 14), ("w0", 4), ("a0", 4), ("k_k", 4), ("k_a", 4), ("r_k", 4), ("lnx_w", 4),
               ("lnx_b", 4), ("conv_w", 8), ("conv_b", 2), ("b_rg", 2), ("b_ig", 2), ("lam", 2),
               ("g_out_b", 2), ("s5_d", 2), ("b_glu", 2), ("g_out_c", 2), ("s5_lr", 8), ("s5_li", 8),
               ("s5_ldt", 8), ("g_mix", 8), ("g_ffn", 8), ("g_ple", 8)]:
    PC[_n] = _off
    _off += _c
NPAR = _off

CC = {}
_off = 0
for _n, _c in [("ident", 128), ("onesblk", 128), ("avgblk", 128), ("ones", 128), ("maskA", 192),
               ("maskSL", 128), ("cumcoef", 256), ("startmask", 128)]:
    CC[_n] = _off
    _off += _c
NCONST = _off


def _cols(v, n):
    return np.ascontiguousarray(np.asarray(v, np.float32).reshape(n, 128).T)


def make_consts():
    c = np.zeros((128, NCONST), np.float32)
    p = np.arange(128)[:, None]
    q = np.arange(128)[None, :]
    same = (p // 64) == (q // 64)
    c[:, CC["ident"]:CC["ident"] + 128] = np.eye(128)
    c[:, CC["onesblk"]:CC["onesblk"] + 128] = same
    c[:, CC["avgblk"]:CC["avgblk"] + 128] = same / 64.0
    c[:, CC["ones"]:CC["ones"] + 128] = 1.0
    c[:, CC["maskA"]:CC["maskA"] + 128] = same & ((p % 64) < (q % 64))
    q64 = np.arange(64)[None, :]
    c[:, CC["maskA"] + 128:CC["maskA"] + 192] = (p % 64) <= q64
    c[:, CC["maskSL"]:CC["maskSL"] + 128] = same & ((p % 64) > (q % 64))
    cc = np.ones((128, 256), np.float32)
    cc[:, 0::64] = 0.0
    c[:, CC["cumcoef"]:CC["cumcoef"] + 256] = cc
    sm = np.ones((128, 128), np.float32)
    sm[:, 0::8] = 0.0
    c[:, CC["startmask"]:CC["startmask"] + 128] = sm
    return c


def pack_params(inp, l):
    P = np.zeros((128, NPAR), np.float32)

    def put(name, arr, n):
        P[:, PC[name]:PC[name] + n] = _cols(arr, n)
    put("mu", inp["mu_a"][l], 14)
    for nm in ("w0", "a0", "k_k", "k_a", "lnx_w", "lnx_b"):
        put(nm, inp[nm][l], 4)
    put("r_k", inp["r_k"][l].reshape(512), 4)
    cw = np.asarray(inp["conv_w"][l], np.float32)
    for ct in range(2):
        for j in range(4):
            P[:, PC["conv_w"] + ct * 4 + j] = cw[j, ct * 128:(ct + 1) * 128]
    for nm, src in (("conv_b", "conv_b"), ("b_rg", "b_rg"), ("b_ig", "b_ig"), ("lam", "lru_lambda"),
                    ("g_out_b", "g_out_b"), ("s5_d", "s5_d"), ("b_glu", "b_glu"), ("g_out_c", "g_out_c")):
        put(nm, inp[src][l], 2)
    put("s5_lr", inp["s5_lam_re"][l].reshape(1024), 8)
    put("s5_li", inp["s5_lam_im"][l].reshape(1024), 8)
    put("s5_ldt", np.repeat(np.asarray(inp["s5_log_dt"][l], np.float32), 64), 8)
    put("g_mix", inp["g_mix"][l], 8)
    put("g_ffn", inp["g_ffn"][l], 8)
    put("g_ple", inp["g_ple"][l], 8)
    return P


def prep_shared(inp):
    f = lambda a: np.ascontiguousarray(np.asarray(a, np.float32))
    sh = {}
    sh["consts"] = make_consts()
    sh["params"] = np.stack([pack_params(inp, l) for l in range(NL)], 0)
    order = np.concatenate([np.arange(t * 128, (t + 1) * 128) for t in COL_ORDER])
    sh["w_in"] = f(np.asarray(inp["w_in"])[:, :, order])
    sh["w_out"] = f(inp["w_out"])
    wu = np.asarray(inp["w_ffn_up"], np.float32)
    uo = []
    for p in range(NF // 2):
        uo.append(np.arange(256 * p, 256 * p + 256))
        uo.append(np.arange(DFF + 256 * p, DFF + 256 * p + 256))
    sh["w_up"] = f(wu[:, :, np.concatenate(uo)])
    sh["w_down"] = f(inp["w_ffn_down"])
    sh["w_ple"] = f(inp["w_ple"])
    sh["w_gate"] = f(inp["w_ple_gate"])
    sh["g_final"] = f(inp["g_final"]).reshape(1, D)
    sh["lora1"] = f(np.concatenate([np.asarray(inp["w_dec2"]), np.asarray(inp["w_a2"])], axis=1))
    sh["w_g2"] = f(inp["w_g2"])
    def bd(w):
        w = np.asarray(w, np.float32)
        o = np.zeros((NL, 2, 128, 128), np.float32)
        for ct in range(2):
            for h in range(2):
                o[:, ct, h * 64:(h + 1) * 64, h * 64:(h + 1) * 64] = w[:, ct * 2 + h]
        return o
    sh["w_rg"] = bd(inp["w_rg"])
    sh["w_ig"] = bd(inp["w_ig"])
    sh["w_glu"] = f(inp["w_glu"])
    def bpad(b):
        b = np.asarray(b, np.float32)
        o = np.zeros((NL, 8, 128, 128), np.float32)
        for s in range(8):
            for g2 in range(2):
                g = 2 * s + g2
                k0 = (g % 8) * 16
                o[:, s, k0:k0 + 16, g2 * 64:(g2 + 1) * 64] = np.transpose(b[:, g], (0, 2, 1))
        return o
    def cpad(c):
        c = np.asarray(c, np.float32)
        o = np.zeros((NL, 8, 128, 128), np.float32)
        for s in range(8):
            for g2 in range(2):
                g = 2 * s + g2
                m0 = (g % 8) * 16
                o[:, s, g2 * 64:(g2 + 1) * 64, m0:m0 + 16] = np.transpose(c[:, g], (0, 2, 1))
        return o
    s5w = np.stack([bpad(inp["s5_b_re"]), bpad(inp["s5_b_im"]), cpad(inp["s5_c_re"]), cpad(inp["s5_c_im"])], 1)
    sh["s5w"] = f(np.transpose(s5w, (0, 1, 3, 2, 4)))
    return sh


def prep_core(inp, c):
    f = lambda a: np.ascontiguousarray(np.asarray(a, np.float32))
    b0, b1 = c * SB_PER, (c + 1) * SB_PER
    d = {}
    d["xp"] = f(inp["x_prompt"][c])
    d["xs"] = f(np.asarray(inp["x_sample"])[b0:b1].reshape(SB_PER * DEC_T, D))
    d["pp"] = f(np.asarray(inp["p_prompt"])[:, c])
    d["psm"] = f(np.asarray(inp["p_sample"])[:, b0:b1].reshape(NL, SB_PER * DEC_T, 256))
    ss = np.asarray(inp["state_shift"], np.float32)[:, b0:b1]
    d["st_shift"] = f(np.transpose(ss.reshape(NL, SB_PER, 14, 128), (0, 3, 2, 1)))
    d["st_wkv"] = f(np.asarray(inp["state_wkv"])[:, b0:b1].reshape(NL, SB_PER * 8, 4096))
    sc = np.asarray(inp["state_conv"], np.float32)[:, b0:b1]
    d["st_conv"] = f(np.transpose(sc.reshape(NL, SB_PER, 3, 2, 128), (0, 4, 3, 1, 2)))
    sl = np.asarray(inp["state_lru"], np.float32)[:, b0:b1]
    d["st_lru"] = f(np.transpose(sl.reshape(NL, SB_PER, 2, 128), (0, 3, 2, 1)))
    for nm, src in (("st_s5re", "state_s5_re"), ("st_s5im", "state_s5_im")):
        s5 = np.asarray(inp[src], np.float32)[:, b0:b1].reshape(NL, SB_PER, 8, 128)
        d[nm] = f(np.transpose(s5, (0, 3, 2, 1)))
    return d


class V:
    __slots__ = ("t", "ap")

    def __init__(self, t, ap):
        self.t = t
        self.ap = ap

    def __getitem__(self, k):
        return V(self.t, self.ap[k])

    def re(self, s, **kw):
        return V(self.t, self.ap.rearrange(s, **kw))

    def bc(self, shape):
        return V(self.t, self.ap.to_broadcast(shape))

    def un(self, axis):
        return V(self.t, self.ap.unsqueeze(axis))


class T:
    def __init__(self, ap, name="", scr=False):
        self.ap = ap
        self.name = name
        self.w = None
        self.r = []
        self.scr = scr
        self.psum = False

    def __getitem__(self, k):
        return V(self, self.ap[k])

    def v(self):
        return V(self, self.ap)

    def re(self, s, **kw):
        return V(self, self.ap.rearrange(s, **kw))

    def un(self, axis):
        return V(self, self.ap.unsqueeze(axis))

    def bc(self, shape):
        return V(self, self.ap.to_broadcast(shape))


class KB:
    def __init__(self, nc):
        self.nc = nc
        self.eng = {"pe": nc.tensor, "act": nc.scalar, "dve": nc.vector, "pool": nc.gpsimd, "sp": nc.sync}
        self.sems = {}
        self.semval = defaultdict(int)
        for e in self.eng:
            self.sems[e] = nc.alloc_semaphore(name=e + "_c")
        for q in ("sp", "act", "pool"):
            for i in range(NDSEM):
                self.sems[(q, i)] = nc.alloc_semaphore(name=f"{q}_d{i}")
        self.dma_i = defaultdict(int)
        self.waited = {e: defaultdict(int) for e in self.eng}
        self.ninst = defaultdict(int)
        self.out_dma = []
        self.scr_dma = []
        self._n = 0
        self.rr_i = 0
        self.rec = None

    def sb(self, shape, dt=F32, name=None):
        self._n += 1
        name = "s_" + (name or f"sb{self._n}")
        h = self.nc.alloc_sbuf_tensor(name, list(shape), dt)
        return T(h.ap(), name)

    def ps(self, shape, dt=F32, name=None):
        self._n += 1
        name = "p_" + (name or f"ps{self._n}")
        h = self.nc.alloc_psum_tensor(name, list(shape), dt)
        t = T(h.ap(), name)
        t.psum = True
        return t

    def _wait(self, e, deps):
        eng = self.eng[e]
        best = {}
        for (sk, val, de) in deps:
            if de == e and (e == "pe" or not SAME_ENGINE_SYNC):
                continue
            if val > best.get(sk, 0):
                best[sk] = val
        for sk, val in best.items():
            if self.waited[e][sk] < val:
                eng.wait_ge(self.sems[sk], val)
                self.waited[e][sk] = val
                self.ninst[e] += 1

    @staticmethod
    def _deps(reads, writes, e=None):
        deps = []
        for t in reads:
            if t.w is not None:
                deps.append(t.w)
            if t.psum:
                deps.extend(x for x in t.r if x[2] != e)
        for t in writes:
            if t.w is not None:
                deps.append(t.w)
            deps.extend(t.r)
        return deps

    @staticmethod
    def _commit(reads, writes, tok):
        for t in reads:
            if len(t.r) > 6:
                best = {}
                for x in t.r:
                    if x[1] > best.get(x[0], (None, 0, None))[1]:
                        best[x[0]] = x
                t.r = list(best.values())
            t.r.append(tok)
        for t in writes:
            t.w = tok
            t.r = []

    def op(self, e, meth, *args, reads=(), writes=(), **kw):
        if self.rec is not None:
            self.rec.append(lambda: self._op(e, meth, *args, reads=reads, writes=writes, **kw))
            return None
        return self._op(e, meth, *args, reads=reads, writes=writes, **kw)

    def _op(self, e, meth, *args, reads=(), writes=(), **kw):
        rd = list(reads)
        wr = list(writes)
        kw2 = {}
        for kk_, v in kw.items():
            if isinstance(v, V):
                (wr if kk_ in ("out", "accum_out") else rd).append(v.t)
                kw2[kk_] = v.ap
            elif isinstance(v, T):
                (wr if kk_ in ("out", "accum_out") else rd).append(v)
                kw2[kk_] = v.ap
            else:
                kw2[kk_] = v
        self._wait(e, self._deps(rd, wr, e))
        inst = getattr(self.eng[e], meth)(*args, **kw2)
        self.semval[e] += 1
        inst.then_inc(self.sems[e], 1)
        self.ninst[e] += 1
        self._commit(rd, wr, (e, self.semval[e], e))
        return inst

    def mm(self, out, lhsT, rhs, start=True, stop=True):
        return self.op("pe", "matmul", out=out, lhsT=lhsT, rhs=rhs, start=start, stop=stop)

    def tr(self, out, in_, ident):
        return self.op("pe", "transpose", out=out, in_=in_, identity=ident)

    def dma(self, q, out, in_, is_output=False, **kw):
        if self.rec is not None:
            self.rec.append(lambda: self._dma(q, out, in_, is_output=is_output, **kw))
            return None
        return self._dma(q, out, in_, is_output=is_output, **kw)

    def start_rec(self):
        assert self.rec is None
        self.rec = []

    def stop_rec(self):
        r = self.rec
        self.rec = None
        return r

    @staticmethod
    def interleave(*lists):
        lists = [x for x in lists if x]
        pos = [0] * len(lists)
        while True:
            best, bf = -1, 2.0
            for i, x in enumerate(lists):
                if pos[i] < len(x):
                    f = pos[i] / len(x)
                    if f < bf:
                        best, bf = i, f
            if best < 0:
                break
            lists[best][pos[best]]()
            pos[best] += 1

    def _dma(self, q, out, in_, is_output=False, **kw):
        rd, wr = [], []
        if isinstance(in_, (V, T)):
            rd.append(in_.t if isinstance(in_, V) else in_)
            in_ap = in_.ap
        else:
            in_ap = in_
        if isinstance(out, (V, T)):
            wr.append(out.t if isinstance(out, V) else out)
            out_ap = out.ap
        else:
            out_ap = out
        i = self.dma_i[q] % NDSEM
        self.dma_i[q] += 1
        sk = (q, i)
        deps = self._deps(rd, wr)
        if self.semval[sk] > 0:
            deps.append((sk, self.semval[sk], "dma"))
        self._wait(q, deps)
        inst = self.eng[q].dma_start(out=out_ap, in_=in_ap, allow_slow_non_contiguous=True, **kw)
        self.semval[sk] += 16
        inst.then_inc(self.sems[sk], 16)
        self.ninst[q] += 1
        tok = (sk, self.semval[sk], "dma")
        self._commit(rd, wr, tok)
        if is_output:
            self.out_dma.append(tok)
        if any(t.scr for t in rd + wr):
            self.scr_dma.append(tok)
        return inst

    def copy(self, out, in_, eng=None):
        if eng is None:
            self.rr_i += 1
            eng = "dve" if self.rr_i % 2 else "act"
        if eng == "act":
            return self.op("act", "copy", out=out, in_=in_)
        return self.op(eng, "tensor_copy", out=out, in_=in_)

    def tt(self, out, in0, in1, op, eng="dve"):
        return self.op(eng, "tensor_tensor", out=out, in0=in0, in1=in1, op=op)

    def ts(self, out, in0, s1, op0, s2=None, op1=None, eng="dve"):
        if op1 is None:
            return self.op(eng, "tensor_scalar", out=out, in0=in0, scalar1=s1, scalar2=None, op0=op0)
        return self.op(eng, "tensor_scalar", out=out, in0=in0, scalar1=s1, scalar2=s2, op0=op0, op1=op1)

    def stt(self, out, in0, scalar, in1, op0, op1, eng="dve"):
        return self.op(eng, "scalar_tensor_tensor", out=out, in0=in0, scalar=scalar, in1=in1, op0=op0, op1=op1)

    def act(self, out, in_, func, bias=None, scale=None, accum_out=None):
        kw = {}
        if bias is not None:
            kw["bias"] = bias
        if scale is not None:
            kw["scale"] = scale
        if accum_out is not None:
            kw["accum_out"] = accum_out
        return self.op("act", "activation", out=out, in_=in_, func=func, **kw)

    def memset(self, view, val, eng="dve"):
        if isinstance(view, T):
            view = view.v()
        return self.op(eng, "memset", view.ap, val, writes=[view.t])

    def barrier(self):
        assert self.rec is None
        toks = []
        for e in ("pe", "act", "dve", "pool"):
            if self.semval[e] > 0:
                toks.append((e, self.semval[e], "x"))
        toks.extend((sk, v, "dma") for (sk, v, _) in self.scr_dma)
        self.scr_dma = []
        for e in ("pe", "act", "dve", "pool", "sp"):
            self._wait(e, [(sk, v, "x") for (sk, v, _) in toks])

    def finish(self):
        deps = list(self.out_dma)
        for q in ("sp", "act", "pool"):
            for i in range(NDSEM):
                sk = (q, i)
                if self.semval[sk] > 0:
                    deps.append((sk, self.semval[sk], "dma"))
        for e in ("pe", "act", "dve", "pool"):
            if self.semval[e] > 0:
                deps.append((e, self.semval[e], "x"))
        self._wait("sp", deps)


class Scratch:
    def __init__(self, k, nbytes):
        self.k = k
        self.words = nbytes // 4
        self.base = k.nc.alloc_sbuf_tensor("scratch", [128, self.words], F32).ap()
        self.off = 0

    def reset(self):
        self.off = 0

    def alloc(self, shape, dt=F32, name=""):
        n = 1
        for s in shape[1:]:
            n *= s
        if dt == F32:
            w = n
        else:
            w = (n + 1) // 2
        w = (w + 7) // 8 * 8
        assert self.off + w <= self.words, f"scratch overflow {name} {self.off + w} > {self.words}"
        ap = self.base[:, self.off:self.off + w]
        self.off += w
        if dt != F32:
            ap = ap.bitcast(dt)
        ap = ap[:, 0:n]
        if len(shape) == 3:
            ap = ap.rearrange("p (a b) -> p a b", a=shape[1])
        elif len(shape) == 4:
            ap = ap.rearrange("p (a b c) -> p a b c", a=shape[1], b=shape[2])
        return T(ap, name, scr=True)


class Blk:
    def __init__(self, kind, idx):
        self.kind = kind
        self.idx = idx
        if kind == "p":
            self.nseq, self.T, self.ntok, self.nt = 1, TBP, TBP, TBP // 128
        else:
            self.nseq, self.T, self.ntok, self.nt = SB_PER, DEC_T, SB_PER * DEC_T, 1
        self.first = (kind == "s") or idx == 0
        self.last = (kind == "s") or idx == NPB - 1


def build_program(dbg=None):
    dbg = dbg or {}
    nc = bass.Bass("TRN2", target_bir_lowering=False)

    def din(name, shape):
        return nc.dram_tensor(name, list(shape), F32, kind="ExternalInput").ap()

    def dout(name, shape):
        return nc.dram_tensor(name, list(shape), F32, kind="ExternalOutput").ap()

    I = {}
    for nm, shp in [("xp", [SEQ, D]), ("xs", [128, D]), ("pp", [NL, SEQ, 256]), ("psm", [NL, 128, 256]),
                    ("st_shift", [NL, 128, 14, 16]), ("st_wkv", [NL, 128, 4096]), ("st_conv", [NL, 128, 2, 16, 3]),
                    ("st_lru", [NL, 128, 2, 16]), ("st_s5re", [NL, 128, 8, 16]), ("st_s5im", [NL, 128, 8, 16]),
                    ("consts", [128, NCONST]), ("params", [NL, 128, NPAR]), ("w_in", [NL, D, IN_COLS]),
                    ("w_out", [NL, D, D]), ("w_up", [NL, D, 2 * DFF]), ("w_down", [NL, DFF, D]),
                    ("w_ple", [NL, 256, D]), ("w_gate", [NL, D, D]), ("g_final", [1, D]),
                    ("lora1", [NL, 128, 512]), ("w_g2", [NL, 128, 512]), ("w_rg", [NL, 2, 128, 128]),
                    ("w_ig", [NL, 2, 128, 128]), ("w_glu", [NL, 256, 256]), ("s5w", [NL, 4, 128, 8, 128])]:
        I[nm] = din(nm, shp)
    O = {}
    for nm, shp in [("y_p", [SEQ, D]), ("y_s", [128, D]),
                    ("o_shift_p", [NL, 128, 14]), ("o_wkv_p", [NL, 8, 64, 64]), ("o_conv_p", [NL, 128, 2, 3]),
                    ("o_lru_p", [NL, 128, 2]), ("o_s5re_p", [NL, 128, 8]), ("o_s5im_p", [NL, 128, 8]),
                    ("o_shift_s", [NL, 128, 14, 16]), ("o_wkv_s", [NL, 128, 4096]), ("o_conv_s", [NL, 128, 2, 16, 3]),
                    ("o_lru_s", [NL, 128, 2, 16]), ("o_s5re_s", [NL, 128, 8, 16]), ("o_s5im_s", [NL, 128, 8, 16])]:
        O[nm] = dout(nm, shp)
    DBG = {}

    k = KB(nc)
    MUL, ADD, SUB = ALU.mult, ALU.add, ALU.subtract

    def dump(name, view, shape):
        if name in dbg:
            DBG[name] = dout("dbg_" + name, shape)
            isbf = (view.ap if isinstance(view, (V, T)) else view).dtype == BF16
            k.dma("pool" if isbf else "sp", DBG[name], view, is_output=True)

    consts = k.sb([128, NCONST], F32, "consts")
    C = lambda nm, n: consts[:, CC[nm]:CC[nm] + n]
    ident = C("ident", 128)
    idb = k.sb([128, 128], BF16, "idb")
    par = [k.sb([128, NPAR], F32, f"par{l}") for l in range(NL)]
    P = lambda l, nm, i=0, n=1: par[l][:, PC[nm] + i:PC[nm] + i + n]
    lora1 = [k.sb([128, 512], F32, f"lora1_{l}") for l in range(NL)]
    wg2 = [k.sb([128, 512], F32, f"wg2_{l}") for l in range(NL)]
    wrg = [k.sb([128, 2, 128], F32, f"wrg{l}") for l in range(NL)]
    wig = [k.sb([128, 2, 128], F32, f"wig{l}") for l in range(NL)]
    wglu = [k.sb([128, 2, 256], F32, f"wglu{l}") for l in range(NL)]
    gfin = k.sb([128, D], F32, "gfin")
    s5d = [k.sb([128, 16, 8], F32, f"s5d{l}") for l in range(NL)]
    S5N = {n: i for i, n in enumerate(["dt", "mag", "ang", "ar", "ai", "nai", "cr", "ci", "c1", "s1", "t0", "t1", "Qr", "Qi", "nQi"])}
    S5 = lambda l, nm: s5d[l][:, S5N[nm], :]
    Tc = [k.sb([128, 8, 128], F32, f"Tc{l}") for l in range(NL)]
    Ts = [k.sb([128, 8, 128], F32, f"Ts{l}") for l in range(NL)]
    c8 = [k.sb([128, 2], F32, f"c8_{l}") for l in range(NL)]
    h = k.sb([128, 4, D], F32, "h")
    xn = [k.sb([128, D], BF16, f"xn{i}") for i in range(2)]
    junk = k.sb([128, D], BF16, "junk")
    stat = [k.sb([128, 4], F32, f"stat{i}") for i in range(2)]
    xnT = k.sb([128, 8, TBP], BF16, "xnT")
    wring = [k.sb([128, 4096], BF16, f"wring{i}") for i in range(NW)]
    mixed = k.sb([128, 8, TBP], BF16, "mixed")
    ptile = k.sb([128, 4, 256], F32, "ptile")
    pbf = k.sb([128, 4, 256], BF16, "pbf")
    pT = k.sb([128, 2, TBP], BF16, "pT")
    tmpt = [k.sb([128, 512], F32, f"tmpt{i}") for i in range(2)]
    wple_t = k.sb([128, 2, D], BF16, "wple_t")
    histA = [k.sb([128, 14], F32, f"histA{l}") for l in range(NL)]
    histB = [k.sb([128, 2, 3], F32, f"histB{l}") for l in range(NL)]
    lruc = [k.sb([128, 2], F32, f"lruc{l}") for l in range(NL)]
    s5cr = [k.sb([128, 8], F32, f"s5cr{l}") for l in range(NL)]
    s5ci = [k.sb([128, 8], F32, f"s5ci{l}") for l in range(NL)]
    Pbd = [[k.sb([128, 128], F32, f"Pbd{l}_{hp}") for hp in range(4)] for l in range(NL)]
    small = k.sb([128, 64], F32, "small")
    PF = [k.ps([128, 512], F32, f"PF{i}") for i in range(4)]
    PHb = [k.ps([128, 512], F32, f"PHb{i}") for i in range(2)]
    PT = [k.ps([128, 1024], BF16, f"PTb{i}") for i in range(2)]
    scr = Scratch(k, min(nc.sbuf_bytes_remaining - 1024, 77 * 1024))
    ps_i = [0]

    pf_pool = [[PF[0], PF[1], PF[2]]]

    def pf():
        ps_i[0] += 1
        return pf_pool[0][ps_i[0] % len(pf_pool[0])]
    ph_i = [0]
    PHpool = [PHb[0], PHb[1], PF[0], PF[1], PF[2]]

    ph_pool = [PHpool]

    def ph():
        ph_i[0] += 1
        return ph_pool[0][ph_i[0] % len(ph_pool[0])]

    blocks = [Blk("p", i) for i in range(NPB)] + [Blk("s", 0)]
    sched = []
    NCH = 32
    for bi, B in enumerate(blocks):
        for l in range(NL):
            for c in range(5):
                sched.append((I["w_in"][l][:, c * 512:(c + 1) * 512].rearrange("(c p) n -> p c n", p=128), (8, 512)))
            for j in range(2):
                sched.append((I["w_out"][l][:, j * 512:(j + 1) * 512].rearrange("(c p) n -> p c n", p=128), (8, 512)))
            for p_ in range(11):
                sched.append((I["w_up"][l][:, p_ * 512:(p_ + 1) * 512].rearrange("(c p) n -> p c n", p=128), (8, 512)))
            for j in range(2):
                for g in range(6):
                    nfc = 4 if g < 5 else 2
                    sched.append((I["w_down"][l][g * 512:g * 512 + nfc * 128, j * 512:(j + 1) * 512]
                                  .rearrange("(c p) n -> p c n", p=128), (nfc, 512)))
            for j in range(2):
                sched.append((I["w_gate"][l][:, j * 512:(j + 1) * 512].rearrange("(c p) n -> p c n", p=128), (8, 512)))
    issued = [0]

    def wget(idx):
        while issued[0] < min(idx + NW, len(sched)):
            i = issued[0]
            src, (a, b) = sched[i]
            dst = wring[i % NW][:, 0:a * b].re("p (a b) -> p a b", a=a)
            k.dma("pool", dst, src)
            issued[0] += 1
        a, b = sched[idx][1]
        return wring[idx % NW][:, 0:a * b].re("p (a b) -> p a b", a=a)

    k.dma("sp", consts, I["consts"])
    for l in range(NL):
        k.dma("sp", par[l], I["params"][l])
    k.dma("sp", gfin, I["g_final"].to_broadcast([128, D]))
    for l in range(NL):
        k.dma("sp", lora1[l], I["lora1"][l])
        k.dma("sp", wg2[l], I["w_g2"][l])
        k.dma("sp", wrg[l], I["w_rg"][l].rearrange("c p n -> p c n"))
        k.dma("sp", wig[l], I["w_ig"][l].rearrange("c p n -> p c n"))
        k.dma("sp", wglu[l], I["w_glu"][l].rearrange("(c p) n -> p c n", p=128))
    k.copy(idb, ident, "dve")
    wget(0)
    for l in range(NL):
        k.memset(histA[l], 0.0)
        k.memset(histB[l], 0.0)
        k.memset(lruc[l], 0.0)
        k.memset(s5cr[l], 0.0)
        k.memset(s5ci[l], 0.0)
        for hp in range(4):
            k.memset(Pbd[l][hp], 0.0)

    def range_reduce(v):
        tq = S5(0, "t1") if False else small[:, 0:8]
        for m in range(6):
            k.ts(tq, v, math.pi, ALU.is_ge, -2 * math.pi, MUL)
            k.tt(v, v, tq, ADD)

    for l in range(NL):
        lr = P(l, "s5_lr", 0, 8)
        li = P(l, "s5_li", 0, 8)
        k.act(S5(l, "dt"), P(l, "s5_ldt", 0, 8), AF.Exp)
        k.tt(S5(l, "t0"), lr, S5(l, "dt"), MUL)
        k.act(S5(l, "mag"), S5(l, "t0"), AF.Exp)
        k.tt(S5(l, "ang"), li, S5(l, "dt"), MUL)
        k.copy(S5(l, "t0"), S5(l, "ang"), "dve")
        range_reduce(S5(l, "t0"))
        k.act(S5(l, "s1"), S5(l, "t0"), AF.Sin)
        k.ts(S5(l, "t0"), S5(l, "ang"), math.pi / 2, ADD)
        range_reduce(S5(l, "t0"))
        k.act(S5(l, "c1"), S5(l, "t0"), AF.Sin)
        k.tt(S5(l, "ar"), S5(l, "mag"), S5(l, "c1"), MUL)
        k.tt(S5(l, "ai"), S5(l, "mag"), S5(l, "s1"), MUL)
        k.ts(S5(l, "nai"), S5(l, "ai"), -1.0, MUL)
        k.tt(S5(l, "t0"), lr, lr, MUL)
        k.tt(S5(l, "t1"), li, li, MUL)
        k.tt(S5(l, "t0"), S5(l, "t0"), S5(l, "t1"), ADD)
        k.op("dve", "reciprocal", out=S5(l, "t0"), in_=S5(l, "t0"))
        k.ts(S5(l, "t1"), S5(l, "ar"), -1.0, ADD)
        k.tt(S5(l, "cr"), S5(l, "t1"), lr, MUL)
        k.tt(S5(l, "ci"), S5(l, "ai"), li, MUL)
        k.tt(S5(l, "cr"), S5(l, "cr"), S5(l, "ci"), ADD)
        k.tt(S5(l, "cr"), S5(l, "cr"), S5(l, "t0"), MUL)
        k.tt(S5(l, "ci"), S5(l, "ai"), lr, MUL)
        k.tt(S5(l, "t1"), S5(l, "t1"), li, MUL)
        k.tt(S5(l, "ci"), S5(l, "ci"), S5(l, "t1"), SUB)
        k.tt(S5(l, "ci"), S5(l, "ci"), S5(l, "t0"), MUL)
        k.memset(Tc[l][:, :, 0:1], 1.0)
        k.memset(Ts[l][:, :, 0:1], 0.0)
        cn, sn = S5(l, "c1"), S5(l, "s1")
        ta = tmpt[0][:, 0:8 * 64].re("p (a b) -> p a b", a=8)
        tb = tmpt[1][:, 0:8 * 64].re("p (a b) -> p a b", a=8)
        n = 1
        while n < 128:
            cb = cn.un(2).bc([128, 8, n])
            sb_ = sn.un(2).bc([128, 8, n])
            k.tt(ta[:, :, 0:n], Ts[l][:, :, 0:n], sb_, MUL)
            k.tt(tb[:, :, 0:n], Tc[l][:, :, 0:n], sb_, MUL)
            k.tt(Tc[l][:, :, n:2 * n], Tc[l][:, :, 0:n], cb, MUL)
            k.tt(Tc[l][:, :, n:2 * n], Tc[l][:, :, n:2 * n], ta[:, :, 0:n], SUB)
            k.tt(Ts[l][:, :, n:2 * n], Ts[l][:, :, 0:n], cb, MUL)
            k.tt(Ts[l][:, :, n:2 * n], Ts[l][:, :, n:2 * n], tb[:, :, 0:n], ADD)
            k.tt(S5(l, "t0"), cn, cn, MUL)
            k.tt(S5(l, "t1"), sn, sn, MUL)
            k.tt(sn, cn, sn, MUL)
            k.ts(sn, sn, 2.0, MUL)
            k.tt(cn, S5(l, "t0"), S5(l, "t1"), SUB)
            n *= 2
        c127, s127 = Tc[l][:, :, 127], Ts[l][:, :, 127]
        k.tt(S5(l, "Qr"), S5(l, "ar"), c127, MUL)
        k.tt(S5(l, "t0"), S5(l, "ai"), s127, MUL)
        k.tt(S5(l, "Qr"), S5(l, "Qr"), S5(l, "t0"), SUB)
        k.tt(S5(l, "Qi"), S5(l, "ar"), s127, MUL)
        k.tt(S5(l, "t0"), S5(l, "ai"), c127, MUL)
        k.tt(S5(l, "Qi"), S5(l, "Qi"), S5(l, "t0"), ADD)
        k.ts(S5(l, "nQi"), S5(l, "Qi"), -1.0, MUL)
        k.act(c8[l], P(l, "lam", 0, 2), AF.Exp, scale=-1.0)
        k.act(c8[l], c8[l], AF.Ln, bias=1.0)
        k.ts(c8[l], c8[l], -8.0, MUL)

    def rstd_of(ss_view, out_view, scale, eps):
        k.ts(out_view, ss_view, scale, MUL, eps, ADD)
        k.act(out_view, out_view, AF.Ln)
        k.act(out_view, out_view, AF.Exp, scale=-0.5)

    def norm_T(B, l, gname):
        for i in range(B.nt):
            st = stat[i % 2]
            xb = xn[i % 2]
            k.act(junk, h[:, i, :], AF.Square, accum_out=st[:, 0:1])
            rstd_of(st[:, 0:1], st[:, 1:2], 1.0 / D, RMS_EPS)
            k.act(xb, h[:, i, :], AF.Copy, scale=st[:, 1:2])
            for half in range(2):
                pt = PT[half]
                for c4 in range(4):
                    c = half * 4 + c4
                    k.tr(pt[:, c4 * 128:(c4 + 1) * 128], xb[:, c * 128:(c + 1) * 128], idb)
                for c4 in range(4):
                    c = half * 4 + c4
                    dst = xnT[:, c, i * 128:(i + 1) * 128]
                    if half == 0:
                        k.act(dst, pt[:, c4 * 128:(c4 + 1) * 128], AF.Copy, scale=P(l, gname, c))
                    else:
                        k.ts(dst, pt[:, c4 * 128:(c4 + 1) * 128], P(l, gname, c), MUL)

    def inproj(B, l, wbase, tile, evac):
        pos = COL_ORDER.index(tile)
        wc = wget(wbase + pos // 4)
        w0_ = (pos % 4) * 128
        ps = pf()
        for c in range(8):
            k.mm(ps[:, 0:B.ntok], wc[:, c, w0_:w0_ + 128], xnT[:, c, 0:B.ntok], start=(c == 0), stop=(c == 7))
        evac(ps[:, 0:B.ntok])

    def seq3(v, B):
        return v.re("p (s t) -> p s t", t=B.T)

    def rms_pair(B, l, ytiles, gname, mix0):
        nt_ = B.ntok
        ps = pf()
        for ct in range(2):
            sq = tmpt[ct]
            k.act(sq[:, 0:nt_], ytiles[ct], AF.Square)
            k.mm(ps[:, 0:nt_], C("ones", 128), sq[:, 0:nt_], start=(ct == 0), stop=(ct == 1))
        rs = tmpt[0]
        rstd_of(ps[:, 0:nt_], rs[:, 0:nt_], 1.0 / 256, RMS_EPS)
        for ct in range(2):
            k.stt(mixed[:, mix0 + ct, 0:nt_], ytiles[ct], P(l, gname, ct), rs[:, 0:nt_], MUL, MUL)

    small2 = k.sb([128, 64], F32, "small2")

    def mixer_c(B, l, wbase):
        nt_ = B.ntok
        scr.reset()
        s5w = [scr.alloc([128, 8, 128], F32, f"s5w{i}") for i in range(4)]
        u = scr.alloc([128, 2, TBP], F32, "u")
        bufs = []
        for si in range(2):
            bufs.append([scr.alloc([128, TBP], F32, f"{nm}{si}") for nm in ("t1", "t2", "bur", "bui", "gr", "gi")])
        yc = scr.alloc([128, 2, TBP], F32, "yc")
        z = scr.alloc([128, 2, TBP], F32, "z")
        coefs = [scr.alloc([128, 128], F32, f"coef{si}") for si in range(2)]
        st_r = scr.alloc([128, 8, 16], F32, "st_r")
        st_i = scr.alloc([128, 8, 16], F32, "st_i")
        so_r = scr.alloc([128, 8, 16], F32, "so_r")
        so_i = scr.alloc([128, 8, 16], F32, "so_i")
        for kind in range(4):
            k.dma("sp", s5w[kind], I["s5w"][l, kind])
        k.ts(s5w[3], s5w[3], -1.0, MUL)
        if B.kind == "s":
            k.dma("sp", st_r, I["st_s5re"][l])
            k.dma("sp", st_i, I["st_s5im"][l])
        for ct in range(2):
            inproj(B, l, wbase, 18 + ct, lambda ps, ct=ct: k.copy(u[:, ct, 0:nt_], ps))
        dump(f"u{l}", u[:, :, 0:nt_], [128, 2, nt_])
        if B.kind == "p":
            nseg, sl = 4, 128
        else:
            nseg, sl = 16, 8
        v3 = lambda T_: T_[:, 0:nt_].re("p (s t) -> p s t", t=sl)
        Yps = [PF[0], PF[1]]
        banks = [(PF[2], PF[3]), (PHb[0], PHb[1])]
        smalls = [small, small2]

        def tile_body(s, si):
            t1, t2, bur, bui, gr, gi = bufs[si]
            bre, bim = banks[si]
            sm = smalls[si]
            coef = coefs[si]
            ct = s // 4
            cr_, ci_ = S5(l, "cr")[:, s:s + 1], S5(l, "ci")[:, s:s + 1]
            ar_, ai_, nai_ = S5(l, "ar")[:, s:s + 1], S5(l, "ai")[:, s:s + 1], S5(l, "nai")[:, s:s + 1]
            Qr_, Qi_, nQi_ = S5(l, "Qr")[:, s:s + 1], S5(l, "Qi")[:, s:s + 1], S5(l, "nQi")[:, s:s + 1]
            k.mm(bre[:, 0:nt_], s5w[0][:, s, :], u[:, ct, 0:nt_])
            k.mm(bim[:, 0:nt_], s5w[1][:, s, :], u[:, ct, 0:nt_])
            k.act(t1[:, 0:nt_], bim[:, 0:nt_], AF.Copy, scale=ci_)
            k.stt(bur[:, 0:nt_], bre[:, 0:nt_], cr_, t1[:, 0:nt_], MUL, SUB)
            k.act(t2[:, 0:nt_], bre[:, 0:nt_], AF.Copy, scale=ci_)
            k.stt(bui[:, 0:nt_], bim[:, 0:nt_], cr_, t2[:, 0:nt_], MUL, ADD)
            tcb = Tc[l][:, s, 0:sl].un(1).bc([128, nseg, sl])
            tsb = Ts[l][:, s, 0:sl].un(1).bc([128, nseg, sl])
            k.tt(v3(t1), v3(bur), tcb, MUL)
            k.tt(v3(t2), v3(bui), tsb, MUL)
            k.tt(gr[:, 0:nt_], t1[:, 0:nt_], t2[:, 0:nt_], ADD)
            k.tt(v3(t1), v3(bui), tcb, MUL)
            k.tt(v3(t2), v3(bur), tsb, MUL)
            k.tt(gi[:, 0:nt_], t1[:, 0:nt_], t2[:, 0:nt_], SUB)
            magb = S5(l, "mag")[:, s:s + 1]
            if B.kind == "p":
                cre, cim = s5cr[l][:, s:s + 1], s5ci[l][:, s:s + 1]
                for sg in range(nseg):
                    c0 = sg * sl
                    if sg == 0:
                        k.stt(sm[:, 0:1], cre, ar_, gr[:, c0:c0 + 1], MUL, ADD)
                        k.stt(gr[:, c0:c0 + 1], cim, nai_, sm[:, 0:1], MUL, ADD)
                        k.stt(sm[:, 1:2], cim, ar_, gi[:, c0:c0 + 1], MUL, ADD)
                        k.stt(gi[:, c0:c0 + 1], cre, ai_, sm[:, 1:2], MUL, ADD)
                    else:
                        glr, gli = t1[:, c0 - 1:c0], t2[:, c0 - 1:c0]
                        k.stt(sm[:, 0:1], glr, Qr_, gr[:, c0:c0 + 1], MUL, ADD)
                        k.stt(gr[:, c0:c0 + 1], gli, nQi_, sm[:, 0:1], MUL, ADD)
                        k.stt(sm[:, 1:2], gli, Qr_, gi[:, c0:c0 + 1], MUL, ADD)
                        k.stt(gi[:, c0:c0 + 1], glr, Qi_, sm[:, 1:2], MUL, ADD)
                    k.op("dve", "tensor_tensor_scan", out=t1[:, c0:c0 + sl], data0=magb.bc([128, sl]),
                         data1=gr[:, c0:c0 + sl], initial=0.0, op0=MUL, op1=ADD)
                    k.op("dve", "tensor_tensor_scan", out=t2[:, c0:c0 + sl], data0=magb.bc([128, sl]),
                         data1=gi[:, c0:c0 + sl], initial=0.0, op0=MUL, op1=ADD)
                cl, sl_ = Tc[l][:, s, sl - 1:sl], Ts[l][:, s, sl - 1:sl]
                hrl, hil = t1[:, nt_ - 1:nt_], t2[:, nt_ - 1:nt_]
                k.ts(sm[:, 2:3], hil, sl_, MUL)
                k.stt(cre, hrl, cl, sm[:, 2:3], MUL, SUB)
                k.ts(sm[:, 3:4], hrl, sl_, MUL)
                k.stt(cim, hil, cl, sm[:, 3:4], MUL, ADD)
            else:
                g3r, g3i = v3(gr), v3(gi)
                hr0, hi0 = st_r[:, s, :], st_i[:, s, :]
                k.stt(sm[:, 0:16], hr0, ar_, g3r[:, :, 0], MUL, ADD)
                k.stt(g3r[:, :, 0], hi0, nai_, sm[:, 0:16], MUL, ADD)
                k.stt(sm[:, 16:32], hi0, ar_, g3i[:, :, 0], MUL, ADD)
                k.stt(g3i[:, :, 0], hr0, ai_, sm[:, 16:32], MUL, ADD)
                k.ts(coef, C("startmask", 128), magb, MUL)
                k.op("dve", "tensor_tensor_scan", out=t1[:, 0:nt_], data0=coef[:, 0:nt_],
                     data1=gr[:, 0:nt_], initial=0.0, op0=MUL, op1=ADD)
                k.op("dve", "tensor_tensor_scan", out=t2[:, 0:nt_], data0=coef[:, 0:nt_],
                     data1=gi[:, 0:nt_], initial=0.0, op0=MUL, op1=ADD)
            RB = dbg.get("rb_eng", "dve")
            k.tt(v3(gr), v3(t1), tcb, MUL, eng=RB)
            k.tt(v3(gi), v3(t2), tsb, MUL, eng=RB)
            k.tt(bur[:, 0:nt_], gr[:, 0:nt_], gi[:, 0:nt_], SUB, eng=RB)
            k.tt(v3(gr), v3(t2), tcb, MUL, eng=RB)
            k.tt(v3(gi), v3(t1), tsb, MUL, eng=RB)
            k.tt(bui[:, 0:nt_], gr[:, 0:nt_], gi[:, 0:nt_], ADD, eng=RB)
            if B.kind == "s":
                k.copy(so_r[:, s, :], v3(bur)[:, :, sl - 1], "dve")
                k.copy(so_i[:, s, :], v3(bui)[:, :, sl - 1], "dve")
            k.mm(Yps[ct][:, 0:nt_], s5w[2][:, s, :], bur[:, 0:nt_], start=(s % 4 == 0), stop=False)
            k.mm(Yps[ct][:, 0:nt_], s5w[3][:, s, :], bui[:, 0:nt_], start=False, stop=(s % 4 == 3))

        for s0 in range(0, 8, 2):
            k.start_rec()
            tile_body(s0, 0)
            sa = k.stop_rec()
            k.start_rec()
            tile_body(s0 + 1, 1)
            sb_ = k.stop_rec()
            k.interleave(sa, sb_)
        if B.kind == "s":
            k.dma("sp", O["o_s5re_s"][l], so_r, is_output=True)
            k.dma("sp", O["o_s5im_s"][l], so_i, is_output=True)
        elif B.last:
            k.dma("sp", O["o_s5re_p"][l], s5cr[l], is_output=True)
            k.dma("sp", O["o_s5im_p"][l], s5ci[l], is_output=True)
        for ct in range(2):
            k.stt(yc[:, ct, 0:nt_], u[:, ct, 0:nt_], P(l, "s5_d", ct), Yps[ct][:, 0:nt_], MUL, ADD)
            k.act(z[:, ct, 0:nt_], yc[:, ct, 0:nt_], AF.Gelu)
        for co in range(2):
            ps = PF[2 + co]
            for ci2 in range(2):
                k.mm(ps[:, 0:nt_], wglu[l][:, ci2, co * 128:(co + 1) * 128], z[:, ci2, 0:nt_],
                     start=(ci2 == 0), stop=(ci2 == 1))
            k.act(yc[:, co, 0:nt_], ps[:, 0:nt_], AF.Sigmoid, bias=P(l, "b_glu", co))
            k.tt(yc[:, co, 0:nt_], yc[:, co, 0:nt_], z[:, co, 0:nt_], MUL)
        dump(f"yc{l}", yc[:, :, 0:nt_], [128, 2, nt_])
        rms_pair(B, l, [yc[:, 0, 0:nt_], yc[:, 1, 0:nt_]], "g_out_c", 6)

    def mixer_b(B, l, wbase):
        nt_ = B.ntok
        k.barrier()
        scr.reset()
        Tn = B.T
        gate = scr.alloc([128, 2, TBP], F32, "gate")
        xbuf = [scr.alloc([128, B.nseq, 3 + Tn], F32, f"xbuf{ct}") for ct in range(2)]
        xc = scr.alloc([128, TBP], F32, "xc")
        gr_ = scr.alloc([128, TBP], F32, "gr_")
        gi_ = scr.alloc([128, TBP], F32, "gi_")
        a_ = scr.alloc([128, TBP], F32, "a_")
        b_ = scr.alloc([128, TBP], F32, "b_")
        yb = scr.alloc([128, 2, TBP], F32, "yb")
        so = scr.alloc([128, 2, 16], F32, "so")
        st = scr.alloc([128, 2, 16], F32, "st")
        sconv = scr.alloc([128, 2, 16, 3], F32, "sconv")
        if B.kind == "s":
            k.dma("sp", st, I["st_lru"][l])
            k.dma("sp", sconv, I["st_conv"][l])
        for ct in range(2):
            inproj(B, l, wbase, 14 + ct, lambda ps, ct=ct: k.copy(gate[:, ct, 0:nt_], ps))
        for ct in range(2):
            if B.kind == "p":
                k.copy(xbuf[ct][:, 0, 0:3], histB[l][:, ct, :], "dve")
            else:
                k.copy(xbuf[ct][:, :, 0:3], sconv[:, ct, :, :], "dve")
            inproj(B, l, wbase, 16 + ct, lambda ps, ct=ct: k.copy(xbuf[ct][:, :, 3:3 + Tn], seq3(ps, B)))
        for ct in range(2):
            xb = xbuf[ct]
            if B.kind == "p":
                k.copy(histB[l][:, ct, :], xb[:, 0, Tn:Tn + 3], "dve")
                if B.last:
                    k.dma("sp", O["o_conv_p"][l][:, ct, :], histB[l][:, ct, :], is_output=True)
            else:
                k.dma("sp", O["o_conv_s"][l][:, ct], xb[:, :, Tn:Tn + 3], is_output=True)
            xc3 = seq3(xc[:, 0:nt_], B)
            k.ts(xc3, xb[:, :, 0:Tn], P(l, "conv_w", ct * 4 + 0), MUL, P(l, "conv_b", ct), ADD)
            for j in range(1, 4):
                k.stt(xc3, xb[:, :, j:j + Tn], P(l, "conv_w", ct * 4 + j), xc3, MUL, ADD)
            ps1, ps2 = pf(), pf()
            k.mm(ps1[:, 0:nt_], wrg[l][:, ct, :], xc[:, 0:nt_])
            k.mm(ps2[:, 0:nt_], wig[l][:, ct, :], xc[:, 0:nt_])
            k.act(gr_[:, 0:nt_], ps1[:, 0:nt_], AF.Sigmoid, bias=P(l, "b_rg", ct))
            k.act(gi_[:, 0:nt_], ps2[:, 0:nt_], AF.Sigmoid, bias=P(l, "b_ig", ct))
            k.act(a_[:, 0:nt_], gr_[:, 0:nt_], AF.Exp, scale=c8[l][:, ct:ct + 1])
            k.tt(b_[:, 0:nt_], a_[:, 0:nt_], a_[:, 0:nt_], MUL)
            k.ts(b_[:, 0:nt_], b_[:, 0:nt_], -1.0, MUL, 1.0, ADD)
            k.act(b_[:, 0:nt_], b_[:, 0:nt_], AF.Sqrt)
            k.tt(b_[:, 0:nt_], b_[:, 0:nt_], gi_[:, 0:nt_], MUL)
            k.tt(b_[:, 0:nt_], b_[:, 0:nt_], xc[:, 0:nt_], MUL)
            hs = gr_
            if B.kind == "p":
                k.op("dve", "tensor_tensor_scan", out=hs[:, 0:nt_], data0=a_[:, 0:nt_], data1=b_[:, 0:nt_],
                     initial=lruc[l][:, ct:ct + 1], op0=MUL, op1=ADD)
                k.copy(lruc[l][:, ct:ct + 1], hs[:, nt_ - 1:nt_], "dve")
                if B.last and ct == 1:
                    k.dma("sp", O["o_lru_p"][l], lruc[l], is_output=True)
            else:
                a3, b3 = seq3(a_[:, 0:nt_], B), seq3(b_[:, 0:nt_], B)
                k.tt(small[:, 0:16], a3[:, :, 0], st[:, ct, :], MUL)
                k.tt(b3[:, :, 0], b3[:, :, 0], small[:, 0:16], ADD)
                k.tt(a_[:, 0:nt_], a_[:, 0:nt_], C("startmask", 128), MUL)
                k.op("dve", "tensor_tensor_scan", out=hs[:, 0:nt_], data0=a_[:, 0:nt_], data1=b_[:, 0:nt_],
                     initial=0.0, op0=MUL, op1=ADD)
                k.copy(so[:, ct, :], seq3(hs[:, 0:nt_], B)[:, :, Tn - 1], "dve")
            k.act(gi_[:, 0:nt_], gate[:, ct, 0:nt_], AF.Gelu)
            k.tt(yb[:, ct, 0:nt_], hs[:, 0:nt_], gi_[:, 0:nt_], MUL)
        if B.kind == "s":
            k.dma("sp", O["o_lru_s"][l], so, is_output=True)
        dump(f"yb{l}", yb[:, :, 0:nt_], [128, 2, nt_])
        rms_pair(B, l, [yb[:, 0, 0:nt_], yb[:, 1, 0:nt_]], "g_out_b", 4)

    D1 = [T(nc.dram_tensor(f"d1_{l}", [128, 6 * 512], F32, kind="Internal").ap(), f"d1_{l}") for l in range(NL)]
    D2 = [T(nc.dram_tensor(f"d2_{l}", [128, 512], F32, kind="Internal").ap(), f"d2_{l}") for l in range(NL)]
    EXPM05 = math.exp(-0.5)

    def mixer_a(B, l, wbase):
        nt_ = B.ntok
        Tn, ns = B.T, B.nseq
        prompt = B.kind == "p"
        k.barrier()
        scr.reset()
        lo = [scr.alloc([128, ns, 1 + Tn], F32, f"lo{j}") for j in range(2)]
        rkv = [scr.alloc([128, ns, 1 + Tn], F32, f"rkv{j}") for j in range(3)]
        dtmp = scr.alloc([128, nt_], F32, "dtmp")
        l12 = scr.alloc([128, nt_], F32, "l12")
        sgc = scr.alloc([128, nt_], F32, "sgc")
        xs = [scr.alloc([128, nt_], F32, f"xs{j}") for j in range(3)]
        UN = 256 if prompt else 128
        lw, cs, a_, g_, kk, kmod, beta, tA, tB, bonus, ysb, yc = [
            scr.alloc([128, UN], F32, nm) for nm in
            ("lw", "cs", "a_", "g_", "kk", "kmod", "beta", "tA", "tB", "bonus", "ysb", "yc")]
        gam = scr.alloc([128, 8], F32, "gam")
        if prompt:
            ARt = scr.alloc([128, 4, 192], BF16, "ARt")
            Bt_bd = scr.alloc([128, 4, 128], BF16, "Bt_bd")
            Kt_bd = scr.alloc([128, 4, 128], BF16, "Kt_bd")
            Vt_bd = scr.alloc([128, 4, 128], BF16, "Vt_bd")
            XM = scr.alloc([128, 4, 192], BF16, "XM")
            KM = scr.alloc([128, 4, 192], BF16, "KM")
            Btok = scr.alloc([128, 4, 128], BF16, "Btok")
            Ktok = scr.alloc([128, 4, 128], BF16, "Ktok")
            Vbd = scr.alloc([128, 4, 128], BF16, "Vbd")
            TT5 = scr.alloc([128, 4, 128], BF16, "TT5")
            Xr = [scr.alloc([128, 4, 128], BF16, f"Xr{i}") for i in range(2)]
            Zr = [scr.alloc([128, 4, 128], BF16, f"Zr{i}") for i in range(2)]
            TTt = [scr.alloc([128, 4, 128], BF16, f"TTt{i}") for i in range(2)]
            Wsb = scr.alloc([128, 128], BF16, "Wsb")
            Ubd = scr.alloc([128, 128], BF16, "Ubd")
            Pb16 = scr.alloc([128, 128], BF16, "Pb16")
            for t_ in (ARt, Bt_bd, Kt_bd, Vt_bd):
                k.memset(t_, 0.0)
        else:
            S = scr.alloc([128, 4096], F32, "S")
            tmp3 = scr.alloc([128, 4096], F32, "tmp3")
            RW = scr.alloc([128, 8, 6, 64], F32, "RW")
            KKA = scr.alloc([128, 8, 64], F32, "KKA")
            Ys = scr.alloc([128, 8, 64], F32, "Ys")
            TM = T(tmp3.ap[:, 0:3072].rearrange("p (q h j) -> p q h j", q=6, h=4), "TM", scr=True)
            YT = scr.alloc([128, 8, 64], F32, "YT")
            skk = scr.alloc([128, 64], F32, "skk")
            bonus_s = scr.alloc([128, 4, 128], F32, "bonus_s")
            g_s = scr.alloc([128, 4, 128], F32, "g_s")

        def load_shift(raw, tile):
            if prompt:
                k.copy(raw[:, 0, 0:1], histA[l][:, tile:tile + 1], "dve")
            else:
                k.dma("sp", raw[:, :, 0], I["st_shift"][l][:, tile, :])
            inproj(B, l, wbase, tile, lambda ps: k.copy(raw[:, :, 1:1 + Tn], seq3(ps, B)))
            if prompt:
                k.copy(histA[l][:, tile:tile + 1], raw[:, 0, Tn:Tn + 1], "dve")
            else:
                k.dma("sp", O["o_shift_s"][l][:, tile, :], raw[:, :, Tn], is_output=True)

        def shift(raw, tile, dst):
            k.tt(seq3(dtmp[:, 0:nt_], B), raw[:, :, 0:Tn], raw[:, :, 1:1 + Tn], SUB)
            k.stt(seq3(dst[:, 0:nt_], B), seq3(dtmp[:, 0:nt_], B), P(l, "mu", tile), raw[:, :, 1:1 + Tn], MUL, ADD)

        load_shift(lo[0], 12)
        shift(lo[0], 12, l12)
        load_shift(lo[1], 13)
        shift(lo[1], 13, sgc)
        k.act(l12[0:64, 0:nt_], l12[0:64, 0:nt_], AF.Tanh)
        k.act(sgc[:, 0:nt_], sgc[:, 0:nt_], AF.Sigmoid)

        units = [(0, 256), (256, 256)] if prompt else [(0, 128)]
        for hp in range(4):
            Pb = Pbd[l][hp]
            if prompt:
                k.copy(Pb16, Pb, "dve")
            for j, tile in enumerate((hp, 4 + hp, 8 + hp)):
                load_shift(rkv[j], tile)
                shift(rkv[j], tile, xs[j])
            if l == 0 and hp == 0:
                dump("xs0", xs[0][:, 0:nt_], [128, nt_])
            hsl = slice(hp * 128, (hp + 1) * 128)
            for (c0, n) in units:
                sl = slice(c0, c0 + n)
                xr_, xk_, xv_ = xs[0][:, sl], xs[1][:, sl], xs[2][:, sl]
                N_ = slice(0, n)
                ps = ph()
                k.mm(ps[:, N_], lora1[l][0:64, hsl], l12[0:64, sl])
                k.act(lw[:, N_], ps[:, N_], AF.Sigmoid, bias=P(l, "w0", hp))
                k.ts(lw[:, N_], lw[:, N_], -EXPM05, MUL)
                ps = ph()
                k.mm(ps[:, N_], lora1[l][64:128, hsl], l12[64:128, sl])
                k.act(a_[:, N_], ps[:, N_], AF.Sigmoid, bias=P(l, "a0", hp))
                ps = ph()
                k.mm(ps[:, N_], wg2[l][:, hsl], sgc[:, sl])
                gdst = g_[:, N_] if prompt else g_s[:, hp, :]
                k.copy(gdst, ps[:, N_])
                k.ts(kk[:, N_], xk_, P(l, "k_k", hp), MUL)
                k.tt(tA[:, N_], kk[:, N_], kk[:, N_], MUL)
                ps = ph()
                k.mm(ps[:, N_], C("onesblk", 128), tA[:, N_])
                k.ts(tB[:, N_], ps[:, N_], 1e-24, ALU.max)
                k.act(tB[:, N_], tB[:, N_], AF.Ln)
                k.act(tB[:, N_], tB[:, N_], AF.Exp, scale=-0.5)
                k.tt(kk[:, N_], kk[:, N_], tB[:, N_], MUL)
                k.ts(tA[:, N_], a_[:, N_], -1.0, ADD, P(l, "k_a", hp), MUL)
                k.stt(kmod[:, N_], tA[:, N_], 1.0, xk_, ADD, MUL)
                k.tt(beta[:, N_], kk[:, N_], a_[:, N_], MUL)
                k.stt(tA[:, N_], xr_, P(l, "r_k", hp), kmod[:, N_], MUL, MUL)
                ps = ph()
                k.mm(ps[:, N_], C("onesblk", 128), tA[:, N_])
                bdst = bonus[:, N_] if prompt else bonus_s[:, hp, :]
                k.tt(bdst, ps[:, N_], xv_, MUL)
                if prompt:
                    ch = lambda v: v.re("p (c t) -> p c t", t=64)
                    k.op("dve", "tensor_tensor_scan", out=cs[:, N_], data0=C("cumcoef", 256), data1=lw[:, N_],
                         initial=0.0, op0=MUL, op1=ADD)
                    k.tt(lw[:, N_], cs[:, N_], lw[:, N_], SUB)
                    k.act(tA[:, N_], cs[:, N_], AF.Exp)
                    k.copy(gam[:, 0:4], tA[:, 63:256:64], "dve")
                    k.tt(ARt[:, :, 128:192], ch(xr_), ch(tA[:, N_]), MUL)
                    k.act(tB[:, N_], cs[:, N_], AF.Exp, scale=-1.0)
                    for hh in range(2):
                        rows = slice(hh * 64, hh * 64 + 64)
                        cols = slice(hh * 64, hh * 64 + 64)
                        k.tt(Bt_bd[rows, :, cols], ch(beta[rows, N_]), ch(tB[rows, N_]), MUL)
                        k.tt(Kt_bd[rows, :, cols], ch(kmod[rows, N_]), ch(tB[rows, N_]), MUL)
                        k.copy(Vt_bd[rows, :, cols], ch(xs[2][rows, sl]), "act")
                    k.act(tA[:, N_], lw[:, N_], AF.Exp)
                    for hh in range(2):
                        rows = slice(hh * 64, hh * 64 + 64)
                        cols = slice(hh * 64, hh * 64 + 64)
                        k.stt(ARt[rows, :, cols], ch(kk[rows, N_]), -1.0, ch(tA[rows, N_]), MUL, MUL)
                    for c in range(4):
                        psa = ph()
                        k.mm(psa[:, 0:192], Bt_bd[:, c, :], ARt[:, c, :])
                        k.tt(XM[:, c, :], psa[:, 0:192], C("maskA", 192), MUL)
                        psb = ph()
                        k.mm(psb[:, 0:192], Kt_bd[:, c, :], ARt[:, c, :])
                        k.tt(KM[:, c, :], psb[:, 0:192], C("maskA", 192), MUL)
                        psc = ph()
                        k.mm(psc[:, 0:128], ARt[:, c, 0:128], Bt_bd[:, c, :])
                        k.tt(Zr[0][:, c, :], psc[:, 0:128], C("maskSL", 128), MUL)
                        for (src, dst) in ((Bt_bd, Btok), (Kt_bd, Ktok), (Vt_bd, Vbd)):
                            pst = ph()
                            pst16 = V(pst, pst.ap.bitcast(BF16))
                            k.tr(pst16[:, 0:128], src[:, c, :], idb)
                            k.copy(dst[:, c, :], pst16[:, 0:128])
                        k.tt(TTt[0][:, c, :], XM[:, c, 0:128], ident, ADD)
                    for r in range(1, 7):
                        for c in range(4):
                            Zp = Zr[(r - 1) % 2][:, c, :]
                            Xp = XM[:, c, 0:128] if r == 1 else Xr[(r - 1) % 2][:, c, :]
                            if r <= 4:
                                ps1 = ph()
                                k.mm(ps1[:, 0:128], Zp, Xp)
                                k.copy(Xr[r % 2][:, c, :], ps1[:, 0:128])
                            if 2 <= r <= 5:
                                psT = ph()
                                k.mm(psT[:, 0:128], Zp, TTt[(r - 2) % 2][:, c, :])
                                k.tt(TTt[(r - 1) % 2][:, c, :], psT[:, 0:128], TTt[(r - 2) % 2][:, c, :], ADD)
                            if r <= 5:
                                ps2 = ph()
                                k.mm(ps2[:, 0:128], Xp, Zp)
                                k.copy(Zr[r % 2][:, c, :], ps2[:, 0:128])
                            if r == 6:
                                psT = ph()
                                k.mm(psT[:, 0:128], Zp, TTt[0][:, c, :])
                                k.tt(TT5[:, c, :], psT[:, 0:128], TTt[0][:, c, :], ADD)
                    Yps = PF[3]
                    for c in range(4):
                        Wps = ph()
                        k.mm(Wps[:, 0:128], ARt[:, c, 0:128], Pb16, True, False)
                        k.mm(Wps[:, 0:128], KM[:, c, 0:128], Vbd[:, c, :], False, True)
                        k.copy(Wsb, Wps[:, 0:128])
                        k.ts(Pb, Pb, gam[:, c:c + 1], MUL)
                        Ups = ph()
                        k.mm(Ups[:, 0:128], TT5[:, c, :], Wsb)
                        k.copy(Ubd, Ups[:, 0:128])
                        ycol = Yps[:, c * 64:(c + 1) * 64]
                        k.mm(ycol, Pb16, ARt[:, c, 128:192], True, False)
                        k.mm(ycol, Ubd, XM[:, c, 128:192], False, False)
                        k.mm(ycol, Vbd[:, c, :], KM[:, c, 128:192], False, True)
                        Pps = ph()
                        k.mm(Pps[:, 0:128], Btok[:, c, :], Ubd, True, False)
                        k.mm(Pps[:, 0:128], Ktok[:, c, :], Vbd[:, c, :], False, True)
                        k.stt(Pb, Pps[:, 0:128], gam[:, c:c + 1], Pb, MUL, ADD)
                        k.copy(Pb16, Pb, "dve")
                    k.copy(ysb[:, N_], Yps[:, N_])
                    finish_unit(l, hp, sl, N_, ysb, yc, tA, tB, bonus[:, N_], g_[:, N_])
                else:
                    k.act(tA[:, N_], lw[:, N_], AF.Exp)
                    for qi, src in enumerate((xr_, tA[:, N_], kmod[:, N_], xv_, kk[:, N_], a_[:, N_])):
                        pst = ph()
                        k.tr(pst[:, 0:128], src, ident)
                        k.copy(TM[:, qi, hp, :], pst[:, 0:128])
            if prompt and B.last:
                for hh in range(2):
                    rows = slice(hh * 64, hh * 64 + 64)
                    k.dma("sp", O["o_wkv_p"][l][2 * hp + hh], Pb[rows, rows], is_output=True)
        if prompt:
            if B.last:
                k.dma("sp", O["o_shift_p"][l], histA[l], is_output=True)
            return
        k.dma("sp", D1[l], TM.re("p q h j -> p (q h j)"))
        for b in range(SB_PER):
            src = D1[l][b * 8:(b + 1) * 8, :].re("t (q h j) -> h (t q) j", q=6, h=8)
            k.dma("sp", RW[b * 8:(b + 1) * 8].re("p t q j -> p (t q) j"), src)
        k.tt(KKA, RW[:, :, 4, :], RW[:, :, 5, :], MUL)
        ISPL = dbg.get("ispl", 64)
        parts = []
        for (i0, i1, eng) in ((0, ISPL, "dve"), (ISPL, 64, "pool")):
            if i1 <= i0:
                continue
            ni = i1 - i0
            S_ = T(S.ap[:, i0 * 64:i1 * 64].rearrange("p (i j) -> p i j", j=64), f"S_{eng}", scr=True)
            T_ = T(tmp3.ap[:, i0 * 64:i1 * 64].rearrange("p (i j) -> p i j", j=64), f"T3_{eng}", scr=True)
            k_ = T(skk.ap[:, i0:i1], f"skk_{eng}", scr=True)
            Y_ = T(Ys.ap[:, :, i0:i1], f"Ys_{eng}", scr=True)
            k.dma("sp", S_.re("p i j -> p (i j)"), I["st_wkv"][l][:, i0 * 64:i1 * 64])
            parts.append((i0, i1, ni, eng, S_, T_, k_, Y_))
        for t in range(DEC_T):
            for (i0, i1, ni, eng, S_, T_, k_, Y_) in parts:
                bi = lambda v: v.un(1).bc([128, ni, 64])
                bj = lambda v: v.un(2).bc([128, ni, 64])
                k.tt(T_, S_, bi(RW[:, t, 4, :]), MUL, eng=eng)
                k.op("dve", "tensor_reduce", out=k_, in_=T_, axis=AX.X, op=ADD)
                k.tt(S_, S_, bi(RW[:, t, 1, :]), MUL, eng=eng)
                k.tt(T_, bj(k_), bi(KKA[:, t, :]), MUL, eng=eng)
                k.tt(S_, S_, T_, SUB, eng=eng)
                k.tt(T_, bj(RW[:, t, 3, i0:i1]), bi(RW[:, t, 2, :]), MUL, eng=eng)
                k.tt(S_, S_, T_, ADD, eng=eng)
                k.tt(T_, S_, bi(RW[:, t, 0, :]), MUL, eng=eng)
                k.op("dve", "tensor_reduce", out=Y_[:, t, :], in_=T_, axis=AX.X, op=ADD)
        for (i0, i1, ni, eng, S_, T_, k_, Y_) in parts:
            k.dma("sp", O["o_wkv_s"][l][:, i0 * 64:i1 * 64], S_.re("p i j -> p (i j)"), is_output=True)
        Ysd = [p_[7] for p_ in parts]
        for (i0, i1, ni, eng, S_, T_, k_, Y_) in parts:
            k.dma("sp", D2[l].re("p (t i) -> p t i", t=8)[:, :, i0:i1], Y_)
        for b in range(SB_PER):
            src = D2[l][b * 8:(b + 1) * 8, :].re("h (t i) -> t h i", t=8)
            k.dma("sp", YT[b * 8:(b + 1) * 8], src)
        N_ = slice(0, 128)
        for hp in range(4):
            pst = ph()
            k.tr(pst[:, 0:128], YT[:, 2 * hp:2 * hp + 2, :].re("p h i -> p (h i)"), ident)
            k.copy(ysb[:, N_], pst[:, 0:128])
            finish_unit(l, hp, slice(0, 128), N_, ysb, yc, tA, tB, bonus_s[:, hp, :], g_s[:, hp, :])

    def mixer_a_prompt(B, l, wbase):
        nt_, Tn = B.ntok, B.T
        k.barrier()
        scr.reset()
        lo = [scr.alloc([128, 1, 1 + Tn], F32, f"lo{j}") for j in range(2)]
        rkv = [scr.alloc([128, 1, 1 + Tn], F32, f"rkv{j}") for j in range(3)]
        dtmp = scr.alloc([128, nt_], F32, "dtmp")
        l12 = scr.alloc([128, nt_], F32, "l12")
        sgc = scr.alloc([128, nt_], F32, "sgc")
        xs = [scr.alloc([128, nt_], F32, f"xs{j}") for j in range(3)]
        UN = 256
        lw, cs, a_, kk, kmod, beta, tA, tB, ysb, yc, tAf, tBf = [
            scr.alloc([128, UN], F32, nm) for nm in
            ("lw", "cs", "a_", "kk", "kmod", "beta", "tA", "tB", "ysb", "yc", "tAf", "tBf")]
        sets3, sets2e, sets2g = [], [], []
        for si in range(3):
            d = {}
            d["ARt"] = scr.alloc([128, 4, 192], BF16, f"ARt{si}")
            d["gam"] = scr.alloc([128, 8], F32, f"gam{si}")
            d["bonus"] = scr.alloc([128, UN], F32, f"bonus{si}")
            d["g_"] = scr.alloc([128, UN], F32, f"g_{si}")
            sets3.append(d)
        for si in range(2):
            d = {}
            for nm in ("Bt_bd", "Kt_bd", "Vt_bd"):
                d[nm] = scr.alloc([128, 4, 128], BF16, f"{nm}{si}")
            sets2e.append(d)
            d = {}
            d["MrT"] = scr.alloc([128, 8, 64], BF16, f"MrT{si}")
            for nm in ("Xs", "MakT", "Btok", "Ktok", "Vbd", "TT5"):
                d[nm] = scr.alloc([128, 4, 128], BF16, f"{nm}{si}")
            sets2g.append(d)
        Xr = [scr.alloc([128, 4, 128], BF16, f"Xr{i}") for i in range(2)]
        Zr = [scr.alloc([128, 4, 128], BF16, f"Zr{i}") for i in range(2)]
        TTt = [scr.alloc([128, 4, 128], BF16, f"TTt{i}") for i in range(2)]
        Wsb = scr.alloc([128, 128], BF16, "Wsb")
        Ubd = scr.alloc([128, 128], BF16, "Ubd")
        Pb16 = scr.alloc([128, 128], BF16, "Pb16")
        for d in sets3:
            k.memset(d["ARt"], 0.0)
        for d in sets2e:
            for nm in ("Bt_bd", "Kt_bd", "Vt_bd"):
                k.memset(d[nm], 0.0)

        def load_shift(raw, tile):
            k.copy(raw[:, 0, 0:1], histA[l][:, tile:tile + 1], "dve")
            inproj(B, l, wbase, tile, lambda ps: k.copy(raw[:, :, 1:1 + Tn], seq3(ps, B)))
            k.copy(histA[l][:, tile:tile + 1], raw[:, 0, Tn:Tn + 1], "dve")

        def shift(raw, tile, dst):
            k.tt(seq3(dtmp[:, 0:nt_], B), raw[:, :, 0:Tn], raw[:, :, 1:1 + Tn], SUB)
            k.stt(seq3(dst[:, 0:nt_], B), seq3(dtmp[:, 0:nt_], B), P(l, "mu", tile), raw[:, :, 1:1 + Tn], MUL, ADD)

        load_shift(lo[0], 12)
        shift(lo[0], 12, l12)
        load_shift(lo[1], 13)
        shift(lo[1], 13, sgc)
        k.act(l12[0:64, 0:nt_], l12[0:64, 0:nt_], AF.Tanh)
        k.act(sgc[:, 0:nt_], sgc[:, 0:nt_], AF.Sigmoid)
        ch = lambda v: v.re("p (c t) -> p c t", t=64)

        def ew(ui):
            hp, sub = ui // 2, ui % 2
            D3, D2e = sets3[ui % 3], sets2e[ui % 2]
            ARt, gam, bonus, g_ = D3["ARt"], D3["gam"], D3["bonus"], D3["g_"]
            Bt_bd, Kt_bd, Vt_bd = D2e["Bt_bd"], D2e["Kt_bd"], D2e["Vt_bd"]
            if sub == 0:
                for j, tile in enumerate((hp, 4 + hp, 8 + hp)):
                    load_shift(rkv[j], tile)
                    shift(rkv[j], tile, xs[j])
            hsl = slice(hp * 128, (hp + 1) * 128)
            c0, n = sub * 256, 256
            sl = slice(c0, c0 + n)
            xr_, xk_, xv_ = xs[0][:, sl], xs[1][:, sl], xs[2][:, sl]
            N_ = slice(0, n)
            ps = ph()
            k.mm(ps[:, N_], lora1[l][0:64, hsl], l12[0:64, sl])
            k.act(lw[:, N_], ps[:, N_], AF.Sigmoid, bias=P(l, "w0", hp))
            k.act(lw[:, N_], lw[:, N_], AF.Copy, scale=-EXPM05)
            ps = ph()
            k.mm(ps[:, N_], lora1[l][64:128, hsl], l12[64:128, sl])
            k.act(a_[:, N_], ps[:, N_], AF.Sigmoid, bias=P(l, "a0", hp))
            ps = ph()
            k.mm(ps[:, N_], wg2[l][:, hsl], sgc[:, sl])
            k.copy(g_[:, N_], ps[:, N_])
            k.act(kk[:, N_], xk_, AF.Copy, scale=P(l, "k_k", hp))
            k.tt(tA[:, N_], kk[:, N_], kk[:, N_], MUL)
            ps = ph()
            k.mm(ps[:, N_], C("onesblk", 128), tA[:, N_])
            k.ts(tB[:, N_], ps[:, N_], 1e-24, ALU.max)
            k.act(tB[:, N_], tB[:, N_], AF.Ln)
            k.act(tB[:, N_], tB[:, N_], AF.Exp, scale=-0.5)
            k.tt(kk[:, N_], kk[:, N_], tB[:, N_], MUL)
            k.ts(tA[:, N_], a_[:, N_], -1.0, ADD, P(l, "k_a", hp), MUL)
            k.stt(kmod[:, N_], tA[:, N_], 1.0, xk_, ADD, MUL)
            k.tt(beta[:, N_], kk[:, N_], a_[:, N_], MUL)
            k.stt(tA[:, N_], xr_, P(l, "r_k", hp), kmod[:, N_], MUL, MUL)
            ps = ph()
            k.mm(ps[:, N_], C("onesblk", 128), tA[:, N_])
            k.tt(bonus[:, N_], ps[:, N_], xv_, MUL)
            k.op("dve", "tensor_tensor_scan", out=cs[:, N_], data0=C("cumcoef", 256), data1=lw[:, N_],
                 initial=0.0, op0=MUL, op1=ADD)
            k.tt(lw[:, N_], cs[:, N_], lw[:, N_], SUB)
            k.act(tA[:, N_], cs[:, N_], AF.Exp)
            k.copy(gam[:, 0:4], tA[:, 63:256:64], "dve")
            k.tt(ARt[:, :, 128:192], ch(xr_), ch(tA[:, N_]), MUL)
            k.act(tB[:, N_], cs[:, N_], AF.Exp, scale=-1.0)
            for hh in range(2):
                rows = slice(hh * 64, hh * 64 + 64)
                cols = slice(hh * 64, hh * 64 + 64)
                k.tt(Bt_bd[rows, :, cols], ch(beta[rows, N_]), ch(tB[rows, N_]), MUL)
                k.tt(Kt_bd[rows, :, cols], ch(kmod[rows, N_]), ch(tB[rows, N_]), MUL)
                k.copy(Vt_bd[rows, :, cols], ch(xs[2][rows, sl]), "act")
            k.act(tA[:, N_], lw[:, N_], AF.Exp)
            for hh in range(2):
                rows = slice(hh * 64, hh * 64 + 64)
                cols = slice(hh * 64, hh * 64 + 64)
                k.stt(ARt[rows, :, cols], ch(kk[rows, N_]), -1.0, ch(tA[rows, N_]), MUL, MUL)

        def gn(ui):
            D3, D2e, D2g = sets3[ui % 3], sets2e[ui % 2], sets2g[ui % 2]
            ARt = D3["ARt"]
            Bt_bd, Kt_bd, Vt_bd = D2e["Bt_bd"], D2e["Kt_bd"], D2e["Vt_bd"]
            Xs, MakT, MrT, Btok, Ktok, Vbd, TT5 = (D2g[n] for n in ("Xs", "MakT", "MrT", "Btok", "Ktok", "Vbd", "TT5"))
            c4 = lambda v: v[:, 0:512].re("p (c t) -> p c t", c=4)
            mSU = C("maskA", 128).un(1).bc([128, 4, 128])
            mSL = C("maskSL", 128).un(1).bc([128, 4, 128])
            mIU = consts[:, CC["maskA"] + 128:CC["maskA"] + 192].un(1).bc([128, 8, 64])
            idb4 = ident.un(1).bc([128, 4, 128])
            bX = ph()
            for c in range(4):
                k.mm(bX[:, c * 128:(c + 1) * 128], Bt_bd[:, c, :], ARt[:, c, 0:128])
            k.tt(Xs, c4(bX), mSU, MUL)
            bZ = ph()
            for c in range(4):
                k.mm(bZ[:, c * 128:(c + 1) * 128], ARt[:, c, 0:128], Bt_bd[:, c, :])
            k.tt(Zr[0], c4(bZ), mSL, MUL)
            bK = ph()
            for c in range(4):
                k.mm(bK[:, c * 128:(c + 1) * 128], Kt_bd[:, c, :], ARt[:, c, 0:128])
            k.tt(MakT, c4(bK), mSU, MUL)
            bR = ph()
            for c in range(4):
                k.mm(bR[:, c * 64:(c + 1) * 64], Bt_bd[:, c, :], ARt[:, c, 128:192])
            for c in range(4):
                k.mm(bR[:, 256 + c * 64:256 + (c + 1) * 64], Kt_bd[:, c, :], ARt[:, c, 128:192])
            k.tt(MrT, bR[:, 0:512].re("p (c t) -> p c t", c=8), mIU, MUL)
            k.tt(TTt[0], Xs, idb4, ADD)
            for (src, dst) in ((Bt_bd, Btok), (Kt_bd, Ktok), (Vt_bd, Vbd)):
                pst = ph()
                pt_ = pst.t if isinstance(pst, V) else pst
                pst16 = V(pt_, pst.ap.bitcast(BF16))
                for c in range(4):
                    k.tr(pst16[:, c * 128:(c + 1) * 128], src[:, c, :], idb)
                k.copy(dst, pst16[:, 0:512].re("p (c t) -> p c t", c=4))
            for r in range(1, 7):
                Zp = Zr[(r - 1) % 2]
                Xp = Xs if r == 1 else Xr[(r - 1) % 2]
                if r <= 4:
                    bA = ph()
                    for c in range(4):
                        k.mm(bA[:, c * 128:(c + 1) * 128], Zp[:, c, :], Xp[:, c, :])
                    k.copy(Xr[r % 2], c4(bA), "act")
                if 2 <= r <= 5:
                    bB = ph()
                    for c in range(4):
                        k.mm(bB[:, c * 128:(c + 1) * 128], idb, TTt[(r - 2) % 2][:, c, :], True, False)
                        k.mm(bB[:, c * 128:(c + 1) * 128], Zp[:, c, :], TTt[(r - 2) % 2][:, c, :], False, True)
                    k.copy(TTt[(r - 1) % 2], c4(bB), "act")
                if r <= 5:
                    bC = ph()
                    for c in range(4):
                        k.mm(bC[:, c * 128:(c + 1) * 128], Xp[:, c, :], Zp[:, c, :])
                    k.copy(Zr[r % 2], c4(bC), "act")
                if r == 6:
                    bB = ph()
                    for c in range(4):
                        k.mm(bB[:, c * 128:(c + 1) * 128], idb, TTt[0][:, c, :], True, False)
                        k.mm(bB[:, c * 128:(c + 1) * 128], Zp[:, c, :], TTt[0][:, c, :], False, True)
                    k.copy(TT5, c4(bB), "act")

        def tail(ui):
            hp, sub = ui // 2, ui % 2
            D3, D2g = sets3[ui % 3], sets2g[ui % 2]
            ARt, gam, bonus, g_ = D3["ARt"], D3["gam"], D3["bonus"], D3["g_"]
            MakT, MrT, Btok, Ktok, Vbd, TT5 = (D2g[n] for n in ("MakT", "MrT", "Btok", "Ktok", "Vbd", "TT5"))
            Pb = Pbd[l][hp]
            c0, n = sub * 256, 256
            sl = slice(c0, c0 + n)
            N_ = slice(0, n)
            if sub == 0:
                k.copy(Pb16, Pb, "dve")
            Yps = PF[3]
            for c in range(4):
                Wps = ph()
                k.mm(Wps[:, 0:128], ARt[:, c, 0:128], Pb16, True, False)
                k.mm(Wps[:, 0:128], MakT[:, c, :], Vbd[:, c, :], False, True)
                k.copy(Wsb, Wps[:, 0:128], "act")
                k.ts(Pb, Pb, gam[:, c:c + 1], MUL)
                Ups = ph()
                k.mm(Ups[:, 0:128], TT5[:, c, :], Wsb)
                k.copy(Ubd, Ups[:, 0:128], "act")
                ycol = Yps[:, c * 64:(c + 1) * 64]
                k.mm(ycol, Pb16, ARt[:, c, 128:192], True, False)
                k.mm(ycol, Ubd, MrT[:, c, :], False, False)
                k.mm(ycol, Vbd[:, c, :], MrT[:, 4 + c, :], False, True)
                Pps = ph()
                k.mm(Pps[:, 0:128], Btok[:, c, :], Ubd, True, False)
                k.mm(Pps[:, 0:128], Ktok[:, c, :], Vbd[:, c, :], False, True)
                k.stt(Pb, Pps[:, 0:128], gam[:, c:c + 1], Pb, MUL, ADD)
                k.copy(Pb16, Pb, "dve")
            k.copy(ysb[:, N_], Yps[:, N_])
            finish_unit(l, hp, sl, N_, ysb, yc, tAf, tBf, bonus[:, N_], g_[:, N_])
            if sub == 1 and B.last:
                for hh in range(2):
                    rows = slice(hh * 64, hh * 64 + 64)
                    k.dma("sp", O["o_wkv_p"][l][2 * hp + hh], Pb[rows, rows], is_output=True)

        PTf = [V(PT[i], PT[i].ap.bitcast(F32)) for i in range(2)]
        pools = {"tail": ([PHb[0]], None), "gn": ([PHb[1], PF[0], PTf[0], PTf[1]], None),
                 "ew": ([PF[1], PF[2]], [PF[1], PF[2]])}
        pf_save = pf_pool[0]

        def rec(kind, fnc, ui):
            if ui is None or ui > 7:
                return []
            ph_pool[0] = pools[kind][0]
            if pools[kind][1] is not None:
                pf_pool[0] = pools[kind][1]
            k.start_rec()
            fnc(ui)
            return k.stop_rec()

        k.interleave(rec("ew", ew, 0))
        k.interleave(rec("gn", gn, 0), rec("ew", ew, 1))
        for ui in range(8):
            k.interleave(rec("tail", tail, ui), rec("gn", gn, ui + 1), rec("ew", ew, ui + 2))
        pf_pool[0] = pf_save
        ph_pool[0] = PHpool
        if B.last:
            k.dma("sp", O["o_shift_p"][l], histA[l], is_output=True)

    def finish_unit(l, hp, sl, N_, ysb, yc, tA, tB, bonus_v, g_v):
        ps = ph()
        k.mm(ps[:, N_], C("avgblk", 128), ysb[:, N_])
        k.tt(yc[:, N_], ysb[:, N_], ps[:, N_], SUB)
        k.act(tA[:, N_], yc[:, N_], AF.Square)
        ps = ph()
        k.mm(ps[:, N_], C("avgblk", 128), tA[:, N_])
        rstd_of(ps[:, N_], tB[:, N_], 1.0, GN_EPS)
        k.stt(yc[:, N_], yc[:, N_], P(l, "lnx_w", hp), tB[:, N_], MUL, MUL)
        k.stt(yc[:, N_], yc[:, N_], P(l, "lnx_b", hp), bonus_v, ADD, ADD)
        k.tt(mixed[:, hp, sl], yc[:, N_], g_v, MUL)

    def outproj(B, l, wbase):
        for j in range(2):
            wc = wget(wbase + 5 + j)
            for i in range(B.nt):
                ps = pf()
                for m in range(8):
                    k.mm(ps, mixed[:, m, i * 128:(i + 1) * 128], wc[:, m, :], start=(m == 0), stop=(m == 7))
                hv = h[:, i, j * 512:(j + 1) * 512]
                k.tt(hv, hv, ps, ADD)

    ffn_bufs = {}

    def ffn_prepare():
        k.barrier()
        scr.reset()
        ffn_bufs["hT"] = scr.alloc([128, NF, TBP], BF16, "hT")
        ffn_bufs["sgt"] = [scr.alloc([128, TBP], F32, f"sgt{i}") for i in range(2)]

    def ffn(B, l, wbase):
        nt_ = B.ntok
        if "hT" not in ffn_bufs:
            ffn_prepare()
        hT, sgt = ffn_bufs.pop("hT"), ffn_bufs.pop("sgt")
        norm_T(B, l, "g_ffn")
        for p_ in range(11):
            wc = wget(wbase + 7 + p_)
            for q in range(2):
                f_ = 2 * p_ + q
                gps, ups = PF[2 * (f_ % 2)], PF[2 * (f_ % 2) + 1]
                for c in range(8):
                    k.mm(gps[:, 0:nt_], wc[:, c, q * 128:(q + 1) * 128], xnT[:, c, 0:nt_], start=(c == 0), stop=(c == 7))
                for c in range(8):
                    k.mm(ups[:, 0:nt_], wc[:, c, 256 + q * 128:256 + (q + 1) * 128], xnT[:, c, 0:nt_],
                         start=(c == 0), stop=(c == 7))
                sg_ = sgt[f_ % 2]
                k.act(sg_[:, 0:nt_], gps[:, 0:nt_], AF.Silu)
                k.tt(hT[:, f_, 0:nt_], sg_[:, 0:nt_], ups[:, 0:nt_], MUL)
        for j in range(2):
            for g in range(6):
                nfc = 4 if g < 5 else 2
                wc = wget(wbase + 18 + j * 6 + g)
                for fc in range(nfc):
                    f_ = g * 4 + fc
                    for i in range(B.nt):
                        k.mm(PF[i], hT[:, f_, i * 128:(i + 1) * 128], wc[:, fc, :], start=(f_ == 0), stop=(f_ == NF - 1))
            for i in range(B.nt):
                hv = h[:, i, j * 512:(j + 1) * 512]
                k.tt(hv, hv, PF[i], ADD)

    def ple(B, l, wbase, tok0):
        src = (I["pp"][l][tok0:tok0 + B.ntok, :] if B.kind == "p" else I["psm"][l]).rearrange("(i p) q -> p i q", p=128)
        k.dma("sp", ptile[:, 0:B.nt, :], src)
        k.copy(pbf[:, 0:B.nt, :], ptile[:, 0:B.nt, :], "dve")
        for i in range(B.nt):
            pt = PT[i % 2]
            for q in range(2):
                k.tr(pt[:, q * 128:(q + 1) * 128], pbf[:, i, q * 128:(q + 1) * 128], idb)
            for q in range(2):
                k.copy(pT[:, q, i * 128:(i + 1) * 128], pt[:, q * 128:(q + 1) * 128], "act" if i % 2 == 0 else "dve")
        norm_T(B, l, "g_ple")
        wp = wple_t
        k.dma("pool", wple_t, I["w_ple"][l].rearrange("(c p) n -> p c n", p=128))
        for j in range(2):
            wc = wget(wbase + 30 + j)
            for i in range(B.nt):
                gps, pps = PF[2 * (i % 2)], PF[2 * (i % 2) + 1]
                for c in range(8):
                    k.mm(gps, xnT[:, c, i * 128:(i + 1) * 128], wc[:, c, :], start=(c == 0), stop=(c == 7))
                for q in range(2):
                    k.mm(pps, pT[:, q, i * 128:(i + 1) * 128], wp[:, q, j * 512:(j + 1) * 512], start=(q == 0), stop=(q == 1))
                tg = tmpt[i % 2]
                k.act(tg, gps, AF.Sigmoid)
                k.tt(tg, tg, pps, MUL)
                hv = h[:, i, j * 512:(j + 1) * 512]
                k.tt(hv, hv, tg, ADD)

    def final_norm(B, tok0):
        ydst = O["y_p"][tok0:tok0 + B.ntok, :] if B.kind == "p" else O["y_s"]
        ydst = ydst.rearrange("(i p) d -> p i d", p=128)
        for i in range(B.nt):
            st = stat[i % 2]
            k.act(junk, h[:, i, :], AF.Square, accum_out=st[:, 0:1])
            rstd_of(st[:, 0:1], st[:, 1:2], 1.0 / D, RMS_EPS)
            k.stt(h[:, i, :], h[:, i, :], st[:, 1:2], gfin, MUL, MUL)
        k.dma("sp", ydst, h[:, 0:B.nt, :], is_output=True)

    stop_after = dbg.get("stop")
    for bi, B in enumerate(blocks):
        if dbg.get("blocks") is not None and bi not in dbg["blocks"]:
            continue
        tok0 = B.idx * TBP
        src = (I["xp"][tok0:tok0 + B.ntok, :] if B.kind == "p" else I["xs"]).rearrange("(i p) d -> p i d", p=128)
        k.dma("sp", h[:, 0:B.nt, :], src)
        stages = dbg.get("stages", "nCBAoFP")
        for l in range(NL):
            if dbg.get("layers") is not None and l not in dbg["layers"]:
                continue
            wbase = (bi * NL + l) * NCH
            if "n" in stages:
                norm_T(B, l, "g_mix")
            k.barrier()
            if "C" in stages:
                mixer_c(B, l, wbase)
            if "B" in stages:
                mixer_b(B, l, wbase)
            if "A" in stages:
                if B.kind == "p" and not dbg.get("nopipe"):
                    mixer_a_prompt(B, l, wbase)
                else:
                    mixer_a(B, l, wbase)
            dump(f"mixed{l}", mixed[:, :, 0:B.ntok], [128, 8, B.ntok])
            if "o" in stages:
                if "F" in stages:
                    ffn_prepare()
                outproj(B, l, wbase)
            dump(f"h_mix{l}", h[:, 0:B.nt, :], [128, B.nt, D])
            if "F" in stages:
                ffn(B, l, wbase)
            dump(f"h_ffn{l}", h[:, 0:B.nt, :], [128, B.nt, D])
            if "P" in stages:
                ple(B, l, wbase, tok0)
            dump(f"h_ple{l}", h[:, 0:B.nt, :], [128, B.nt, D])
        final_norm(B, tok0)
    k.finish()
    nc._kb = (k, I, O, DBG)
    return nc


_CACHE = {}


def kernel(**inputs):
    dbg = dict(DEBUG)
    sh = prep_shared(inputs)
    in_maps = []
    for c in range(NCORES):
        d = dict(sh)
        d.update(prep_core(inputs, c))
        in_maps.append(d)
    nc = build_program(dbg)
    res = run_bass_kernel_spmd(nc, in_maps, core_ids=list(range(NCORES)))
    R = res.results
    if dbg:
        _CACHE["res"] = R
    g = lambda nm: np.stack([np.asarray(R[c][nm], np.float32) for c in range(NCORES)], 0)
    y_prompt = g("y_p")
    y_sample = g("y_s").reshape(DEC_B, DEC_T, D)
    shift_p = np.transpose(g("o_shift_p"), (1, 0, 3, 2)).reshape(NL, NCORES, A_COLS)
    wkv_p = np.transpose(g("o_wkv_p"), (1, 0, 2, 4, 3))
    conv_p = np.transpose(g("o_conv_p"), (1, 0, 4, 3, 2)).reshape(NL, NCORES, 3, 256)
    lru_p = np.transpose(g("o_lru_p"), (1, 0, 3, 2)).reshape(NL, NCORES, 256)
    s5re_p = np.transpose(g("o_s5re_p"), (1, 0, 3, 2)).reshape(NL, NCORES, 16, 64)
    s5im_p = np.transpose(g("o_s5im_p"), (1, 0, 3, 2)).reshape(NL, NCORES, 16, 64)
    t_ = g("o_shift_s")
    shift_s = np.transpose(t_, (1, 0, 4, 3, 2)).reshape(NL, DEC_B, A_COLS)
    wkv_s = np.transpose(g("o_wkv_s"), (1, 0, 2, 3)).reshape(NL, DEC_B, 8, 64, 64)
    t_ = g("o_conv_s")
    conv_s = np.transpose(t_, (1, 0, 4, 5, 3, 2)).reshape(NL, DEC_B, 3, 256)
    t_ = g("o_lru_s")
    lru_s = np.transpose(t_, (1, 0, 4, 3, 2)).reshape(NL, DEC_B, 256)
    t_ = g("o_s5re_s")
    s5re_s = np.transpose(t_, (1, 0, 4, 3, 2)).reshape(NL, DEC_B, 16, 64)
    t_ = g("o_s5im_s")
    s5im_s = np.transpose(t_, (1, 0, 4, 3, 2)).reshape(NL, DEC_B, 16, 64)
    outs = (y_prompt, y_sample, shift_p, wkv_p, conv_p, lru_p, s5re_p, s5im_p,
            shift_s, wkv_s, conv_s, lru_s, s5re_s, s5im_s)
    return tuple(np.ascontiguousarray(o, dtype=np.float32) for o in outs)
```
